# Optimizing a Trainium2 kernel written in Bass

```python
import math
import jax, jax.numpy as jnp
from jax import lax
import numpy as np

D_MODEL = 2048
BATCH = 16
SEQ = 256
DEPTH = 2
DEC_BATCH = 8
DEC_SEQ = 4096
PAST_LEN = 256

GRID_W = 64
N_MIXERS = 2
N_ATTN_LAYERS = (DEPTH + 1) // 2
N_HYENA_LAYERS = DEPTH // 2
HEAD_DIM = 128
N_HEADS = D_MODEL // HEAD_DIM
N_KV_HEADS = 4
GROUP = N_HEADS // N_KV_HEADS
QKV_DIM = (N_HEADS + 2 * N_KV_HEADS) * HEAD_DIM
WINDOW = 128
BLOCK = 128
ROPE_THETA = 10000.0
ROPE_PAIRS = HEAD_DIM // 4
HY_ORDER = 2
HY_BANDS = 16
HY_EMB = 1 + 2 * HY_BANDS
HY_FH = 64
HY_MIN_DECAY = math.log(1e-2) / 1.5
HY_MAX_DECAY = math.log(1e-2) / 0.3
D_FF = 5632
EPS = 1e-6
NEG_INF = -1e30

kernel_name = "hybrid_dit_window_gqa_hyena_step"


def _rmsnorm(x, g):
    xf = x.astype(jnp.float32)
    y = xf * lax.rsqrt(jnp.mean(xf * xf, axis=-1, keepdims=True) + EPS)
    return (y * g).astype(x.dtype)


def _modulate(x, g, shift, scale):
    return _rmsnorm(x, g) * (1.0 + scale) + shift


def _dwconv3(x, w):
    xp = jnp.pad(x, ((0, 0), (1, 1), (0, 0)))
    return xp[:, :-2] * w[0] + xp[:, 1:-1] * w[1] + xp[:, 2:] * w[2]


def _qkv(h, w):
    B, L, _ = h.shape
    qkv = h @ w
    q = qkv[..., :N_HEADS * HEAD_DIM].reshape(B, L, N_HEADS, HEAD_DIM)
    k = qkv[..., N_HEADS * HEAD_DIM:(N_HEADS + N_KV_HEADS) * HEAD_DIM].reshape(B, L, N_KV_HEADS, HEAD_DIM)
    v = qkv[..., (N_HEADS + N_KV_HEADS) * HEAD_DIM:].reshape(B, L, N_KV_HEADS, HEAD_DIM)
    return q, k, v


def _axial_rope_angles(L):
    t = jnp.arange(L)
    row = (t // GRID_W).astype(jnp.float32)
    col = (t % GRID_W).astype(jnp.float32)
    inv = ROPE_THETA ** (-jnp.arange(ROPE_PAIRS, dtype=jnp.float32) / ROPE_PAIRS)
    return row[:, None] * inv, col[:, None] * inv


def _rotate(x, ang):
    x1, x2 = jnp.split(x, 2, axis=-1)
    cos = jnp.cos(ang)[None, :, None, :]
    sin = jnp.sin(ang)[None, :, None, :]
    return jnp.concatenate([x1 * cos - x2 * sin, x2 * cos + x1 * sin], axis=-1).astype(x.dtype)


def _axial_rope(x, ang_row, ang_col):
    half = HEAD_DIM // 2
    return jnp.concatenate([_rotate(x[..., :half], ang_row), _rotate(x[..., half:], ang_col)], axis=-1)


def _context_attention(q, k, v, sink):
    B, C = q.shape[:2]
    qg = q.reshape(B, C, N_KV_HEADS, GROUP, HEAD_DIM)
    s = jnp.einsum("bqhgd,bchd->bhgqc", qg, k).astype(jnp.float32) * (HEAD_DIM ** -0.5)
    sk = sink.astype(jnp.float32).reshape(N_KV_HEADS, GROUP)[None, :, :, None, None]
    m = jnp.maximum(jnp.max(s, axis=-1, keepdims=True), sk)
    p = jnp.exp(s - m)
    denom = jnp.sum(p, axis=-1, keepdims=True) + jnp.exp(sk - m)
    o = jnp.einsum("bhgqc,bchd->bqhgd", (p / denom).astype(v.dtype), v)
    return o.reshape(B, C, N_HEADS * HEAD_DIM)


def _latent_attention(q, k, v, ck, cv, sink):
    B, L = q.shape[:2]
    nb = L // BLOCK
    qb = q.reshape(B, nb, BLOCK, N_KV_HEADS, GROUP, HEAD_DIM)
    pad = ((0, 0), (BLOCK, BLOCK), (0, 0), (0, 0))
    kp = jnp.pad(k, pad).reshape(B, nb + 2, BLOCK, N_KV_HEADS, HEAD_DIM)
    vp = jnp.pad(v, pad).reshape(B, nb + 2, BLOCK, N_KV_HEADS, HEAD_DIM)
    kw = jnp.concatenate([kp[:, :-2], kp[:, 1:-1], kp[:, 2:]], axis=2)
    vw = jnp.concatenate([vp[:, :-2], vp[:, 1:-1], vp[:, 2:]], axis=2)
    scale = HEAD_DIM ** -0.5
    sw = jnp.einsum("bnqhgd,bnjhd->bnhgqj", qb, kw).astype(jnp.float32) * scale
    sc = jnp.einsum("bnqhgd,bchd->bnhgqc", qb, ck).astype(jnp.float32) * scale
    blk = jnp.arange(nb)[:, None, None] * BLOCK
    qpos = blk + jnp.arange(BLOCK)[None, :, None]
    kpos = blk - BLOCK + jnp.arange(3 * BLOCK)[None, None, :]
    valid = (jnp.abs(qpos - kpos) <= WINDOW) & (kpos >= 0) & (kpos < L)
    sw = jnp.where(valid[None, :, None, None], sw, NEG_INF)
    sk = sink.astype(jnp.float32).reshape(N_KV_HEADS, GROUP)[None, None, :, :, None, None]
    m = jnp.maximum(jnp.maximum(jnp.max(sw, axis=-1, keepdims=True), jnp.max(sc, axis=-1, keepdims=True)), sk)
    pw = jnp.exp(sw - m)
    pc = jnp.exp(sc - m)
    denom = jnp.sum(pw, axis=-1, keepdims=True) + jnp.sum(pc, axis=-1, keepdims=True) + jnp.exp(sk - m)
    o = (jnp.einsum("bnhgqj,bnjhd->bnqhgd", (pw / denom).astype(v.dtype), vw)
         + jnp.einsum("bnhgqc,bchd->bnqhgd", (pc / denom).astype(cv.dtype), cv))
    return o.reshape(B, L, N_HEADS * HEAD_DIM)


def _hyena_filters(L, w_f1, b_f1, w_f2, b_f2, w_f3, freq):
    t = jnp.linspace(0.0, 1.0, L, dtype=jnp.float32)[:, None]
    w = 2.0 * math.pi * jnp.arange(L, dtype=jnp.float32)[:, None] / L
    f = jnp.linspace(1e-4, HY_BANDS - 1, HY_BANDS, dtype=jnp.float32)[None, :]
    feat = jnp.concatenate([t, jnp.cos(f * w), -jnp.sin(f * w)], axis=-1)
    h = jnp.sin(freq * (feat @ w_f1 + b_f1))
    h = jnp.sin(freq * (h @ w_f2 + b_f2))
    h = (h @ w_f3).astype(jnp.float32).reshape(L, 2, HY_ORDER, D_MODEL)
    deltas = jnp.linspace(HY_MIN_DECAY, HY_MAX_DECAY, D_MODEL, dtype=jnp.float32)
    decay = jnp.exp(-t[:, :, None] * jnp.abs(deltas))
    h_fwd = h[:, 0] * decay
    h_bwd = h[:, 1] * decay
    zero = jnp.zeros((1, HY_ORDER, D_MODEL), jnp.float32)
    return jnp.concatenate([h_fwd, zero, h_bwd[1:][::-1]], axis=0)


def _fft_conv(z, filt_f, bias):
    L = z.shape[1]
    zf = z.astype(jnp.float32)
    y = jnp.fft.irfft(jnp.fft.rfft(zf, n=2 * L, axis=1) * filt_f[None], n=2 * L, axis=1)[:, :L]
    return (y + zf * bias).astype(z.dtype)


def _hyena(u, w_in, conv_w, w_f1, b_f1, w_f2, b_f2, w_f3, freq, bias, w_out):
    L = u.shape[1]
    x1, x2, v = jnp.split(_dwconv3(u @ w_in, conv_w), 3, axis=-1)
    filt_f = jnp.fft.rfft(_hyena_filters(L, w_f1, b_f1, w_f2, b_f2, w_f3, freq), axis=0)
    z = x1 * _fft_conv(v, filt_f[:, 0], bias[0])
    z = x2 * _fft_conv(z, filt_f[:, 1], bias[1])
    return z @ w_out


def _conv_ffn(h, w_up, conv_w, w_down):
    a, b = jnp.split(h @ w_up, 2, axis=-1)
    return (jax.nn.silu(_dwconv3(a, conv_w)) * b) @ w_down


def setup_inputs(seed: int = 0) -> dict:
    key = jax.random.key(seed)
    ks = jax.random.split(key, 32)
    nrm = lambda k, shape, s: jax.random.normal(k, shape, jnp.float32) * s
    centre = jnp.array([0.0, 1.0, 0.0], jnp.float32)[:, None]
    D = D_MODEL
    return {
        "x_prompt": nrm(ks[0], (BATCH, SEQ, D), 1.0),
        "x_sample": nrm(ks[1], (DEC_BATCH, DEC_SEQ, D), 1.0),
        "cache_k": nrm(ks[2], (DEC_BATCH, N_ATTN_LAYERS, PAST_LEN, N_KV_HEADS, HEAD_DIM), 1.0),
        "cache_v": nrm(ks[3], (DEC_BATCH, N_ATTN_LAYERS, PAST_LEN, N_KV_HEADS, HEAD_DIM), 1.0),
        "c": nrm(ks[4], (DEC_BATCH, D), 1.0),
        "c_ctx": nrm(ks[5], (D,), 1.0),
        "w_mod": nrm(ks[6], (DEPTH, D, 6 * D), 0.5 * D ** -0.5),
        "b_mod": nrm(ks[7], (DEPTH, 6 * D), 0.02),
        "norm_mix": 1.0 + nrm(ks[8], (DEPTH, D), 0.02),
        "norm_ffn": 1.0 + nrm(ks[9], (DEPTH, D), 0.02),
        "norm_final": 1.0 + nrm(ks[10], (D,), 0.02),
        "w_qkv": nrm(ks[11], (N_ATTN_LAYERS, D, QKV_DIM), D ** -0.5),
        "w_o": nrm(ks[12], (N_ATTN_LAYERS, N_HEADS * HEAD_DIM, D), (N_HEADS * HEAD_DIM) ** -0.5),
        "attn_sink": nrm(ks[13], (N_ATTN_LAYERS, N_HEADS), 0.5),
        "hy_w_in": nrm(ks[14], (N_HYENA_LAYERS, D, 3 * D), D ** -0.5),
        "hy_conv": centre + nrm(ks[15], (N_HYENA_LAYERS, 3, 3 * D), 0.3),
        "hy_w_f1": nrm(ks[16], (N_HYENA_LAYERS, HY_EMB, HY_FH), HY_EMB ** -0.5),
        "hy_b_f1": nrm(ks[17], (N_HYENA_LAYERS, HY_FH), 0.1),
        "hy_w_f2": nrm(ks[18], (N_HYENA_LAYERS, HY_FH, HY_FH), HY_FH ** -0.5),
        "hy_b_f2": nrm(ks[19], (N_HYENA_LAYERS, HY_FH), 0.1),
        "hy_w_f3": nrm(ks[20], (N_HYENA_LAYERS, HY_FH, 2 * HY_ORDER * D), 0.05 * HY_FH ** -0.5),
        "hy_freq": 1.0 + nrm(ks[21], (N_HYENA_LAYERS, HY_FH), 0.05),
        "hy_bias": nrm(ks[22], (N_HYENA_LAYERS, HY_ORDER, D), 0.1),
        "hy_w_out": nrm(ks[23], (N_HYENA_LAYERS, D, D), D ** -0.5),
        "ffn_w_up": nrm(ks[24], (DEPTH, D, 2 * D_FF), D ** -0.5),
        "ffn_conv": centre + nrm(ks[25], (DEPTH, 3, D_FF), 0.3),
        "ffn_w_down": nrm(ks[26], (DEPTH, D_FF, D), D_FF ** -0.5),
    }


def reference(x_prompt, x_sample, cache_k, cache_v, c, c_ctx, w_mod, b_mod, norm_mix, norm_ffn, norm_final,
              w_qkv, w_o, attn_sink, hy_w_in, hy_conv, hy_w_f1, hy_b_f1, hy_w_f2, hy_b_f2, hy_w_f3, hy_freq,
              hy_bias, hy_w_out, ffn_w_up, ffn_conv, ffn_w_down):
    xp, xs = x_prompt, x_sample
    ang_row, ang_col = _axial_rope_angles(xs.shape[1])
    new_k, new_v = [], []
    for i in range(DEPTH):
        mod_p = jax.nn.silu(c_ctx) @ w_mod[i] + b_mod[i]
        mod_s = (jax.nn.silu(c) @ w_mod[i] + b_mod[i])[:, None, :]
        sh1_p, sc1_p, g1_p, sh2_p, sc2_p, g2_p = jnp.split(mod_p, 6, axis=-1)
        sh1_s, sc1_s, g1_s, sh2_s, sc2_s, g2_s = jnp.split(mod_s, 6, axis=-1)
        hp = _modulate(xp, norm_mix[i], sh1_p, sc1_p)
        hs = _modulate(xs, norm_mix[i], sh1_s, sc1_s)
        if i % N_MIXERS == 0:
            a = i // N_MIXERS
            qp, kp, vp = _qkv(hp, w_qkv[a])
            new_k.append(kp)
            new_v.append(vp)
            mp = _context_attention(qp, kp, vp, attn_sink[a]) @ w_o[a]
            qs, ks_, vs = _qkv(hs, w_qkv[a])
            qs = _axial_rope(qs, ang_row, ang_col)
            ks_ = _axial_rope(ks_, ang_row, ang_col)
            ms = _latent_attention(qs, ks_, vs, cache_k[:, a], cache_v[:, a], attn_sink[a]) @ w_o[a]
        else:
            h = i // N_MIXERS
            mp = _hyena(hp, hy_w_in[h], hy_conv[h], hy_w_f1[h], hy_b_f1[h], hy_w_f2[h], hy_b_f2[h],
                        hy_w_f3[h], hy_freq[h], hy_bias[h], hy_w_out[h])
            ms = _hyena(hs, hy_w_in[h], hy_conv[h], hy_w_f1[h], hy_b_f1[h], hy_w_f2[h], hy_b_f2[h],
                        hy_w_f3[h], hy_freq[h], hy_bias[h], hy_w_out[h])
        xp = xp + g1_p * mp
        xs = xs + g1_s * ms
        xp = xp + g2_p * _conv_ffn(_modulate(xp, norm_ffn[i], sh2_p, sc2_p), ffn_w_up[i], ffn_conv[i], ffn_w_down[i])
        xs = xs + g2_s * _conv_ffn(_modulate(xs, norm_ffn[i], sh2_s, sc2_s), ffn_w_up[i], ffn_conv[i], ffn_w_down[i])
    y_prompt = _rmsnorm(xp, norm_final)
    y_sample = _rmsnorm(xs, norm_final)
    new_cache_k = jnp.stack(new_k, axis=1)
    new_cache_v = jnp.stack(new_v, axis=1)
    return (y_prompt, y_sample, new_cache_k, new_cache_v)
```

```python
from contextlib import ExitStack
import math
import numpy as np
import ml_dtypes
import concourse.bass as bass
import concourse.mybir as mybir
from concourse.bass_utils import run_bass_kernel_spmd

F32 = mybir.dt.float32
BF16 = mybir.dt.bfloat16
AF = mybir.ActivationFunctionType
ALU = mybir.AluOpType

D = 2048
NT = 4608
LS = 4096
LP = 256
DFF = 5632
SEQS = [(0, 4096), (4096, 4352), (4352, 4608)]
EPS = 1e-6


class Res:
    __slots__ = ("name", "w", "r", "multi")

    def __init__(self, name, multi=False):
        self.name = name
        self.w = {}
        self.r = {}
        self.multi = multi


def _merge(d, tok):
    k, v = tok
    if v > d.get(k, 0):
        d[k] = v


class Sched:
    CE = ("pe", "act", "dve", "pool")
    ALLQ = ("pe", "act", "dve", "pool", "sp")

    def __init__(self, nc, n_dma_sems=32):
        self.nc = nc
        self.prog = {e: [] for e in self.ALLQ}
        self.cnt = {e: 0 for e in self.CE}
        self.sem = {}
        self.dma_i = 0
        self.nd = n_dma_sems
        self.dma_val = [0] * n_dma_sems
        self.nsw = 32
        self.sw_i = 0
        self.sw_val = [0] * self.nsw
        self.waited = {e: {} for e in self.ALLQ}
        self.st = ExitStack()

    def sb(self, name, shape, dtype=F32):
        return self.st.enter_context(self.nc.sbuf_tensor(name, list(shape), dtype))

    def ps(self, name, shape, dtype=F32):
        return self.st.enter_context(self.nc.psum_tensor(name, list(shape), dtype))

    def op(self, eng, fn, reads=(), writes=(), dma=0):
        waits = {}
        for r in reads:
            for k, v in r.w.items():
                if v > waits.get(k, 0):
                    waits[k] = v
        for w in writes:
            for k, v in w.r.items():
                if v > waits.get(k, 0):
                    waits[k] = v
            if not (w.multi and not w.r):
                for k, v in w.w.items():
                    if v > waits.get(k, 0):
                        waits[k] = v
        if (not dma) and eng == "pe":
            waits.pop(("E", "pe"), None)
        sems = None
        if dma and eng == "pool":
            tok = {}
            sems = []
            for _ in range(dma):
                idx = self.sw_i % self.nsw
                self.sw_i += 1
                prev = self.sw_val[idx]
                if prev > waits.get(("S", idx), 0):
                    waits[("S", idx)] = prev
                self.sw_val[idx] = prev + 16
                tok[("S", idx)] = prev + 16
                sems.append(("S", idx))
        elif dma:
            idx = self.dma_i % self.nd
            self.dma_i += 1
            prev = self.dma_val[idx]
            if prev > waits.get(("D", idx), 0):
                waits[("D", idx)] = prev
            self.dma_val[idx] = prev + 16 * dma
            tok = {("D", idx): prev + 16 * dma}
            sems = [("D", idx)] * dma
        else:
            self.cnt[eng] += 1
            tok = {("E", eng): self.cnt[eng]}
        wl = []
        wd = self.waited[eng]
        for key, val in waits.items():
            if val <= 0 or wd.get(key, 0) >= val:
                continue
            wd[key] = val
            wl.append((key, val))
        self.prog[eng].append((wl, fn, tok, sems))
        for r in reads:
            for k, v in tok.items():
                if v > r.r.get(k, 0):
                    r.r[k] = v
        for w in writes:
            if w.multi and not w.r:
                for k, v in tok.items():
                    if v > w.w.get(k, 0):
                        w.w[k] = v
            else:
                w.w = dict(tok)
                w.r = {}
        return tok

    def dma(self, q, out, in_, reads=(), writes=()):
        return self.op(q, lambda e: [e.dma_start(out=out, in_=in_)], reads=reads, writes=writes, dma=1)

    def finish(self):
        wl = []
        for i in range(self.nd):
            if self.dma_val[i] > 0:
                wl.append((("D", i), self.dma_val[i]))
        for i in range(self.nsw):
            if self.sw_val[i] > 0:
                wl.append((("S", i), self.sw_val[i]))
        for e in self.CE:
            if self.cnt[e] > 0:
                wl.append((("E", e), self.cnt[e]))
        self.prog["sp"].append((wl, None, None, None))

    def emit(self):
        nc = self.nc
        st = self.st
        for e in self.CE:
            self.sem[("E", e)] = st.enter_context(nc.semaphore(f"s_{e}"))
        for i in range(self.nd):
            self.sem[("D", i)] = st.enter_context(nc.semaphore(f"d_{i}"))
        for i in range(self.nsw):
            self.sem[("S", i)] = st.enter_context(nc.semaphore(f"sw_{i}"))
        block = st.enter_context(nc.Block())
        sched = self

        def mk(engname):
            def body(e):
                for wl, fn, tok, sems in sched.prog[engname]:
                    for key, val in wl:
                        e.wait_ge(sched.sem[key], val)
                    if fn is None:
                        continue
                    ins = fn(e)
                    if sems is not None:
                        assert len(ins) == len(sems), (len(ins), len(sems))
                        for i, sk in zip(ins, sems):
                            i.then_inc(sched.sem[sk], 16)
                    else:
                        if isinstance(ins, (list, tuple)):
                            ins = ins[-1]
                        ins.then_inc(sched.sem[("E", engname)], 1)
            return body

        block.sync(mk("sp"))
        block.tensor(mk("pe"))
        block.scalar(mk("act"))
        block.vector(mk("dve"))
        block.gpsimd(mk("pool"))


_CONST = None


def _consts():
    global _CONST
    if _CONST is not None:
        return _CONST
    bf = ml_dtypes.bfloat16
    c = {}
    c["ident_b"] = np.eye(128, dtype=np.float32).astype(bf)
    c["ident_f"] = np.eye(128, dtype=np.float32)
    t = np.arange(LS)
    row = (t // 64).astype(np.float64)
    col = (t % 64).astype(np.float64)
    inv = 10000.0 ** (-np.arange(32, dtype=np.float64) / 32)
    C = np.ones((128, NT), np.float64)
    Sg = np.zeros((128, NT), np.float64)
    perm = np.zeros((128, 128), np.float32)
    for d in range(128):
        half, e = d // 64, d % 64
        j, first = e % 32, e < 32
        ang = (row if half == 0 else col) * inv[j]
        C[d, :LS] = np.cos(ang)
        Sg[d, :LS] = -np.sin(ang) if first else np.sin(ang)
        perm[d + 32 if first else d - 32, d] = 1.0
    c["ropeC"] = C.astype(np.float32).astype(bf)
    c["ropeS"] = Sg.astype(np.float32).astype(bf)
    c["perm"] = perm.astype(bf)
    kl = np.arange(128)[:, None]
    ql = np.arange(128)[None, :]
    m = np.stack([np.tile((kl >= ql), (1, 4)), np.tile((kl <= ql), (1, 4))], axis=1)
    c["bmask"] = m.astype(np.float32).astype(bf)
    for nm, L in (("S", LS), ("P", LP)):
        N = 2 * L
        tt = np.arange(L, dtype=np.float64)[:, None]
        kk = np.arange(L, dtype=np.float64)[None, :]
        th = 2.0 * np.pi * ((tt * (kk + 0.5)) % N) / N
        fwd = np.concatenate([np.cos(th), np.sin(th)], axis=1)
        CCf = L // 128
        inv = fwd.T * (2.0 / N)
        c["fwd" + nm] = np.ascontiguousarray(
            fwd.reshape(CCf, 128, 2 * CCf, 128).transpose(2, 1, 0, 3)).astype(np.float32).astype(bf)
        c["inv" + nm] = np.ascontiguousarray(
            inv.reshape(2 * CCf, 128, CCf, 128).transpose(2, 1, 0, 3)).astype(np.float32).astype(bf)
        tl = np.linspace(0.0, 1.0, L, dtype=np.float32)[:, None]
        w = (2.0 * np.pi * np.arange(L, dtype=np.float32)[:, None] / L).astype(np.float32)
        f = np.linspace(1e-4, 15, 16, dtype=np.float32)[None, :]
        feat = np.concatenate([tl, np.cos(f * w), -np.sin(f * w)], axis=-1).astype(np.float32)
        c["featT" + nm] = np.ascontiguousarray(feat.T)
        deltas = np.linspace(math.log(1e-2) / 1.5, math.log(1e-2) / 0.3, D, dtype=np.float32)
        dec = np.exp(-tl * np.abs(deltas)[None, :]).astype(np.float32)
        decb = dec.copy()
        decb[0, :] = 0.0
        c["decf" + nm] = dec
        c["decb" + nm] = decb
    N = 2 * LS
    a_ = np.arange(128, dtype=np.float64)[:, None]
    k1 = np.arange(128, dtype=np.float64)[None, :]
    phi = 2.0 * np.pi * a_ * (k1 + 0.5) / 256.0
    c["f1c"] = np.cos(phi).astype(np.float32).astype(bf)
    c["f1s"] = np.sin(phi).astype(np.float32).astype(bf)
    c["i1c"] = (np.cos(phi).T * (2.0 / N)).astype(np.float32).astype(bf)
    c["i1s"] = (-np.sin(phi).T * (2.0 / N)).astype(np.float32).astype(bf)
    r_ = np.arange(32, dtype=np.float64)[None, :]
    psi = 2.0 * np.pi * r_ * (np.arange(128, dtype=np.float64)[:, None] + 0.5) / N
    c["tw1"] = np.concatenate([np.cos(psi), np.sin(psi)], axis=1).astype(np.float32)
    chi = 2.0 * np.pi * np.outer(np.arange(32), np.arange(32)) / 32.0
    cbm = np.kron(np.eye(4), np.cos(chi))
    sbm = np.kron(np.eye(4), np.sin(chi))
    c["cbm"] = cbm.astype(np.float32).astype(bf)
    c["sbm"] = sbm.astype(np.float32).astype(bf)
    c["ncbm"] = (-cbm).astype(np.float32).astype(bf)
    c["nsbm"] = (-sbm).astype(np.float32).astype(bf)
    q_ = (np.arange(128) // 32).astype(np.float64)[:, None]
    rr2 = (np.arange(128) % 32).astype(np.float64)[:, None]
    j_ = np.arange(32, dtype=np.float64)[None, :]
    psi2 = 2.0 * np.pi * rr2 * (4.0 * j_ + q_ + 0.5) / N
    c["tw2"] = np.concatenate([np.cos(psi2), np.sin(psi2)], axis=1).astype(np.float32)
    _CONST = c
    return c


import os
DBG_STOP = int(os.environ.get("MK_STOP", "999"))
DBG_OUT = [x for x in os.environ.get("MK_OUT", "").split(",") if x]


class _Stop(Exception):
    pass


def build():
    nc = bass.Bass("TRN2", target_bir_lowering=False)
    S = Sched(nc)
    cst = _consts()
    IN = {}

    def din(name, shape, dt=F32):
        IN[name] = nc.dram_tensor(name, list(shape), dt, kind="ExternalInput").ap()
        return IN[name]

    def dscr(name, shape, dt):
        return nc.dram_tensor(name, list(shape), dt, kind=("ExternalOutput" if name in DBG_OUT else "Internal")).ap()

    xs = din("xs", [LS, D]); xp = din("xp", [512, D])
    ck = din("ck", [256, 512]); cv = din("cv", [256, 512])
    cvec = din("cvec", [32, 128])
    w_mod = din("w_mod", [2, D, 6 * D]); smallv = din("smallv", [2, 128, 128])
    norm_final = din("norm_final", [D])
    w_qkv = din("w_qkv", [D, 3072]); w_o = din("w_o", [D, D]); sink = din("sink", [16])
    hy_w_in = din("hy_w_in", [D, 3 * D]); hy_conv = din("hy_conv", [3, 3 * D])
    hy_w_f1 = din("hy_w_f1", [33, 64]); hy_w_f2 = din("hy_w_f2", [64, 64]); hy_w_f3 = din("hy_w_f3", [64, 4 * D])
    hy_small = din("hy_small", [64, 3])
    hy_bias = din("hy_bias", [2, D]); hy_w_out = din("hy_w_out", [D, D])
    ffn_w_up = din("ffn_w_up", [2, D, 2 * DFF]); ffn_convT = din("ffn_convT", [2, 132, 128])
    ffn_w_down = din("ffn_w_down", [2, DFF, D])
    for k, v in cst.items():
        din("c_" + k, v.shape, BF16 if v.dtype != np.float32 else F32)

    yp = nc.dram_tensor("yp", [512, D], F32, kind="ExternalOutput").ap()
    ys = nc.dram_tensor("ys", [LS, D], F32, kind="ExternalOutput").ap()
    nk = nc.dram_tensor("nk", [512, 512], F32, kind="ExternalOutput").ap()
    nv = nc.dram_tensor("nv", [512, 512], F32, kind="ExternalOutput").ap()

    hT = dscr("hT", [D, NT], BF16); hT_r = Res("hT", True)
    qT = dscr("qT", [2560, NT], BF16); qT_r = Res("qT", True)
    vtm = dscr("vtm", [NT, 512], BF16); vtm_r = Res("vtm", True)
    oT = dscr("oT", [36, 128, 16, 128], BF16); oT_r = Res("oT", True)
    hTt = dscr("hTt", [36, 128, 16, 128], BF16); hTt_r = Res("hTt", True)
    xres = [dscr(f"xres{i}", [NT, D], F32) for i in range(4)]
    xres_r = [Res(f"xres{i}", True) for i in range(4)]
    abT = dscr("abT", [2 * DFF, NT], BF16); abT_r = Res("abT", True)
    gT = dscr("gT", [DFF, NT], BF16); gT_r = Res("gT", True)
    gate_scr = dscr("gate_scr", [4, 128, D], F32); gate_r = Res("gate", True)
    ptm = dscr("ptm", [NT, 3 * D], BF16); ptm_r = Res("ptm", True)
    c3 = dscr("c3", [NT, 3 * D], BF16); c3_r = Res("c3", True)
    hsd = dscr("hsd", [2, LS, 2 * D], BF16); hsd_r = Res("hsd", True)
    ghat = dscr("ghat", [2, LS, 2 * D], BF16); ghat_r = Res("ghat", True)
    yhat = dscr("yhat", [2, 2 * LS, D], BF16); yhat_r = Res("yhat", True)
    z1 = dscr("z1", [NT, D], BF16); z1_r = Res("z1", True)
    zz = dscr("zz", [NT, D], BF16); zz_r = Res("zz", True)
    in_r = Res("inputs")

    BB = S.sb("BB", [128, 35200], BF16); BB_r = Res("BB")
    WA = [S.sb(f"WA{i}", [128, 12288], BF16) for i in range(2)]; WA_r = [Res(f"WA{i}") for i in range(2)]
    XT = [S.sb(f"XT{i}", [128, D], F32) for i in range(2)]; XT_r = [Res(f"XT{i}") for i in range(2)]
    XN = S.sb("XN", [128, D], BF16); XN_r = Res("XN")
    JK = S.sb("JK", [128, D], BF16); JK_r = Res("JK")
    HTS = S.sb("HTS", [128, 16, 256], BF16); HTS_r = Res("HTS")
    STG = [S.sb(f"STG{i}", [128, 512], F32) for i in range(6)]; STG_r = [Res(f"STG{i}") for i in range(6)]
    STX = [S.sb(f"STX{i}", [128, 512], F32) for i in range(4)]; STX_r = [Res(f"STX{i}") for i in range(4)]
    SB16 = [S.sb(f"SB16{i}", [128, 512], BF16) for i in range(4)]; SB16_r = [Res(f"SB16{i}") for i in range(4)]
    ACC = S.sb("ACC", [128, NT], F32); ACC_r = Res("ACC")
    idb = S.sb("idb", [128, 128], BF16); idf = S.sb("idf", [128, 128], F32); perm = S.sb("perm", [128, 128], BF16)
    cst_r = Res("consts")
    colsT = S.sb("colsT", [128, 128], F32); colsT_r = Res("colsT")
    cT = S.sb("cT", [128, 32], BF16); cT_r = Res("cT")
    cbc = HTS[:].rearrange("p c t -> p (c t)").rearrange("p (a b) -> p a b", a=32); cbc_r = HTS_r
    modc = S.sb("modc", [128, 2, 96], F32); modc_r = Res("modc")
    gsc = S.sb("gsc", [128, 2, 2, 16], F32); gsc_r = Res("gsc")
    sm = S.sb("sm", [128, 8], F32); sm_r = Res("sm")
    esink = S.sb("esink", [128, 16], F32); esink_r = Res("esink")
    fcv = S.sb("fcv", [128, 3, 44], F32); fcv_r = Res("fcv")
    stg_s = S.sb("stg_s", [128, 128], F32); stg_r = Res("stg_s")
    banks = [S.ps(f"bank{i}", [128, 512], F32) for i in range(8)]
    bank_r = [Res(f"bank{i}") for i in range(8)]
    bi = [0]

    def sub(parent, name):
        r = Res(name)
        r.w = dict(parent.w)
        r.r = dict(parent.r)
        return r

    def join(parent, subs):
        for rr_ in subs:
            for k, v in list(rr_.w.items()) + list(rr_.r.items()):
                if v > parent.r.get(k, 0):
                    parent.r[k] = v

    nbn = [8]

    def nb():
        i = bi[0] % nbn[0]
        bi[0] += 1
        return banks[i], bank_r[i]

    rr = {"stg": 0, "sb16": 0, "wa": 0, "xt": 0, "ev": 0}

    def rot(name, n):
        i = rr[name] % n
        rr[name] += 1
        return i

    def evq():
        return "act" if rot("ev", 2) == 0 else "dve"

    def copy(q, out, in_, reads, writes):
        if q == "act":
            return S.op("act", lambda e: e.activation(out=out, in_=in_, func=AF.Copy), reads=reads, writes=writes)
        return S.op(q, lambda e: e.tensor_copy(out=out, in_=in_), reads=reads, writes=writes)

    S.dma("sp", idb[:], IN["c_ident_b"], writes=[cst_r])
    S.dma("sp", idf[:], IN["c_ident_f"], writes=[cst_r])
    S.dma("sp", perm[:], IN["c_perm"], writes=[cst_r])

    def x_src(layer_in):
        def f(ti):
            if layer_in is None:
                return (xs[ti * 128:(ti + 1) * 128, :] if ti < 32 else xp[(ti - 32) * 128:(ti - 31) * 128, :]), in_r
            return xres[layer_in][ti * 128:(ti + 1) * 128, :], xres_r[layer_in]
        return f

    def mod_phase(i):
        S.dma("sp", stg_s[:], smallv[i], reads=[in_r], writes=[stg_r])
        b, br = nb()
        S.op("pe", lambda e: e.transpose(b[:, 0:128], stg_s[:], idf[:]), reads=[stg_r, cst_r], writes=[br])
        copy("act", colsT[:], b[:, 0:128], [br], [colsT_r])
        S.dma("sp", stg_s[0:32, :], cvec, reads=[in_r], writes=[stg_r])
        S.op("act", lambda e: e.activation(out=stg_s[0:32, :], in_=stg_s[0:32, :], func=AF.Silu),
             reads=[stg_r], writes=[stg_r])
        b2, b2r = nb()
        S.op("pe", lambda e: e.transpose(b2[:, 0:32], stg_s[0:32, :], idf[0:32, 0:32]), reads=[stg_r, cst_r], writes=[b2r])
        copy("act", cT[:], b2[:, 0:32], [b2r], [cT_r])

        def mkbc(e):
            last = None
            for c in range(32):
                last = e.tensor_copy(out=cbc[:, c, :], in_=cT[:, c:c + 1].to_broadcast([128, 128]))
            return last
        S.op("dve", mkbc, reads=[cT_r], writes=[cbc_r])
        mb, mbr = banks[7], bank_r[7]
        nbn[0] = 7
        for blk in range(24):
            wi = rot("wa", 2)
            wv = WA[wi][:, 0:8192].rearrange("p (c w) -> p c w", c=16)

            def ldw(e, blk=blk, wv=wv):
                return [e.dma_start(out=wv[:, c4 * 4:(c4 + 1) * 4, :],
                                    in_=w_mod[i, c4 * 512:(c4 + 1) * 512, blk * 512:(blk + 1) * 512]
                                    .rearrange("(c p) n -> p c n", p=128)) for c4 in range(4)]
            S.op("pool", ldw, reads=[in_r], writes=[WA_r[wi]], dma=4)

            def mm(e, blk=blk, wv=wv):
                last = None
                for n in range(4):
                    ch = blk * 4 + n
                    for k in range(16):
                        last = e.matmul(mb[:, ch * 2:ch * 2 + 2], wv[:, k, n * 128:(n + 1) * 128],
                                        cT[:, k:32:16], start=(k == 0), stop=(k == 15))
                return last
            S.op("pe", mm, reads=[WA_r[wi], cT_r], writes=[mbr])
            which = {2: 0, 5: 1}.get(blk // 4)
            if which is not None:
                cb = (blk % 4) * 512
                for r in range(2):
                    gb, gbr = nb()

                    def mg(e, wv=wv, r=r, gb=gb):
                        last = None
                        for k in range(16):
                            last = e.matmul(gb[:], cbc[:, r * 16 + k, :], wv[:, k, :], start=(k == 0), stop=(k == 15))
                        return last
                    S.op("pe", mg, reads=[WA_r[wi], cbc_r], writes=[gbr])
                    si = rot("stg", 6)
                    s2 = rot("stg", 6)
                    S.dma("sp", STG[s2][:], smallv[i, (blk * 4):(blk * 4 + 4), :].rearrange("a b -> (a b)")
                          .partition_broadcast(128), reads=[in_r], writes=[STG_r[s2]])
                    S.op("dve", lambda e, gb=gb, si=si, s2=s2: e.tensor_tensor(
                        out=STG[si][:], in0=gb[:], in1=STG[s2][:], op=ALU.add),
                        reads=[gbr, STG_r[s2]], writes=[STG_r[si]])
                    S.dma("sp", gate_scr[r * 2 + which, :, cb:cb + 512], STG[si][:],
                          reads=[STG_r[si]], writes=[gate_r])
        nbn[0] = 8
        for r in range(2):
            S.op("dve", lambda e, r=r: e.tensor_tensor(out=modc[:, r, :], in0=mb[:, r:192:2], in1=colsT[:, 0:96],
                                                       op=ALU.add), reads=[mbr, colsT_r], writes=[modc_r])
        for r in range(2):
            for wh in range(2):
                S.op("dve", lambda e, r=r, wh=wh: e.scalar_tensor_tensor(
                    out=gsc[:, r, wh, :], in0=modc[:, r, wh * 48 + 16:wh * 48 + 32], scalar=1.0,
                    in1=colsT[:, 96 + wh * 16:112 + wh * 16], op0=ALU.add, op1=ALU.mult),
                    reads=[modc_r, colsT_r], writes=[gsc_r])

    def norm_T(src, wh, plain_src=None):
        for ti in range(36):
            r = 0 if ti < 32 else 1
            if plain_src is None:
                xi = rot("xt", 2)
                ap, res = src(ti)
                S.dma("sp", XT[xi][:], ap, reads=[res], writes=[XT_r[xi]])
                S.op("act", lambda e, xi=xi: e.activation(out=JK[:], in_=XT[xi][:], func=AF.Square,
                                                          accum_out=sm[:, 0:1]),
                     reads=[XT_r[xi]], writes=[JK_r, sm_r])
                S.op("dve", lambda e: e.tensor_scalar(out=sm[:, 1:2], in0=sm[:, 0:1], scalar1=1.0 / D, scalar2=EPS,
                                                      op0=ALU.mult, op1=ALU.add), reads=[sm_r], writes=[sm_r])
                S.op("act", lambda e: e.activation(out=sm[:, 2:3], in_=sm[:, 1:2], func=AF.Sqrt), reads=[sm_r], writes=[sm_r])
                S.op("dve", lambda e: e.reciprocal(out=sm[:, 3:4], in_=sm[:, 2:3]), reads=[sm_r], writes=[sm_r])
                S.op("dve", lambda e, xi=xi: e.tensor_scalar(out=XN[:], in0=XT[xi][:], scalar1=sm[:, 3:4], scalar2=None,
                                                            op0=ALU.mult), reads=[XT_r[xi], sm_r], writes=[XN_r])
            else:
                pa, pr = plain_src
                S.dma("sp", XN[:], pa[ti * 128:(ti + 1) * 128, :], reads=[pr], writes=[XN_r])
            half = ti % 2
            for g in range(2):
                b, br = nb()
                bv = b[:].bitcast(BF16)

                def tp(e, g=g, bv=bv):
                    last = None
                    for j in range(8):
                        c = g * 8 + j
                        last = e.transpose(bv[:, j * 128:(j + 1) * 128], XN[:, c * 128:(c + 1) * 128], idb[:])
                    return last
                S.op("pe", tp, reads=[XN_r, cst_r], writes=[br])

                def ev(e, g=g, bv=bv, r=r, half=half):
                    last = None
                    for j in range(8):
                        c = g * 8 + j
                        o = HTS[:, c, half * 128:(half + 1) * 128]
                        if plain_src is None:
                            last = e.activation(out=o, in_=bv[:, j * 128:(j + 1) * 128], func=AF.Identity,
                                                scale=gsc[:, r, wh, c:c + 1],
                                                bias=modc[:, r, wh * 48 + c:wh * 48 + c + 1])
                        else:
                            last = e.activation(out=o, in_=bv[:, j * 128:(j + 1) * 128], func=AF.Copy)
                    return last
                S.op("act", ev, reads=[br, gsc_r, modc_r], writes=[HTS_r])
            S.dma("sp", hTt[ti], HTS[:, :, half * 128:(half + 1) * 128], reads=[HTS_r], writes=[hTt_r])
            if half == 1:
                t0 = (ti - 1) * 128

                def st(e, t0=t0):
                    return [e.dma_start(out=hT[q * 512:(q + 1) * 512, t0:t0 + 256].rearrange("(c p) t -> p c t", p=128),
                                        in_=HTS[:, q * 4:(q + 1) * 4, :]) for q in range(4)]
                S.op("sp", st, reads=[HTS_r], writes=[hT_r], dma=4)

    BLKS = [(0, 2048), (2048, 4096), (4096, 4608)]

    def gemm_fm(W, ncols, src, src_r, KC, post):
        for (c0, c1) in BLKS:
            wd = c1 - c0
            bv = BB[:, 0:KC * wd].rearrange("p (c w) -> p c w", c=KC)

            def ldb(e, bv=bv, c0=c0, c1=c1):
                return [e.dma_start(out=bv[:, q * 4:(q + 1) * 4, :],
                                    in_=src[q * 512:(q + 1) * 512, c0:c1].rearrange("(c p) t -> p c t", p=128))
                        for q in range(KC // 4)]
            S.op("sp", ldb, reads=[src_r], writes=[BB_r], dma=KC // 4)
            for cg in range(ncols // 256):
                wi = rot("wa", 2)
                wv = WA[wi][:, 0:KC * 256].rearrange("p (c w) -> p c w", c=KC)

                def ldw(e, wv=wv, cg=cg):
                    return [e.dma_start(out=wv[:, q * 4:(q + 1) * 4, :],
                                        in_=W[q * 512:(q + 1) * 512, cg * 256:(cg + 1) * 256]
                                        .rearrange("(c p) n -> p c n", p=128)) for q in range(KC // 4)]
                S.op("pool", ldw, reads=[in_r], writes=[WA_r[wi]], dma=KC // 4)
                for n in range(2):
                    ci = cg * 2 + n
                    for p0 in range(0, wd, 512):
                        b, br = nb()

                        def mm(e, wv=wv, bv=bv, n=n, p0=p0, b=b):
                            last = None
                            for k in range(KC):
                                last = e.matmul(b[:], wv[:, k, n * 128:(n + 1) * 128], bv[:, k, p0:p0 + 512],
                                                start=(k == 0), stop=(k == KC - 1))
                            return last
                        S.op("pe", mm, reads=[WA_r[wi], BB_r], writes=[br])
                        post(ci, c0 + p0, b, br)

    def post_store_fm(dst, dst_r, rowoff=0):
        def post(ci, t0, b, br):
            si = rot("sb16", 4)
            copy(evq(), SB16[si][:], b[:], [br], [SB16_r[si]])
            S.dma("sp", dst[rowoff + ci * 128:rowoff + (ci + 1) * 128, t0:t0 + 512], SB16[si][:],
                  reads=[SB16_r[si]], writes=[dst_r])
        return post

    def gemm_tm(src, src_r, KC, W, ncols, post, tiles=range(36), tiled=False):
        tiles = list(tiles)
        for jb in range(ncols // 512):
            bv = BB[:, 0:KC * 512].rearrange("p (c w) -> p c w", c=KC)

            def ldb(e, bv=bv, jb=jb):
                return [e.dma_start(out=bv[:, q * 4:(q + 1) * 4, :],
                                    in_=W[q * 512:(q + 1) * 512, jb * 512:(jb + 1) * 512]
                                    .rearrange("(c p) n -> p c n", p=128)) for q in range(KC // 4)]
            S.op("pool", ldb, reads=[in_r], writes=[BB_r], dma=KC // 4)
            for t2 in range(0, len(tiles), 2):
                pair = tiles[t2:t2 + 2]
                wi = rot("wa", 2)
                wv = WA[wi][:, 0:KC * 256].rearrange("p (c w) -> p c w", c=KC)
                t0 = pair[0] * 128

                def lda(e, wv=wv, t0=t0, n=len(pair)):
                    return [e.dma_start(out=wv[:, q * 4:(q + 1) * 4, 0:n * 128],
                                        in_=src[q * 512:(q + 1) * 512, t0:t0 + n * 128]
                                        .rearrange("(c p) t -> p c t", p=128)) for q in range(KC // 4)]
                if tiled:
                    wts_ = [WA[wi][:, pi_ * KC * 128:(pi_ + 1) * KC * 128].rearrange("p (c t) -> p c t", c=KC) for pi_ in range(2)]

                    def lda(e, wts_=wts_, pair=pair):
                        return [e.dma_start(out=wts_[pi_], in_=src[ti_]) for pi_, ti_ in enumerate(pair)]
                    S.op("pool", lda, reads=[src_r], writes=[WA_r[wi]], dma=len(pair))
                else:
                    S.op("pool", lda, reads=[src_r], writes=[WA_r[wi]], dma=KC // 4)
                for pi, ti in enumerate(pair):
                    b, br = nb()

                    def mm(e, wv=wv, bv=bv, pi=pi, b=b, wt_=(wts_[pi] if tiled else None)):
                        last = None
                        for k in range(KC):
                            lh = wt_[:, k, :] if wt_ is not None else wv[:, k, pi * 128:(pi + 1) * 128]
                            last = e.matmul(b[:], lh, bv[:, k, :], start=(k == 0), stop=(k == KC - 1))
                        return last
                    S.op("pe", mm, reads=[WA_r[wi], BB_r], writes=[br])
                    post(ti, jb * 512, b, br)

    def post_residual(xin, which, xout, xout_r, final=False):
        def post(ti, j0, b, br):
            r = 0 if ti < 32 else 1
            g = rot("stg", 6)
            S.dma("sp", STG[g][:], gate_scr[r * 2 + which, :, j0:j0 + 512], reads=[gate_r], writes=[STG_r[g]])
            xi = rot("stg", 6)
            ap, res = xin(ti)
            S.dma("sp", STG[xi][:], ap[:, j0:j0 + 512], reads=[res], writes=[STG_r[xi]])
            S.op("dve", lambda e: e.tensor_tensor(out=STG[g][:], in0=b[:], in1=STG[g][:], op=ALU.mult),
                 reads=[br, STG_r[g]], writes=[STG_r[g]])
            S.op("dve", lambda e: e.tensor_tensor(out=STG[xi][:], in0=STG[xi][:], in1=STG[g][:], op=ALU.add),
                 reads=[STG_r[g], STG_r[xi]], writes=[STG_r[xi]])
            S.dma("sp", xout[ti * 128:(ti + 1) * 128, j0:j0 + 512], STG[xi][:], reads=[STG_r[xi]], writes=[xout_r])
        return post

    def attention():
        KT = BB[:, 0:4 * 4352].rearrange("p (h t) -> p h t", h=4)
        VV = BB[:, 17408:17408 + 34 * 512].rearrange("p (b c) -> p b c", b=34)
        PKT = WA[0][:, 0:2048].rearrange("p (h t) -> p h t", h=4)
        PVV = WA[0][:, 2048:4096].rearrange("p (b c) -> p b c", b=4)
        RC = WA[1][:, 0:4608]
        RS = WA[1][:, 4608:9216]
        MK = WA[1][:, 9216:10240].rearrange("p (a q) -> p a q", a=2)
        S.dma("sp", RC, IN["c_ropeC"], writes=[WA_r[1]])
        S.dma("sp", RS, IN["c_ropeS"], writes=[WA_r[1]])
        S.dma("sp", MK, IN["c_bmask"], writes=[WA_r[1]])
        S.dma("sp", esink[:], sink.partition_broadcast(128), reads=[in_r], writes=[esink_r])
        S.op("act", lambda e: e.activation(out=esink[:], in_=esink[:], func=AF.Exp), reads=[esink_r], writes=[esink_r])
        for h in range(4):
            S.dma("sp", KT[:, h, 0:4096], qT[2048 + h * 128:2048 + (h + 1) * 128, 0:4096], reads=[qT_r], writes=[BB_r])
            S.dma("sp", PKT[:, h, :], qT[2048 + h * 128:2048 + (h + 1) * 128, 4096:4608], reads=[qT_r], writes=[WA_r[0]])
        for h in range(4):
            for p0 in range(0, 4096, 512):
                b, br = nb()
                S.op("pe", lambda e, b=b, h=h, p0=p0: e.matmul(b[:], perm[:], KT[:, h, p0:p0 + 512], start=True, stop=True),
                     reads=[BB_r, cst_r], writes=[br])
                si = rot("stg", 6)
                S.op("dve", lambda e, b=b, si=si, p0=p0: e.tensor_tensor(out=STG[si][:], in0=b[:], in1=RS[:, p0:p0 + 512],
                                                                      op=ALU.mult), reads=[br, WA_r[1]], writes=[STG_r[si]])
                s2 = rot("stg", 6)
                S.op("pool", lambda e, s2=s2, h=h, p0=p0: e.tensor_tensor(out=STG[s2][:], in0=KT[:, h, p0:p0 + 512],
                                                                         in1=RC[:, p0:p0 + 512], op=ALU.mult),
                     reads=[BB_r, WA_r[1]], writes=[STG_r[s2]])
                S.op("dve", lambda e, si=si, s2=s2, h=h, p0=p0: e.tensor_tensor(out=KT[:, h, p0:p0 + 512], in0=STG[si][:],
                                                                               in1=STG[s2][:], op=ALU.add),
                     reads=[STG_r[si], STG_r[s2]], writes=[BB_r])
        for blk in range(2):
            xi = rot("xt", 2)
            S.dma("sp", XT[xi][:, 0:512], ck[blk * 128:(blk + 1) * 128, :], reads=[in_r], writes=[XT_r[xi]])
            S.dma("sp", XT[xi][:, 512:1024], cv[blk * 128:(blk + 1) * 128, :], reads=[in_r], writes=[XT_r[xi]])
            copy("dve", VV[:, 32 + blk, :], XT[xi][:, 512:1024], [XT_r[xi]], [BB_r])
            b, br = nb()

            def tp(e, b=b, xi=xi):
                last = None
                for h in range(4):
                    last = e.transpose(b[:, h * 128:(h + 1) * 128], XT[xi][:, h * 128:(h + 1) * 128], idf[:])
                return last
            S.op("pe", tp, reads=[XT_r[xi], cst_r], writes=[br])
            for h in range(4):
                copy("act", KT[:, h, 4096 + blk * 128:4096 + (blk + 1) * 128], b[:, h * 128:(h + 1) * 128], [br], [BB_r])
        S.dma("sp", VV[:, 0:32, :], vtm[0:4096, :].rearrange("(b p) c -> p b c", p=128), reads=[vtm_r], writes=[BB_r])
        S.dma("sp", PVV, vtm[4096:4608, :].rearrange("(b p) c -> p b c", p=128), reads=[vtm_r], writes=[WA_r[0]])
        QBs = [WA[0][:, 4096 + i * 512:4608 + i * 512].rearrange("p (g q) -> p g q", g=4) for i in range(2)]
        QB_r = [sub(WA_r[0], "QB0"), sub(WA_r[0], "QB1")]
        ones = WA[0][:, 5120:5248]
        ones_r = sub(WA_r[0], "ones")
        PTs = [WA[0][:, 5248 + i * 512:5760 + i * 512] for i in range(12)]
        PT_r = [sub(WA_r[0], f"PT{i}") for i in range(12)]
        pti = [0]
        S.op("dve", lambda e: e.memset(ones, 1.0), writes=[ones_r])
        scale = 128 ** -0.5
        it = 0
        for qb in range(36):
            t0 = qb * 128
            if qb < 32:
                kblocks = [("w", kb) for kb in (qb - 1, qb, qb + 1) if 0 <= kb < 32] + [("c", 0), ("c", 1)]
            else:
                s0 = 32 + ((qb - 32) // 2) * 2
                kblocks = [("p", s0 - 32), ("p", s0 - 31)]
            for g in range(4):
                QB = QBs[it % 2]
                qr = QB_r[it % 2]
                it += 1
                QBf = QB.rearrange("p g q -> p (g q)")
                S.dma("pool", QB, qT[g * 512:(g + 1) * 512, t0:t0 + 128].rearrange("(g p) t -> p g t", p=128),
                      reads=[qT_r], writes=[qr])
                if qb < 32:
                    b, br = nb()
                    S.op("pe", lambda e, b=b, QBf=QBf: e.matmul(b[:], perm[:], QBf, start=True, stop=True),
                         reads=[qr, cst_r], writes=[br])
                    si = rot("stg", 6)
                    s2 = rot("stg", 6)

                    def rp(e, b=b, si=si, s2=s2, QB=QB, t0=t0):
                        last = None
                        for gg in range(4):
                            e.tensor_tensor(out=STG[si][:, gg * 128:(gg + 1) * 128], in0=b[:, gg * 128:(gg + 1) * 128],
                                            in1=RS[:, t0:t0 + 128], op=ALU.mult)
                            last = e.tensor_tensor(out=STG[s2][:, gg * 128:(gg + 1) * 128], in0=QB[:, gg, :],
                                                   in1=RC[:, t0:t0 + 128], op=ALU.mult)
                        return last
                    S.op("dve", rp, reads=[br, qr, WA_r[1]], writes=[STG_r[si], STG_r[s2]])
                    S.op("dve", lambda e, si=si, s2=s2, QBf=QBf: e.tensor_tensor(out=QBf, in0=STG[si][:], in1=STG[s2][:], op=ALU.add),
                         reads=[STG_r[si], STG_r[s2]], writes=[qr])
                nkb = len(kblocks)
                ops_ = []
                for ki, (kind, kb) in enumerate(kblocks):
                    if kind == "w":
                        kt = KT[:, g, kb * 128:(kb + 1) * 128]; vv = VV[:, kb, g * 128:(g + 1) * 128]
                    elif kind == "c":
                        kt = KT[:, g, 4096 + kb * 128:4096 + (kb + 1) * 128]; vv = VV[:, 32 + kb, g * 128:(g + 1) * 128]
                    else:
                        kt = PKT[:, g, kb * 128:(kb + 1) * 128]; vv = PVV[:, kb, g * 128:(g + 1) * 128]
                    pidx = pti[0] % 12
                    pti[0] += 1
                    PT, ptr_ = PTs[pidx], PT_r[pidx]
                    sb_, sbr = nb()
                    S.op("pe", lambda e, sb_=sb_, kt=kt, QBf=QBf: e.matmul(sb_[:], kt, QBf, start=True, stop=True),
                         reads=[BB_r, WA_r[0], qr], writes=[sbr])
                    S.op("act", lambda e, sb_=sb_, PT=PT: e.activation(out=PT, in_=sb_[:], func=AF.Exp, scale=scale),
                         reads=[sbr], writes=[ptr_])
                    if kind == "w" and kb != qb:
                        mi = 0 if kb < qb else 1
                        S.op("dve", lambda e, PT=PT, mi=mi: e.tensor_tensor(out=PT, in0=PT, in1=MK[:, mi, :], op=ALU.mult),
                             reads=[ptr_, WA_r[1]], writes=[ptr_])
                    ops_.append((vv, PT, ptr_))
                ob, obr = nb()
                db, dbr = nb()
                for ki, (vv, PT, ptr_) in enumerate(ops_):
                    S.op("pe", lambda e, ob=ob, vv=vv, PT=PT, ki=ki, nkb=nkb: e.matmul(ob[:], vv, PT, start=(ki == 0),
                                                                                   stop=(ki == nkb - 1)),
                         reads=[BB_r, WA_r[0], ptr_], writes=[obr])
                    S.op("pe", lambda e, db=db, PT=PT, ki=ki, nkb=nkb: e.matmul(db[:], ones, PT, start=(ki == 0),
                                                                            stop=(ki == nkb - 1)),
                         reads=[ones_r, ptr_], writes=[dbr])
                si = rot("stg", 6)

                def dn(e, db=db, si=si, g=g):
                    last = None
                    for gg in range(4):
                        last = e.tensor_scalar(out=STG[si][:, gg * 128:(gg + 1) * 128], in0=db[:, gg * 128:(gg + 1) * 128],
                                               scalar1=esink[:, g * 4 + gg:g * 4 + gg + 1], scalar2=None, op0=ALU.add)
                    return last
                S.op("dve", dn, reads=[dbr, esink_r], writes=[STG_r[si]])
                S.op("act", lambda e, si=si: e.activation(out=STG[si][:], in_=STG[si][:], func=AF.Ln), reads=[STG_r[si]], writes=[STG_r[si]])
                S.op("act", lambda e, si=si: e.activation(out=STG[si][:], in_=STG[si][:], func=AF.Exp, scale=-1.0),
                     reads=[STG_r[si]], writes=[STG_r[si]])
                oi = rot("sb16", 4)
                S.op("dve", lambda e, ob=ob, si=si, oi=oi: e.tensor_tensor(out=SB16[oi][:], in0=ob[:], in1=STG[si][:],
                                                                          op=ALU.mult),
                     reads=[obr, STG_r[si]], writes=[SB16_r[oi]])
                S.dma("sp", oT[qb][:, g * 4:(g + 1) * 4, :],
                      SB16[oi][:].rearrange("p (g q) -> p g q", g=4), reads=[SB16_r[oi]], writes=[oT_r])
        join(WA_r[0], QB_r + PT_r + [ones_r])

    def ffn_act(i):
        S.dma("sp", stg_s[:], ffn_convT[i, 0:128, :], reads=[in_r], writes=[stg_r])
        b, br = nb()
        S.op("pe", lambda e: e.transpose(b[:, 0:128], stg_s[:], idf[:]), reads=[stg_r, cst_r], writes=[br])
        copy("act", fcv[:].rearrange("p a c -> p (a c)")[:, 0:128], b[:, 0:128], [br], [fcv_r])
        S.dma("sp", stg_s[0:4, :], ffn_convT[i, 128:132, :], reads=[in_r], writes=[stg_r])
        b2, b2r = nb()
        S.op("pe", lambda e: e.transpose(b2[:, 0:4], stg_s[0:4, :], idf[0:4, 0:4]), reads=[stg_r, cst_r], writes=[b2r])
        copy("act", fcv[:].rearrange("p a c -> p (a c)")[:, 128:132], b2[:, 0:4], [b2r], [fcv_r])
        bufs = []
        for i2 in range(2):
            bufs.append((BB[:, (3 * i2) * NT:(3 * i2 + 1) * NT], BB[:, (3 * i2 + 1) * NT:(3 * i2 + 2) * NT],
                         BB[:, (3 * i2 + 2) * NT:(3 * i2 + 3) * NT],
                         sub(BB_r, f"fa{i2}"), sub(BB_r, f"fb{i2}"), sub(BB_r, f"fg{i2}")))
        PARTS = [(0, 2304), (2304, 4608)]
        accp_r = [sub(ACC_r, f"accq{i_}") for i_ in range(2)]
        for j in range(44):
            AA, BBb, GG, ar, brr, gr = bufs[j % 2]
            S.dma("sp", AA, abT[j * 128:(j + 1) * 128, :], reads=[abT_r], writes=[ar])
            S.dma("sp", BBb, abT[DFF + j * 128:DFF + (j + 1) * 128, :], reads=[abT_r], writes=[brr])
            gpr = [sub(gr, f"gq{j}_{i_}") for i_ in range(2)]
            for pi_, (p0, p1) in enumerate(PARTS):
                acr = accp_r[pi_]
                S.op("act", lambda e, j=j, AA=AA, p0=p0, p1=p1: e.activation(out=ACC[:, p0:p1], in_=AA[:, p0:p1], func=AF.Copy,
                                                                          scale=fcv[:, 1, j:j + 1]),
                     reads=[ar, fcv_r], writes=[acr])
                for tap, (lo, hi) in ((0, (1, 0)), (2, (0, 1))):
                    rngs = []
                    for (s0, s1) in SEQS:
                        o0, o1 = max(s0 + lo, p0), min(s1 - hi, p1)
                        if o1 > o0:
                            rngs.append((o0, o1))

                    def taps(e, j=j, AA=AA, tap=tap, lo=lo, hi=hi, rngs=rngs):
                        last = None
                        for (o0, o1) in rngs:
                            last = e.scalar_tensor_tensor(out=ACC[:, o0:o1], in0=AA[:, o0 - lo + hi:o1 - lo + hi],
                                                          scalar=fcv[:, tap, j:j + 1], in1=ACC[:, o0:o1],
                                                          op0=ALU.mult, op1=ALU.add)
                        return last
                    S.op("dve", taps, reads=[ar, fcv_r, acr], writes=[acr])
                S.op("act", lambda e, GG=GG, p0=p0, p1=p1: e.activation(out=GG[:, p0:p1], in_=ACC[:, p0:p1], func=AF.Silu),
                     reads=[acr], writes=[gpr[pi_]])
                S.op("pool", lambda e, GG=GG, BBb=BBb, p0=p0, p1=p1: e.tensor_tensor(out=GG[:, p0:p1], in0=GG[:, p0:p1],
                                                                                   in1=BBb[:, p0:p1], op=ALU.mult),
                     reads=[gpr[pi_], brr], writes=[gpr[pi_]])
                S.dma("sp", gT[j * 128:(j + 1) * 128, p0:p1], GG[:, p0:p1], reads=[gpr[pi_]], writes=[gT_r])
            join(gr, gpr)
        join(ACC_r, accp_r)
        join(BB_r, [x for bf_ in bufs for x in bf_[3:]])

    CVT = [S.sb(f"CVT{i}", [128, 512], BF16) for i in range(6)]
    CVT_r = [Res(f"CVT{i}") for i in range(6)]
    rr["cvt"] = 0
    hyc = S.sb("hyc", [128, 16], F32); hyc_r = Res("hyc")
    wf1 = S.sb("wf1", [33, 64], F32); wf2 = S.sb("wf2", [64, 64], F32); wf_r = Res("wf")
    hsdP = dscr("hsdP", [2, LP, 2 * D], BF16); hsdP_r = Res("hsdP", True)
    ghatP = dscr("ghatP", [2, LP, 2 * D], BF16); ghatP_r = Res("ghatP", True)
    yhatP = dscr("yhatP", [2, 2 * LP, D], BF16); yhatP_r = Res("yhatP", True)

    def post_ptm(ti, j0, b, br):
        si = rot("sb16", 4)
        copy(evq(), SB16[si][:], b[:], [br], [SB16_r[si]])
        S.dma("sp", ptm[ti * 128:(ti + 1) * 128, j0:j0 + 512], SB16[si][:], reads=[SB16_r[si]], writes=[ptm_r])

    def hy_conv3():
        starts = {0, 32, 34}
        ends = {31, 33, 35}
        W2 = 2048
        ins_ = [[BB[:, (k * 2 + i2) * W2:(k * 2 + i2 + 1) * W2] for i2 in range(2)] for k in range(3)]
        in_r_ = [[sub(BB_r, f"cv{k}{i2}") for i2 in range(2)] for k in range(3)]
        wts = [BB[:, 12288 + k * 4096:12288 + (k + 1) * 4096].bitcast(F32) for k in range(3)]
        wt_r = [sub(BB_r, f"cw{k}") for k in range(3)]
        accs = [BB[:, 24576 + i2 * 4096:24576 + (i2 + 1) * 4096].bitcast(F32) for i2 in range(2)]
        acc_r = [sub(BB_r, f"ca{i2}") for i2 in range(2)]
        T1, T2 = ACC[:, 0:W2], ACC[:, W2:2 * W2]
        T1_r, T2_r = sub(ACC_r, "T1"), sub(ACC_r, "T2")
        it = 0
        for cb in range(3):
            c0 = cb * W2
            for k in range(3):
                S.dma("sp", wts[k], hy_conv[k, c0:c0 + W2].partition_broadcast(128), reads=[in_r], writes=[wt_r[k]])
            def loads(ti, i2):
                t0 = ti * 128
                pc, pm, pp = ins_[0][i2], ins_[1][i2], ins_[2][i2]
                pcr, pmr, ppr = in_r_[0][i2], in_r_[1][i2], in_r_[2][i2]
                S.dma("act", pc, ptm[t0:t0 + 128, c0:c0 + W2], reads=[ptm_r], writes=[pcr])
                if ti in starts:
                    S.op("pool", lambda e, pm=pm: e.memset(pm, 0.0), writes=[pmr])
                    S.dma("act", pm[1:128, :], ptm[t0:t0 + 127, c0:c0 + W2], reads=[ptm_r], writes=[pmr])
                else:
                    S.dma("act", pm, ptm[t0 - 1:t0 + 127, c0:c0 + W2], reads=[ptm_r], writes=[pmr])
                if ti in ends:
                    S.op("pool", lambda e, pp=pp: e.memset(pp, 0.0), writes=[ppr])
                    S.dma("act", pp[0:127, :], ptm[t0 + 1:t0 + 128, c0:c0 + W2], reads=[ptm_r], writes=[ppr])
                else:
                    S.dma("act", pp, ptm[t0 + 1:t0 + 129, c0:c0 + W2], reads=[ptm_r], writes=[ppr])
            loads(0, it % 2)
            for ti in range(36):
                t0 = ti * 128
                i2 = it % 2
                it += 1
                if ti + 1 < 36:
                    loads(ti + 1, it % 2)
                pc, pm, pp = ins_[0][i2], ins_[1][i2], ins_[2][i2]
                pcr, pmr, ppr = in_r_[0][i2], in_r_[1][i2], in_r_[2][i2]
                ac, acr = accs[i2], acc_r[i2]
                S.op("dve", lambda e, ac=ac, pc=pc: e.tensor_tensor(out=ac, in0=pc, in1=wts[1], op=ALU.mult),
                     reads=[pcr, wt_r[1]], writes=[acr])
                S.op("pool", lambda e, pm=pm: e.tensor_tensor(out=T1, in0=pm, in1=wts[0], op=ALU.mult),
                     reads=[pmr, wt_r[0]], writes=[T1_r])
                S.op("dve", lambda e, pp=pp: e.tensor_tensor(out=T2, in0=pp, in1=wts[2], op=ALU.mult),
                     reads=[ppr, wt_r[2]], writes=[T2_r])
                S.op("pool", lambda e, ac=ac: e.tensor_tensor(out=ac, in0=ac, in1=T1, op=ALU.add),
                     reads=[acr, T1_r], writes=[acr])
                S.op("dve", lambda e, ac=ac, pc=pc: e.tensor_tensor(out=pc, in0=ac, in1=T2, op=ALU.add),
                     reads=[acr, T2_r], writes=[pcr])
                S.dma("sp", c3[t0:t0 + 128, c0:c0 + W2], pc, reads=[pcr], writes=[c3_r])
        join(BB_r, [x for l in in_r_ for x in l] + wt_r + acc_r)
        join(ACC_r, [T1_r, T2_r])

    def hy_filters(nm, L, hs_dst, hs_dst_r):
        featT = IN["c_featT" + nm]
        FT = ACC[0:33, 0:L]
        S.dma("sp", FT, featT, writes=[ACC_r])
        S.dma("sp", wf1[:], hy_w_f1, reads=[in_r], writes=[wf_r])
        S.dma("sp", wf2[:], hy_w_f2, reads=[in_r], writes=[wf_r])
        S.dma("sp", hyc[0:64, 0:3], hy_small, reads=[in_r], writes=[hyc_r])
        def cols0(e):
            e.tensor_scalar(out=hyc[0:64, 3:4], in0=hyc[0:64, 2:3], scalar1=0.5, scalar2=None, op0=ALU.mult)
            return e.tensor_scalar(out=hyc[0:64, 4:5], in0=hyc[0:64, 2:3], scalar1=0.25, scalar2=None, op0=ALU.mult)
        S.op("dve", cols0, reads=[hyc_r], writes=[hyc_r])

        def cols(e):
            e.tensor_tensor(out=hyc[0:64, 5:6], in0=hyc[0:64, 3:4], in1=hyc[0:64, 0:1], op=ALU.mult)
            e.tensor_tensor(out=hyc[0:64, 6:7], in0=hyc[0:64, 4:5], in1=hyc[0:64, 0:1], op=ALU.mult)
            e.tensor_tensor(out=hyc[0:64, 7:8], in0=hyc[0:64, 3:4], in1=hyc[0:64, 1:2], op=ALU.mult)
            return e.tensor_tensor(out=hyc[0:64, 8:9], in0=hyc[0:64, 4:5], in1=hyc[0:64, 1:2], op=ALU.mult)
        S.op("dve", cols, reads=[hyc_r], writes=[hyc_r])
        H1 = BB[:, 0:8192].bitcast(F32)
        H2 = BB[:, 8192:16384].bitcast(F32)
        H2b = BB[:, 16384:20480]
        W3 = WA[0][0:64, 0:8192]
        S.dma("pool", W3, hy_w_f3, reads=[in_r], writes=[WA_r[0]])

        def sin_layer(lhsT, K, src, src_r, dst, bcol):
            for p0 in range(0, L, 512):
                w = min(512, L - p0)
                b, br = nb()
                S.op("pe", lambda e, b=b, p0=p0, w=w: e.matmul(b[0:64, 0:w], lhsT, src[0:K, p0:p0 + w], start=True, stop=True),
                     reads=[src_r, wf_r], writes=[br])
                s2, s4 = rot("stg", 6), rot("stg", 6)
                S.op("act", lambda e, b=b, s2=s2, w=w: e.activation(out=STG[s2][0:64, 0:w], in_=b[0:64, 0:w], func=AF.Sin,
                                                                   scale=hyc[0:64, 3:4], bias=hyc[0:64, bcol:bcol + 1]),
                     reads=[br, hyc_r], writes=[STG_r[s2]])
                S.op("act", lambda e, b=b, s4=s4, w=w: e.activation(out=STG[s4][0:64, 0:w], in_=b[0:64, 0:w], func=AF.Sin,
                                                                   scale=hyc[0:64, 4:5], bias=hyc[0:64, bcol + 1:bcol + 2]),
                     reads=[br, hyc_r], writes=[STG_r[s4]])
                S.op("dve", lambda e, s4=s4, w=w: e.tensor_tensor(out=STG[s4][0:64, 0:w], in0=STG[s4][0:64, 0:w],
                                                                 in1=STG[s4][0:64, 0:w], op=ALU.mult),
                     reads=[STG_r[s4]], writes=[STG_r[s4]])
                S.op("dve", lambda e, s4=s4, w=w: e.tensor_scalar(out=STG[s4][0:64, 0:w], in0=STG[s4][0:64, 0:w], scalar1=-2.0,
                                                                 scalar2=1.0, op0=ALU.mult, op1=ALU.add),
                     reads=[STG_r[s4]], writes=[STG_r[s4]])
                S.op("dve", lambda e, s2=s2, s4=s4, w=w, p0=p0: e.scalar_tensor_tensor(
                    out=dst[0:64, p0:p0 + w], in0=STG[s2][0:64, 0:w], scalar=2.0, in1=STG[s4][0:64, 0:w],
                    op0=ALU.mult, op1=ALU.mult), reads=[STG_r[s2], STG_r[s4]], writes=[BB_r])
        sin_layer(wf1[:], 33, ACC, ACC_r, H1, 5)
        sin_layer(wf2[:], 64, H1, BB_r, H2, 7)
        copy("dve", H2b[0:64, 0:L], H2[0:64, 0:L], [BB_r], [BB_r])
        decf, decb = IN["c_decf" + nm], IN["c_decb" + nm]
        dtl = [(XT[0][:], XT[1][:], sub(XT_r[0], "dcf0"), sub(XT_r[1], "dcb0")),
               (ACC[:, 0:2048], ACC[:, 2048:4096], sub(ACC_r, "dcf1"), sub(ACC_r, "dcb1"))]
        for tc in range(L // 128):
            dF, dB, dFr, dBr = dtl[tc % 2]
            S.dma("act", dF, decf[tc * 128:(tc + 1) * 128, :], writes=[dFr])
            S.dma("act", dB, decb[tc * 128:(tc + 1) * 128, :], writes=[dBr])
            for o in range(2):
                for db in range(4):
                    cf = o * 2048 + db * 512
                    cs = slice(db * 512, (db + 1) * 512)
                    bf_, bfr = nb()
                    bb_, bbr = nb()
                    S.op("pe", lambda e, bf_=bf_, tc=tc, cf=cf: e.matmul(bf_[:], H2b[0:64, tc * 128:(tc + 1) * 128],
                                                                      W3[:, cf:cf + 512], start=True, stop=True),
                         reads=[BB_r, WA_r[0]], writes=[bfr])
                    S.op("pe", lambda e, bb_=bb_, tc=tc, cf=cf: e.matmul(bb_[:], H2b[0:64, tc * 128:(tc + 1) * 128],
                                                                      W3[:, 4096 + cf:4096 + cf + 512], start=True, stop=True),
                         reads=[BB_r, WA_r[0]], writes=[bbr])
                    d0, d1 = rot("stg", 6), rot("stg", 6)
                    S.op("dve", lambda e, bf_=bf_, d0=d0, dF=dF, cs=cs: e.tensor_tensor(out=STG[d0][:], in0=bf_[:], in1=dF[:, cs], op=ALU.mult),
                         reads=[bfr, dFr], writes=[STG_r[d0]])
                    S.op("dve", lambda e, bb_=bb_, d1=d1, dB=dB, cs=cs: e.tensor_tensor(out=STG[d1][:], in0=bb_[:], in1=dB[:, cs], op=ALU.mult),
                         reads=[bbr, dBr], writes=[STG_r[d1]])
                    c0_, c1_ = rot("cvt", 6), rot("cvt", 6)
                    S.op("pool", lambda e, d0=d0, d1=d1, c0_=c0_: e.tensor_tensor(out=CVT[c0_][:], in0=STG[d0][:], in1=STG[d1][:],
                                                                              op=ALU.add),
                         reads=[STG_r[d0], STG_r[d1]], writes=[CVT_r[c0_]])
                    S.op("pool", lambda e, d0=d0, d1=d1, c1_=c1_: e.tensor_tensor(out=CVT[c1_][:], in0=STG[d1][:], in1=STG[d0][:],
                                                                              op=ALU.subtract),
                         reads=[STG_r[d0], STG_r[d1]], writes=[CVT_r[c1_]])
                    S.dma("sp", hs_dst[0, tc * 128:(tc + 1) * 128, cf:cf + 512], CVT[c0_][:], reads=[CVT_r[c0_]], writes=[hs_dst_r])
                    S.dma("sp", hs_dst[1, tc * 128:(tc + 1) * 128, cf:cf + 512], CVT[c1_][:], reads=[CVT_r[c1_]], writes=[hs_dst_r])
        join(XT_r[0], [dtl[0][2]]); join(XT_r[1], [dtl[0][3]]); join(ACC_r, [dtl[1][2], dtl[1][3]])

    def dft_gemm(A_t, parts, nI, CC, B_r, ncols, JB, post):
        for j0 in range(0, ncols, JB):
            bvs = []
            for pi_, (ioff, bfn) in enumerate(parts):
                if pi_ > 0 and bfn is parts[0][1]:
                    bvs.append(bvs[0])
                    continue
                off = pi_ * CC * JB
                bv = BB[:, off:off + CC * JB].rearrange("p (c w) -> p c w", c=CC)
                src = bfn(j0, JB)

                def ldb(e, bv=bv, src=src):
                    n = max(1, CC // 8)
                    step = CC // n
                    return [e.dma_start(out=bv[:, q * step:(q + 1) * step, :],
                                        in_=src[q * step * 128:(q + 1) * step * 128, :].rearrange("(c p) w -> p c w", p=128))
                            for q in range(n)]
                S.op("sp", ldb, reads=[B_r], writes=[BB_r], dma=max(1, CC // 8))
                bvs.append(bv)
            for i in range(nI):
                wi = rot("wa", 2)
                avs = []
                for pi_, (ioff, bfn) in enumerate(parts):
                    av = WA[wi][:, pi_ * CC * 128:(pi_ + 1) * CC * 128].rearrange("p (c m) -> p c m", c=CC)
                    S.dma("pool", av, A_t[ioff + i], writes=[WA_r[wi]])
                    avs.append(av)
                res = []
                for pi_ in range(len(parts)):
                    bl = []
                    for h0 in range(0, JB, 512):
                        b, br = nb()

                        def mm(e, av=avs[pi_], bv=bvs[pi_], h0=h0, b=b):
                            last = None
                            for c in range(CC):
                                last = e.matmul(b[:], av[:, c, :], bv[:, c, h0:h0 + 512], start=(c == 0), stop=(c == CC - 1))
                            return last
                        S.op("pe", mm, reads=[WA_r[wi], BB_r], writes=[br])
                        bl.append((b, br))
                    res.append(bl)
                post(i, j0, res)

    def hy_seq(nm, L, row0, hs_, hs_r_, gh_, gh_r_, yh_, yh_r_):
        CC = L // 128
        fwd_t, inv_t = IN["c_fwd" + nm], IN["c_inv" + nm]
        for part in range(2):
            def post_g(i, j0, res, part=part):
                for h, (b, br) in enumerate(res[0]):
                    ci = rot("cvt", 6)
                    copy(evq(), CVT[ci][:], b[:], [br], [CVT_r[ci]])
                    S.dma("sp", gh_[part, i * 128:(i + 1) * 128, j0 + h * 512:j0 + (h + 1) * 512], CVT[ci][:],
                          reads=[CVT_r[ci]], writes=[gh_r_])
            dft_gemm(fwd_t, [(part * CC, lambda j0, JB, part=part: hs_[part, :, j0:j0 + JB])], CC, CC, hs_r_, 2 * D, 1024, post_g)
        for o in range(2):
            if o == 0:
                vsrc = lambda j0, JB: c3[row0:row0 + L, 2 * D + j0:2 * D + j0 + JB]
                v_r = c3_r
            else:
                vsrc = lambda j0, JB: z1[row0:row0 + L, j0:j0 + JB]
                v_r = z1_r

            def post_f(i, j0, res, o=o):
                for h in range(len(res[0])):
                    (a, ar), (b, br) = res[0][h], res[1][h]
                    cg_, sg_ = rot("cvt", 6), rot("cvt", 6)
                    col = o * D + j0 + h * 512
                    S.dma("sp", CVT[cg_][:], gh_[0, i * 128:(i + 1) * 128, col:col + 512], reads=[gh_r_], writes=[CVT_r[cg_]])
                    S.dma("sp", CVT[sg_][:], gh_[1, i * 128:(i + 1) * 128, col:col + 512], reads=[gh_r_], writes=[CVT_r[sg_]])
                    t1, t2 = rot("stg", 6), rot("stg", 6)
                    S.op("dve", lambda e, a=a, t1=t1, cg_=cg_: e.tensor_tensor(out=STG[t1][:], in0=a[:], in1=CVT[cg_][:], op=ALU.mult),
                         reads=[ar, CVT_r[cg_]], writes=[STG_r[t1]])
                    S.op("dve", lambda e, b=b, t2=t2, sg_=sg_: e.tensor_tensor(out=STG[t2][:], in0=b[:], in1=CVT[sg_][:], op=ALU.mult),
                         reads=[br, CVT_r[sg_]], writes=[STG_r[t2]])
                    y0 = rot("sb16", 4)
                    S.op("dve", lambda e, t1=t1, t2=t2, y0=y0: e.tensor_tensor(out=SB16[y0][:], in0=STG[t1][:], in1=STG[t2][:], op=ALU.add),
                         reads=[STG_r[t1], STG_r[t2]], writes=[SB16_r[y0]])
                    S.dma("sp", yh_[o, i * 128:(i + 1) * 128, j0 + h * 512:j0 + (h + 1) * 512], SB16[y0][:],
                          reads=[SB16_r[y0]], writes=[yh_r_])
                    t3, t4 = rot("stg", 6), rot("stg", 6)
                    S.op("dve", lambda e, b=b, t3=t3, cg_=cg_: e.tensor_tensor(out=STG[t3][:], in0=b[:], in1=CVT[cg_][:], op=ALU.mult),
                         reads=[br, CVT_r[cg_]], writes=[STG_r[t3]])
                    S.op("dve", lambda e, a=a, t4=t4, sg_=sg_: e.tensor_tensor(out=STG[t4][:], in0=a[:], in1=CVT[sg_][:], op=ALU.mult),
                         reads=[ar, CVT_r[sg_]], writes=[STG_r[t4]])
                    y1 = rot("sb16", 4)
                    S.op("dve", lambda e, t3=t3, t4=t4, y1=y1: e.tensor_tensor(out=SB16[y1][:], in0=STG[t3][:], in1=STG[t4][:],
                                                                             op=ALU.subtract),
                         reads=[STG_r[t3], STG_r[t4]], writes=[SB16_r[y1]])
                    S.dma("sp", yh_[o, L + i * 128:L + (i + 1) * 128, j0 + h * 512:j0 + (h + 1) * 512], SB16[y1][:],
                          reads=[SB16_r[y1]], writes=[yh_r_])
            dft_gemm(fwd_t, [(0, vsrc), (CC, vsrc)], CC, CC, v_r, D, 1024, post_f)

            def post_i(i, j0, res, o=o):
                (y, yr) = res[0][0]
                t0 = row0 + i * 128
                bi_ = rot("stg", 6)
                S.dma("sp", STG[bi_][:], hy_bias[o, j0:j0 + 512].partition_broadcast(128), reads=[in_r], writes=[STG_r[bi_]])
                vv, gg = rot("cvt", 6), rot("cvt", 6)
                if o == 0:
                    S.dma("sp", CVT[vv][:], c3[t0:t0 + 128, 2 * D + j0:2 * D + j0 + 512], reads=[c3_r], writes=[CVT_r[vv]])
                    S.dma("sp", CVT[gg][:], c3[t0:t0 + 128, j0:j0 + 512], reads=[c3_r], writes=[CVT_r[gg]])
                else:
                    S.dma("sp", CVT[vv][:], z1[t0:t0 + 128, j0:j0 + 512], reads=[z1_r], writes=[CVT_r[vv]])
                    S.dma("sp", CVT[gg][:], c3[t0:t0 + 128, D + j0:D + j0 + 512], reads=[c3_r], writes=[CVT_r[gg]])
                S.op("dve", lambda e, bi_=bi_, vv=vv: e.tensor_tensor(out=STG[bi_][:], in0=CVT[vv][:], in1=STG[bi_][:], op=ALU.mult),
                     reads=[CVT_r[vv], STG_r[bi_]], writes=[STG_r[bi_]])
                S.op("dve", lambda e, y=y, bi_=bi_: e.tensor_tensor(out=STG[bi_][:], in0=y[:], in1=STG[bi_][:], op=ALU.add),
                     reads=[yr, STG_r[bi_]], writes=[STG_r[bi_]])
                S.op("dve", lambda e, bi_=bi_, gg=gg, vv=vv: e.tensor_tensor(out=CVT[vv][:], in0=STG[bi_][:], in1=CVT[gg][:], op=ALU.mult),
                     reads=[STG_r[bi_], CVT_r[gg]], writes=[CVT_r[vv]])
                dst, dr = (z1, z1_r) if o == 0 else (zz, zz_r)
                S.dma("sp", dst[t0:t0 + 128, j0:j0 + 512], CVT[vv][:], reads=[CVT_r[vv]], writes=[dr])
            dft_gemm(inv_t, [(0, lambda j0, JB, o=o: yh_[o, :, j0:j0 + JB])], CC, 2 * CC, yh_r_, D, 512, post_i)


    PL = {"fp": [], "bf": [], "fi": 0, "bi": 0, "subs": []}

    def pools_open():
        fp = [(STG[i][:], STG_r[i]) for i in range(1, 6)]
        bfp = [(CVT[i][:], CVT_r[i]) for i in range(6)] + [(SB16[i][:], SB16_r[i]) for i in range(4)]
        subs = []
        for k in range(2):
            for q in range(4):
                r_ = sub(XT_r[k], f"xtp{k}{q}")
                subs.append((XT_r[k], r_))
                fp.append((XT[k][:, q * 512:(q + 1) * 512], r_))
        for q in range(9):
            r_ = sub(ACC_r, f"accp{q}")
            subs.append((ACC_r, r_))
            fp.append((ACC[:, q * 512:(q + 1) * 512], r_))
        hv = HTS[:].rearrange("p c t -> p (c t)")
        for q in range(8):
            r_ = sub(HTS_r, f"htsp{q}")
            subs.append((HTS_r, r_))
            bfp.append((hv[:, q * 512:(q + 1) * 512], r_))
        for (t_, tr_, nm_) in ((XN, XN_r, "xnp"), (JK, JK_r, "jkp")):
            for q in range(4):
                r_ = sub(tr_, f"{nm_}{q}")
                subs.append((tr_, r_))
                bfp.append((t_[:, q * 512:(q + 1) * 512], r_))
        PL["fp"], PL["bf"], PL["subs"] = fp, bfp, subs

    def pools_close():
        for parent, r_ in PL["subs"]:
            join(parent, [r_])
        PL["fp"], PL["bf"], PL["subs"] = [], [], []

    def fpt():
        i = PL["fi"] % len(PL["fp"])
        PL["fi"] += 1
        return PL["fp"][i]

    def bft():
        i = PL["bi"] % len(PL["bf"])
        PL["bi"] += 1
        return PL["bf"][i]

    B1 = dscr("B1", [2, 128, 32, D], BF16); B1_r = Res("B1", True)
    D1 = dscr("D1", [2, 128, 32, D], BF16); D1_r = Res("D1", True)
    G2 = dscr("G2", [2, 128, 32, 2 * D], BF16); G2_r = Res("G2", True)
    fcs = S.sb("fcs", [128, 8, 128], BF16); fcs_r = Res("fcs")
    tws = S.sb("tws", [128, 2, 64], F32); tws_r = Res("tws")

    def fft_consts():
        for i_, nm_ in enumerate(("f1c", "f1s", "i1c", "i1s", "cbm", "sbm", "ncbm", "nsbm")):
            S.dma("sp", fcs[:, i_, :], IN["c_" + nm_], writes=[fcs_r])
        S.dma("sp", tws[:, 0, :], IN["c_tw1"], writes=[tws_r])
        S.dma("sp", tws[:, 1, :], IN["c_tw2"], writes=[tws_r])

    BT = {"t": [], "i": 0, "subs": []}

    def bt_open():
        t, subs = [], []
        for (tile_, tr_, nm_, n_) in ((ACC, ACC_r, "ba", 4), (XT[0], XT_r[0], "bx0", 2), (XT[1], XT_r[1], "bx1", 2)):
            v = tile_[:].bitcast(BF16)
            for q in range(n_):
                r_ = sub(tr_, f"{nm_}{q}")
                subs.append((tr_, r_))
                t.append((v[:, q * 2048:(q + 1) * 2048], r_))
        hv = HTS[:].rearrange("p c t -> p (c t)")
        for q in range(2):
            r_ = sub(HTS_r, f"bh{q}")
            subs.append((HTS_r, r_))
            t.append((hv[:, q * 2048:(q + 1) * 2048], r_))
        for (tile_, tr_, nm_) in ((XN, XN_r, "bxn"), (JK, JK_r, "bjk")):
            r_ = sub(tr_, nm_)
            subs.append((tr_, r_))
            t.append((tile_[:], r_))
        BT["t"], BT["subs"] = t, subs

    def bt_close():
        for parent, r_ in BT["subs"]:
            join(parent, [r_])
        BT["t"], BT["subs"] = [], []

    def btt():
        i = BT["i"] % len(BT["t"])
        BT["i"] += 1
        return BT["t"][i]

    rr["sfp"] = 0

    def sfp():
        i = rot("sfp", 10)
        return (STG[i][:], STG_r[i]) if i < 6 else (STX[i - 6][:], STX_r[i - 6])

    def sbf():
        k = rot("sbf", 10)
        return (CVT[k][:], CVT_r[k]) if k < 6 else (SB16[k - 6][:], SB16_r[k - 6])
    rr["sbf"] = 0

    def twiddle_evac(pb, pbr, qb_, qbr, ti_, col, o1, o1r, o2, o2r):
        cc = tws[:, ti_, col:col + 1]
        ss = tws[:, ti_, 32 + col:32 + col + 1]
        (u1, u1r), (u2, u2r) = sfp(), sfp()
        S.op("act", lambda e: e.activation(out=u1, in_=qb_[:], func=AF.Copy, scale=ss), reads=[qbr, tws_r], writes=[u1r])
        S.op("act", lambda e: e.activation(out=u2, in_=qb_[:], func=AF.Copy, scale=cc), reads=[qbr, tws_r], writes=[u2r])
        S.op("dve", lambda e: e.scalar_tensor_tensor(out=o1, in0=pb[:], scalar=cc, in1=u1, op0=ALU.mult, op1=ALU.subtract),
             reads=[pbr, u1r, tws_r], writes=[o1r])
        S.op("dve", lambda e: e.scalar_tensor_tensor(out=o2, in0=pb[:], scalar=ss, in1=u2, op0=ALU.mult, op1=ALU.add),
             reads=[pbr, u2r, tws_r], writes=[o2r])

    def fft_s1(src_fn, src_r):
        xb = [BB[:, i2 * 16384:(i2 + 1) * 16384].rearrange("p (r c) -> p r c", r=8) for i2 in range(2)]
        xb_r = [sub(BB_r, f"xb{i2}") for i2 in range(2)]
        srcv = src_fn().rearrange("(a r) c -> a r c", r=32)
        for rq in range(4):
            i2 = rq % 2

            def ld(e, i2=i2, rq=rq):
                return [e.dma_start(out=xb[i2][:, q * 2:(q + 1) * 2, :], in_=srcv[:, rq * 8 + q * 2:rq * 8 + (q + 1) * 2, :])
                        for q in range(4)]
            S.op("pool", ld, reads=[src_r], writes=[xb_r[i2]], dma=4)
            for r8 in range(8):
                r = rq * 8 + r8
                (ore, orr), (oim, oir) = btt(), btt()
                for db in range(4):
                    cs = slice(db * 512, (db + 1) * 512)
                    pb, pbr = nb()
                    qb_, qbr = nb()
                    S.op("pe", lambda e, pb=pb, i2=i2, r8=r8, cs=cs: e.matmul(pb[:], fcs[:, 0, :], xb[i2][:, r8, cs], start=True, stop=True),
                         reads=[xb_r[i2], fcs_r], writes=[pbr])
                    S.op("pe", lambda e, qb_=qb_, i2=i2, r8=r8, cs=cs: e.matmul(qb_[:], fcs[:, 1, :], xb[i2][:, r8, cs], start=True, stop=True),
                         reads=[xb_r[i2], fcs_r], writes=[qbr])
                    twiddle_evac(pb, pbr, qb_, qbr, 0, r, ore[:, cs], orr, oim[:, cs], oir)
                S.dma("sp", B1[0, :, r, :], ore, reads=[orr], writes=[B1_r])
                S.dma("sp", B1[1, :, r, :], oim, reads=[oir], writes=[B1_r])
        join(BB_r, xb_r)

    def fft_s2_load(src, src_r, jq, i2, bufs, bufs_r):
        for c_ in range(2):
            v = src[c_].rearrange("(j q) r d -> (q r) j d", q=4)
            dst = bufs[i2][c_]

            def ld(e, v=v, dst=dst, jq=jq):
                return [e.dma_start(out=dst[:, q * 2:(q + 1) * 2, :], in_=v[:, jq * 8 + q * 2:jq * 8 + (q + 1) * 2, :]) for q in range(4)]
            S.op("pool", ld, reads=[src_r], writes=[bufs_r[i2][c_]], dma=4)

    def s2_bufs():
        bufs = [[BB[:, (i2 * 2 + c_) * 8192:(i2 * 2 + c_ + 1) * 8192].rearrange("p (j c) -> p j c", j=4) for c_ in range(2)]
                for i2 in range(2)]
        bufs_r = [[sub(BB_r, f"s2b{i2}{c_}") for c_ in range(2)] for i2 in range(2)]
        return bufs, bufs_r

    def fft_filters(o):
        for part in range(2):
            fft_s1(lambda part=part: hsd[part, :, o * D:(o + 1) * D], hsd_r)
            bufs, bufs_r = s2_bufs()
            for jq in range(8):
                i2 = jq % 2
                for c_ in range(2):
                    v = B1[c_].rearrange("(j q) r d -> (q r) j d", q=4)

                    def ld(e, v=v, dst=bufs[i2][c_], jq=jq):
                        return [e.dma_start(out=dst[:, q:q + 1, :], in_=v[:, jq * 4 + q:jq * 4 + q + 1, :]) for q in range(4)]
                    S.op("pool", ld, reads=[B1_r], writes=[bufs_r[i2][c_]], dma=4)
                bre, bim = bufs[i2]
                for j4 in range(4):
                    j = jq * 4 + j4
                    og, ogr = btt()
                    for db in range(4):
                        cs = slice(db * 512, (db + 1) * 512)
                        b, br = nb()
                        m0, m1 = (4, 7) if part == 0 else (5, 4)
                        S.op("pe", lambda e, b=b, j4=j4, cs=cs, bre=bre, bim=bim, m0=m0, m1=m1: (
                            e.matmul(b[:], fcs[:, m0, :], bre[:, j4, cs], start=True, stop=False),
                            e.matmul(b[:], fcs[:, m1, :], bim[:, j4, cs], start=False, stop=True))[-1],
                            reads=[bufs_r[i2][0], bufs_r[i2][1], fcs_r], writes=[br])
                        copy(evq(), og[:, cs], b[:], [br], [ogr])
                    S.dma("sp", G2[part, :, j, o * D:(o + 1) * D], og, reads=[ogr], writes=[G2_r])
            join(BB_r, [x for l in bufs_r for x in l])

    def fft_conv(o):
        if o == 0:
            vsrc, v_r = (lambda: c3[0:LS, 2 * D:3 * D]), c3_r
        else:
            vsrc, v_r = (lambda: z1[0:LS, :]), z1_r
        fft_s1(vsrc, v_r)
        bufs, bufs_r = s2_bufs()
        d1v = [D1[c_].rearrange("(j q) r d -> j (q r) d", q=4) for c_ in range(2)]
        for jq in range(8):
            i2 = jq % 2
            for c_ in range(2):
                v = B1[c_].rearrange("(j q) r d -> (q r) j d", q=4)

                def ld(e, v=v, dst=bufs[i2][c_], jq=jq):
                    return [e.dma_start(out=dst[:, q:q + 1, :], in_=v[:, jq * 4 + q:jq * 4 + q + 1, :]) for q in range(4)]
                S.op("pool", ld, reads=[B1_r], writes=[bufs_r[i2][c_]], dma=4)
            bre, bim = bufs[i2]
            for j4 in range(4):
                j = jq * 4 + j4
                (gc, gcr), (gs, gsr) = btt(), btt()
                S.dma("sp", gc, G2[0, :, j, o * D:(o + 1) * D], reads=[G2_r], writes=[gcr])
                S.dma("sp", gs, G2[1, :, j, o * D:(o + 1) * D], reads=[G2_r], writes=[gsr])
                (ore, orr), (oim, oir) = btt(), btt()
                for db in range(4):
                    cs = slice(db * 512, (db + 1) * 512)
                    a, ar = nb()
                    b, br = nb()
                    S.op("pe", lambda e, a=a, j4=j4, cs=cs, bre=bre, bim=bim: (
                        e.matmul(a[:], fcs[:, 4, :], bre[:, j4, cs], start=True, stop=False),
                        e.matmul(a[:], fcs[:, 7, :], bim[:, j4, cs], start=False, stop=True))[-1],
                        reads=[bufs_r[i2][0], bufs_r[i2][1], fcs_r], writes=[ar])
                    S.op("pe", lambda e, b=b, j4=j4, cs=cs, bre=bre, bim=bim: (
                        e.matmul(b[:], fcs[:, 5, :], bre[:, j4, cs], start=True, stop=False),
                        e.matmul(b[:], fcs[:, 4, :], bim[:, j4, cs], start=False, stop=True))[-1],
                        reads=[bufs_r[i2][0], bufs_r[i2][1], fcs_r], writes=[br])
                    (t1, t1r), (t2, t2r), (t3, t3r), (t4, t4r) = sfp(), sfp(), sfp(), sfp()
                    S.op("dve", lambda e, a=a, t1=t1, gc=gc, cs=cs: e.tensor_tensor(out=t1, in0=a[:], in1=gc[:, cs], op=ALU.mult),
                         reads=[ar, gcr], writes=[t1r])
                    S.op("dve", lambda e, b=b, t2=t2, gs=gs, cs=cs: e.tensor_tensor(out=t2, in0=b[:], in1=gs[:, cs], op=ALU.mult),
                         reads=[br, gsr], writes=[t2r])
                    S.op("dve", lambda e, b=b, t3=t3, gc=gc, cs=cs: e.tensor_tensor(out=t3, in0=b[:], in1=gc[:, cs], op=ALU.mult),
                         reads=[br, gcr], writes=[t3r])
                    S.op("dve", lambda e, a=a, t4=t4, gs=gs, cs=cs: e.tensor_tensor(out=t4, in0=a[:], in1=gs[:, cs], op=ALU.mult),
                         reads=[ar, gsr], writes=[t4r])
                    (y0, y0r), (y1, y1r) = sbf(), sbf()
                    S.op("pool", lambda e, t1=t1, t2=t2, y0=y0: e.tensor_tensor(out=y0, in0=t1, in1=t2, op=ALU.add),
                         reads=[t1r, t2r], writes=[y0r])
                    S.op("pool", lambda e, t3=t3, t4=t4, y1=y1: e.tensor_tensor(out=y1, in0=t3, in1=t4, op=ALU.subtract),
                         reads=[t3r, t4r], writes=[y1r])
                    cb_, cbr = nb()
                    db_, dbr = nb()
                    S.op("pe", lambda e, cb_=cb_, y0=y0, y1=y1: (e.matmul(cb_[:], fcs[:, 4, :], y0, start=True, stop=False),
                                                                e.matmul(cb_[:], fcs[:, 5, :], y1, start=False, stop=True))[-1],
                         reads=[y0r, y1r, fcs_r], writes=[cbr])
                    S.op("pe", lambda e, db_=db_, y0=y0, y1=y1: (e.matmul(db_[:], fcs[:, 5, :], y0, start=True, stop=False),
                                                                e.matmul(db_[:], fcs[:, 6, :], y1, start=False, stop=True))[-1],
                         reads=[y0r, y1r, fcs_r], writes=[dbr])
                    twiddle_evac(cb_, cbr, db_, dbr, 1, j, ore[:, cs], orr, oim[:, cs], oir)
                S.dma("sp", d1v[0][j], ore, reads=[orr], writes=[D1_r])
                S.dma("sp", d1v[1][j], oim, reads=[oir], writes=[D1_r])
        join(BB_r, [x for l in bufs_r for x in l])
        dbufs = [[BB[:, (i2 * 2 + c_) * 8192:(i2 * 2 + c_ + 1) * 8192].rearrange("p (r c) -> p r c", r=4) for c_ in range(2)]
                 for i2 in range(2)]
        dbufs_r = [[sub(BB_r, f"d1b{i2}{c_}") for c_ in range(2)] for i2 in range(2)]
        S.dma("sp", WA[1][:, 0:4096].bitcast(F32), hy_bias[o, :].partition_broadcast(128), reads=[in_r], writes=[WA_r[1]])
        biasv = WA[1][:, 0:4096].bitcast(F32)
        c3v = c3[0:LS, :].rearrange("(a r) c -> r a c", r=32)
        z1v = z1[0:LS, :].rearrange("(a r) c -> r a c", r=32)
        zzv = zz[0:LS, :].rearrange("(a r) c -> r a c", r=32)
        for rq in range(8):
            i2 = rq % 2
            for c_ in range(2):
                def ld(e, c_=c_, dst=dbufs[i2][c_], rq=rq):
                    return [e.dma_start(out=dst[:, q:q + 1, :], in_=D1[c_, :, rq * 4 + q:rq * 4 + q + 1, :]) for q in range(4)]
                S.op("pool", ld, reads=[D1_r], writes=[dbufs_r[i2][c_]], dma=4)
            dre, dim_ = dbufs[i2]
            for r4 in range(4):
                r = rq * 4 + r4
                (vv, vvr), (gg, ggr) = btt(), btt()
                if o == 0:
                    S.dma("sp", vv, c3v[r, :, 2 * D:3 * D], reads=[c3_r], writes=[vvr])
                    S.dma("sp", gg, c3v[r, :, 0:D], reads=[c3_r], writes=[ggr])
                else:
                    S.dma("sp", vv, z1v[r, :, :], reads=[z1_r], writes=[vvr])
                    S.dma("sp", gg, c3v[r, :, D:2 * D], reads=[c3_r], writes=[ggr])
                for db in range(4):
                    cs = slice(db * 512, (db + 1) * 512)
                    y, yr = nb()
                    S.op("pe", lambda e, y=y, r4=r4, cs=cs, dre=dre, dim_=dim_: (
                        e.matmul(y[:], fcs[:, 2, :], dre[:, r4, cs], start=True, stop=False),
                        e.matmul(y[:], fcs[:, 3, :], dim_[:, r4, cs], start=False, stop=True))[-1],
                        reads=[dbufs_r[i2][0], dbufs_r[i2][1], fcs_r], writes=[yr])
                    t_, tr_ = sfp()
                    S.op("dve", lambda e, t_=t_, vv=vv, cs=cs: e.tensor_tensor(out=t_, in0=vv[:, cs], in1=biasv[:, cs], op=ALU.mult),
                         reads=[vvr, WA_r[1]], writes=[tr_])
                    S.op("dve", lambda e, y=y, t_=t_: e.tensor_tensor(out=t_, in0=y[:], in1=t_, op=ALU.add),
                         reads=[yr, tr_], writes=[tr_])
                    S.op("dve", lambda e, t_=t_, gg=gg, vv=vv, cs=cs: e.tensor_tensor(out=vv[:, cs], in0=t_, in1=gg[:, cs], op=ALU.mult),
                         reads=[tr_, ggr], writes=[vvr])
                dstv, dr = (z1v, z1_r) if o == 0 else (zzv, zz_r)
                S.dma("sp", dstv[r, :, :], vv, reads=[vvr], writes=[dr])
        join(BB_r, [x for l in dbufs_r for x in l])

    def hy_seq_fft():
        fft_consts()
        bt_open()
        for o in range(2):
            fft_filters(o)
        for o in range(2):
            fft_conv(o)
        bt_close()

    phase = [0]

    def P(fn, *a, **k):
        if phase[0] >= DBG_STOP:
            raise _Stop()
        fn(*a, **k)
        phase[0] += 1

    def post_v(ti, j0, b, br):
        si = rot("sb16", 4)
        if ti < 32:
            copy(evq(), SB16[si][:], b[:], [br], [SB16_r[si]])
        else:
            s2 = rot("stg", 6)
            copy("dve", STG[s2][:], b[:], [br], [STG_r[s2]])
            for hh in range(2):
                S.dma("sp", nv[(ti - 32) * 128:(ti - 31) * 128, hh * 256:(hh + 1) * 256], STG[s2][:, hh * 256:(hh + 1) * 256],
                      reads=[STG_r[s2]], writes=[Res("nv", True)])
            copy("act", SB16[si][:], STG[s2][:], [STG_r[s2]], [SB16_r[si]])
        S.dma("sp", vtm[ti * 128:(ti + 1) * 128, :], SB16[si][:], reads=[SB16_r[si]], writes=[vtm_r])

    def post_k(ti, j0, b, br):
        s2 = rot("stg", 6)
        copy(evq(), STG[s2][:], b[:], [br], [STG_r[s2]])
        for hh in range(2):
            S.dma("sp", nk[(ti - 32) * 128:(ti - 31) * 128, hh * 256:(hh + 1) * 256], STG[s2][:, hh * 256:(hh + 1) * 256],
                  reads=[STG_r[s2]], writes=[Res("nk", True)])

    def final_norm(xi_):
        for ti in range(36):
            xi = rot("xt", 2)
            S.dma("sp", XT[xi][:], xres[xi_][ti * 128:(ti + 1) * 128, :], reads=[xres_r[xi_]], writes=[XT_r[xi]])
            S.op("act", lambda e, xi=xi: e.activation(out=JK[:], in_=XT[xi][:], func=AF.Square, accum_out=sm[:, 0:1]),
                 reads=[XT_r[xi]], writes=[JK_r, sm_r])
            S.op("dve", lambda e: e.tensor_scalar(out=sm[:, 1:2], in0=sm[:, 0:1], scalar1=1.0 / D, scalar2=EPS,
                                                  op0=ALU.mult, op1=ALU.add), reads=[sm_r], writes=[sm_r])
            S.op("act", lambda e: e.activation(out=sm[:, 2:3], in_=sm[:, 1:2], func=AF.Sqrt), reads=[sm_r], writes=[sm_r])
            S.op("dve", lambda e: e.reciprocal(out=sm[:, 3:4], in_=sm[:, 2:3]), reads=[sm_r], writes=[sm_r])
            if ti == 0:
                S.dma("sp", ACC[:, 0:D], norm_final.partition_broadcast(128), reads=[in_r], writes=[ACC_r])
            S.op("dve", lambda e, xi=xi: e.scalar_tensor_tensor(out=XT[xi][:], in0=XT[xi][:], scalar=sm[:, 3:4],
                                                               in1=ACC[:, 0:D], op0=ALU.mult, op1=ALU.mult),
                 reads=[XT_r[xi], sm_r, ACC_r], writes=[XT_r[xi]])
            dst = ys[ti * 128:(ti + 1) * 128, :] if ti < 32 else yp[(ti - 32) * 128:(ti - 31) * 128, :]
            for hh in range(4):
                S.dma("sp", dst[:, hh * 512:(hh + 1) * 512], XT[xi][:, hh * 512:(hh + 1) * 512], reads=[XT_r[xi]],
                      writes=[Res("y", True)])

    if os.environ.get("MK_VAR", "") == "nvtest":
        S.op("dve", lambda e: e.memset(STG[0][:], 1.0), writes=[STG_r[0]])
        S.dma("sp", nv[0:128, :], STG[0][:], reads=[STG_r[0]], writes=[Res("nv", True)])
    try:
        P(mod_phase, 0)
        P(norm_T, x_src(None), 0)
        P(gemm_fm, w_qkv, 2560, hT, hT_r, 16, post_store_fm(qT, qT_r))
        P(gemm_tm, hTt, hTt_r, 16, w_qkv[:, 2560:3072], 512, post_v, tiled=True)
        P(gemm_tm, hTt, hTt_r, 16, w_qkv[:, 2048:2560], 512, post_k, tiles=range(32, 36), tiled=True)
        P(attention)
        P(gemm_tm, oT, oT_r, 16, w_o, D, post_residual(x_src(None), 0, xres[0], xres_r[0]), tiled=True)
        P(norm_T, x_src(0), 1)
        P(gemm_fm, ffn_w_up[0], 2 * DFF, hT, hT_r, 16, post_store_fm(abT, abT_r))
        P(ffn_act, 0)
        P(gemm_tm, gT, gT_r, 44, ffn_w_down[0], D, post_residual(x_src(0), 1, xres[1], xres_r[1]))
        P(mod_phase, 1)
        P(norm_T, x_src(1), 0)
        P(gemm_tm, hTt, hTt_r, 16, hy_w_in, 3 * D, post_ptm, tiled=True)
        P(hy_conv3)
        P(hy_filters, "S", LS, hsd, hsd_r)
        if os.environ.get("MK_DFT", "") == "big":
            P(hy_seq, "S", LS, 0, hsd, hsd_r, ghat, ghat_r, yhat, yhat_r)
        else:
            P(hy_seq_fft)
        P(hy_filters, "P", LP, hsdP, hsdP_r)
        P(hy_seq, "P", LP, 4096, hsdP, hsdP_r, ghatP, ghatP_r, yhatP, yhatP_r)
        P(hy_seq, "P", LP, 4352, hsdP, hsdP_r, ghatP, ghatP_r, yhatP, yhatP_r)
        P(norm_T, None, 0, plain_src=(zz, zz_r))
        P(gemm_tm, hTt, hTt_r, 16, hy_w_out, D, post_residual(x_src(1), 0, xres[2], xres_r[2]), tiled=True)
        P(norm_T, x_src(2), 1)
        P(gemm_fm, ffn_w_up[1], 2 * DFF, hT, hT_r, 16, post_store_fm(abT, abT_r))
        P(ffn_act, 1)
        P(gemm_tm, gT, gT_r, 44, ffn_w_down[1], D, post_residual(x_src(2), 1, xres[3], xres_r[3]))
        P(final_norm, 3)
    except _Stop:
        pass
    S.finish()
    S.emit()
    S.st.close()
    return nc


_NC = None


def kernel(x_prompt, x_sample, cache_k, cache_v, c, c_ctx, w_mod, b_mod, norm_mix, norm_ffn, norm_final,
           w_qkv, w_o, attn_sink, hy_w_in, hy_conv, hy_w_f1, hy_b_f1, hy_w_f2, hy_b_f2, hy_w_f3, hy_freq,
           hy_bias, hy_w_out, ffn_w_up, ffn_conv, ffn_w_down):
    global _NC
    f = lambda a: np.ascontiguousarray(np.asarray(a, dtype=np.float32))
    x_prompt, x_sample, cache_k, cache_v, c, c_ctx = map(f, (x_prompt, x_sample, cache_k, cache_v, c, c_ctx))
    cst = _consts()
    if _NC is None:
        _NC = build()
    nc = _NC
    smallv = np.concatenate([f(b_mod).reshape(2, 96, 128), f(norm_mix).reshape(2, 16, 128),
                             f(norm_ffn).reshape(2, 16, 128)], axis=1)
    fc = f(ffn_conv).reshape(2, 3, 44, 128).reshape(2, 132, 128)
    shared = {
        "w_mod": f(w_mod), "smallv": np.ascontiguousarray(smallv), "norm_final": f(norm_final),
        "w_qkv": f(w_qkv)[0], "w_o": f(w_o)[0], "sink": f(attn_sink)[0],
        "hy_w_in": f(hy_w_in)[0], "hy_conv": f(hy_conv)[0], "hy_w_f1": f(hy_w_f1)[0], "hy_w_f2": f(hy_w_f2)[0],
        "hy_w_f3": f(hy_w_f3)[0],
        "hy_small": np.ascontiguousarray(np.stack([f(hy_b_f1)[0], f(hy_b_f2)[0], f(hy_freq)[0]], axis=1)),
        "hy_bias": f(hy_bias)[0], "hy_w_out": f(hy_w_out)[0],
        "ffn_w_up": f(ffn_w_up), "ffn_convT": np.ascontiguousarray(fc), "ffn_w_down": f(ffn_w_down),
    }
    for k, v in cst.items():
        shared["c_" + k] = v
    in_maps = []
    for b in range(8):
        m = dict(shared)
        m["xs"] = x_sample[b]
        m["xp"] = x_prompt[2 * b:2 * b + 2].reshape(512, D)
        m["ck"] = cache_k[b, 0].reshape(256, 512)
        m["cv"] = cache_v[b, 0].reshape(256, 512)
        m["cvec"] = np.ascontiguousarray(np.concatenate([c[b].reshape(16, 128), c_ctx.reshape(16, 128)], axis=0))
        in_maps.append(m)
    res = run_bass_kernel_spmd(nc, in_maps, core_ids=list(range(8)))
    R = res.results
    y_prompt = np.concatenate([R[b]["yp"].reshape(2, 256, D) for b in range(8)], axis=0)
    y_sample = np.stack([R[b]["ys"] for b in range(8)], axis=0)
    nk = np.concatenate([R[b]["nk"].reshape(2, 1, 256, 4, 128) for b in range(8)], axis=0)
    nv = np.concatenate([R[b]["nv"].reshape(2, 1, 256, 4, 128) for b in range(8)], axis=0)
    return (y_prompt.astype(np.float32), y_sample.astype(np.float32), nk.astype(np.float32), nv.astype(np.float32))
```

```python
from contextlib import ExitStack
import math
import numpy as np
import ml_dtypes
import concourse.bass as bass
import concourse.mybir as mybir
from concourse.bass_utils import run_bass_kernel_spmd

F32 = mybir.dt.float32
BF16 = mybir.dt.bfloat16
AF = mybir.ActivationFunctionType
ALU = mybir.AluOpType

D = 2048
NT = 4608
LS = 4096
LP = 256
DFF = 5632
SEQS = [(0, 4096), (4096, 4352), (4352, 4608)]
EPS = 1e-6


class Res:
    __slots__ = ("name", "w", "r", "multi")

    def __init__(self, name, multi=False):
        self.name = name
        self.w = {}
        self.r = {}
        self.multi = multi


def _merge(d, tok):
    k, v = tok
    if v > d.get(k, 0):
        d[k] = v


class Sched:
    CE = ("pe", "act", "dve", "pool")
    ALLQ = ("pe", "act", "dve", "pool", "sp")

    def __init__(self, nc, n_dma_sems=32):
        self.nc = nc
        self.prog = {e: [] for e in self.ALLQ}
        self.cnt = {e: 0 for e in self.CE}
        self.sem = {}
        self.dma_i = 0
        self.nd = n_dma_sems
        self.dma_val = [0] * n_dma_sems
        self.nsw = 32
        self.sw_i = 0
        self.sw_val = [0] * self.nsw
        self.waited = {e: {} for e in self.ALLQ}
        self.st = ExitStack()

    def sb(self, name, shape, dtype=F32):
        return self.st.enter_context(self.nc.sbuf_tensor(name, list(shape), dtype))

    def ps(self, name, shape, dtype=F32):
        return self.st.enter_context(self.nc.psum_tensor(name, list(shape), dtype))

    def op(self, eng, fn, reads=(), writes=(), dma=0):
        waits = {}
        for r in reads:
            for k, v in r.w.items():
                if v > waits.get(k, 0):
                    waits[k] = v
        for w in writes:
            for k, v in w.r.items():
                if v > waits.get(k, 0):
                    waits[k] = v
            if not (w.multi and not w.r):
                for k, v in w.w.items():
                    if v > waits.get(k, 0):
                        waits[k] = v
        if (not dma) and eng == "pe":
            waits.pop(("E", "pe"), None)
        sems = None
        if dma and eng == "pool":
            tok = {}
            sems = []
            for _ in range(dma):
                idx = self.sw_i % self.nsw
                self.sw_i += 1
                prev = self.sw_val[idx]
                if prev > waits.get(("S", idx), 0):
                    waits[("S", idx)] = prev
                self.sw_val[idx] = prev + 16
                tok[("S", idx)] = prev + 16
                sems.append(("S", idx))
        elif dma:
            idx = self.dma_i % self.nd
            self.dma_i += 1
            prev = self.dma_val[idx]
            if prev > waits.get(("D", idx), 0):
                waits[("D", idx)] = prev
            self.dma_val[idx] = prev + 16 * dma
            tok = {("D", idx): prev + 16 * dma}
            sems = [("D", idx)] * dma
        else:
            self.cnt[eng] += 1
            tok = {("E", eng): self.cnt[eng]}
        wl = []
        wd = self.waited[eng]
        for key, val in waits.items():
            if val <= 0 or wd.get(key, 0) >= val:
                continue
            wd[key] = val
            wl.append((key, val))
        self.prog[eng].append((wl, fn, tok, sems))
        for r in reads:
            for k, v in tok.items():
                if v > r.r.get(k, 0):
                    r.r[k] = v
        for w in writes:
            if w.multi and not w.r:
                for k, v in tok.items():
                    if v > w.w.get(k, 0):
                        w.w[k] = v
            else:
                w.w = dict(tok)
                w.r = {}
        return tok

    def dma(self, q, out, in_, reads=(), writes=()):
        return self.op(q, lambda e: [e.dma_start(out=out, in_=in_)], reads=reads, writes=writes, dma=1)

    def finish(self):
        wl = []
        for i in range(self.nd):
            if self.dma_val[i] > 0:
                wl.append((("D", i), self.dma_val[i]))
        for i in range(self.nsw):
            if self.sw_val[i] > 0:
                wl.append((("S", i), self.sw_val[i]))
        for e in self.CE:
            if self.cnt[e] > 0:
                wl.append((("E", e), self.cnt[e]))
        self.prog["sp"].append((wl, None, None, None))

    def emit(self):
        nc = self.nc
        st = self.st
        for e in self.CE:
            self.sem[("E", e)] = st.enter_context(nc.semaphore(f"s_{e}"))
        for i in range(self.nd):
            self.sem[("D", i)] = st.enter_context(nc.semaphore(f"d_{i}"))
        for i in range(self.nsw):
            self.sem[("S", i)] = st.enter_context(nc.semaphore(f"sw_{i}"))
        block = st.enter_context(nc.Block())
        sched = self

        def mk(engname):
            def body(e):
                for wl, fn, tok, sems in sched.prog[engname]:
                    for key, val in wl:
                        e.wait_ge(sched.sem[key], val)
                    if fn is None:
                        continue
                    ins = fn(e)
                    if sems is not None:
                        assert len(ins) == len(sems), (len(ins), len(sems))
                        for i, sk in zip(ins, sems):
                            i.then_inc(sched.sem[sk], 16)
                    else:
                        if isinstance(ins, (list, tuple)):
                            ins = ins[-1]
                        ins.then_inc(sched.sem[("E", engname)], 1)
            return body

        block.sync(mk("sp"))
        block.tensor(mk("pe"))
        block.scalar(mk("act"))
        block.vector(mk("dve"))
        block.gpsimd(mk("pool"))


_CONST = None


def _consts():
    global _CONST
    if _CONST is not None:
        return _CONST
    bf = ml_dtypes.bfloat16
    c = {}
    c["ident_b"] = np.eye(128, dtype=np.float32).astype(bf)
    c["ident_f"] = np.eye(128, dtype=np.float32)
    t = np.arange(LS)
    row = (t // 64).astype(np.float64)
    col = (t % 64).astype(np.float64)
    inv = 10000.0 ** (-np.arange(32, dtype=np.float64) / 32)
    C = np.ones((128, NT), np.float64)
    Sg = np.zeros((128, NT), np.float64)
    perm = np.zeros((128, 128), np.float32)
    for d in range(128):
        half, e = d // 64, d % 64
        j, first = e % 32, e < 32
        ang = (row if half == 0 else col) * inv[j]
        C[d, :LS] = np.cos(ang)
        Sg[d, :LS] = -np.sin(ang) if first else np.sin(ang)
        perm[d + 32 if first else d - 32, d] = 1.0
    c["ropeC"] = C.astype(np.float32).astype(bf)
    c["ropeS"] = Sg.astype(np.float32).astype(bf)
    c["perm"] = perm.astype(bf)
    kl = np.arange(128)[:, None]
    ql = np.arange(128)[None, :]
    m = np.stack([np.tile((kl >= ql), (1, 4)), np.tile((kl <= ql), (1, 4))], axis=1)
    c["bmask"] = m.astype(np.float32).astype(bf)
    for nm, L in (("S", LS), ("P", LP)):
        N = 2 * L
        tt = np.arange(L, dtype=np.float64)[:, None]
        kk = np.arange(L, dtype=np.float64)[None, :]
        th = 2.0 * np.pi * ((tt * (kk + 0.5)) % N) / N
        fwd = np.concatenate([np.cos(th), np.sin(th)], axis=1)
        CCf = L // 128
        inv = fwd.T * (2.0 / N)
        c["fwd" + nm] = np.ascontiguousarray(
            fwd.reshape(CCf, 128, 2 * CCf, 128).transpose(2, 1, 0, 3)).astype(np.float32).astype(bf)
        c["inv" + nm] = np.ascontiguousarray(
            inv.reshape(2 * CCf, 128, CCf, 128).transpose(2, 1, 0, 3)).astype(np.float32).astype(bf)
        tl = np.linspace(0.0, 1.0, L, dtype=np.float32)[:, None]
        w = (2.0 * np.pi * np.arange(L, dtype=np.float32)[:, None] / L).astype(np.float32)
        f = np.linspace(1e-4, 15, 16, dtype=np.float32)[None, :]
        feat = np.concatenate([tl, np.cos(f * w), -np.sin(f * w)], axis=-1).astype(np.float32)
        c["featT" + nm] = np.ascontiguousarray(feat.T)
        deltas = np.linspace(math.log(1e-2) / 1.5, math.log(1e-2) / 0.3, D, dtype=np.float32)
        dec = np.exp(-tl * np.abs(deltas)[None, :]).astype(np.float32)
        decb = dec.copy()
        decb[0, :] = 0.0
        c["decf" + nm] = dec
        c["decb" + nm] = decb
    N = 2 * LS
    a_ = np.arange(128, dtype=np.float64)[:, None]
    k1 = np.arange(128, dtype=np.float64)[None, :]
    phi = 2.0 * np.pi * a_ * (k1 + 0.5) / 256.0
    c["f1c"] = np.cos(phi).astype(np.float32).astype(bf)
    c["f1s"] = np.sin(phi).astype(np.float32).astype(bf)
    c["i1c"] = (np.cos(phi).T * (2.0 / N)).astype(np.float32).astype(bf)
    c["i1s"] = (-np.sin(phi).T * (2.0 / N)).astype(np.float32).astype(bf)
    r_ = np.arange(32, dtype=np.float64)[None, :]
    psi = 2.0 * np.pi * r_ * (np.arange(128, dtype=np.float64)[:, None] + 0.5) / N
    c["tw1"] = np.concatenate([np.cos(psi), np.sin(psi)], axis=1).astype(np.float32)
    chi = 2.0 * np.pi * np.outer(np.arange(32), np.arange(32)) / 32.0
    cbm = np.kron(np.eye(4), np.cos(chi))
    sbm = np.kron(np.eye(4), np.sin(chi))
    c["cbm"] = cbm.astype(np.float32).astype(bf)
    c["sbm"] = sbm.astype(np.float32).astype(bf)
    c["ncbm"] = (-cbm).astype(np.float32).astype(bf)
    c["nsbm"] = (-sbm).astype(np.float32).astype(bf)
    q_ = (np.arange(128) // 32).astype(np.float64)[:, None]
    rr2 = (np.arange(128) % 32).astype(np.float64)[:, None]
    j_ = np.arange(32, dtype=np.float64)[None, :]
    psi2 = 2.0 * np.pi * rr2 * (4.0 * j_ + q_ + 0.5) / N
    c["tw2"] = np.concatenate([np.cos(psi2), np.sin(psi2)], axis=1).astype(np.float32)
    _CONST = c
    return c


import os
DBG_STOP = int(os.environ.get("MK_STOP", "999"))
DBG_OUT = [x for x in os.environ.get("MK_OUT", "").split(",") if x]


class _Stop(Exception):
    pass


def build():
    nc = bass.Bass("TRN2", target_bir_lowering=False)
    S = Sched(nc)
    cst = _consts()
    IN = {}

    def din(name, shape, dt=F32):
        IN[name] = nc.dram_tensor(name, list(shape), dt, kind="ExternalInput").ap()
        return IN[name]

    def dscr(name, shape, dt):
        return nc.dram_tensor(name, list(shape), dt, kind=("ExternalOutput" if name in DBG_OUT else "Internal")).ap()

    xs = din("xs", [LS, D]); xp = din("xp", [512, D])
    ck = din("ck", [256, 512]); cv = din("cv", [256, 512])
    cvec = din("cvec", [32, 128])
    w_mod = din("w_mod", [2, D, 6 * D]); smallv = din("smallv", [2, 128, 128])
    norm_final = din("norm_final", [D])
    w_qkv = din("w_qkv", [D, 3072]); w_o = din("w_o", [D, D]); sink = din("sink", [16])
    hy_w_in = din("hy_w_in", [D, 3 * D]); hy_conv = din("hy_conv", [3, 3 * D])
    hy_w_f1 = din("hy_w_f1", [33, 64]); hy_w_f2 = din("hy_w_f2", [64, 64]); hy_w_f3 = din("hy_w_f3", [64, 4 * D])
    hy_small = din("hy_small", [64, 3])
    hy_bias = din("hy_bias", [2, D]); hy_w_out = din("hy_w_out", [D, D])
    ffn_w_up = din("ffn_w_up", [2, D, 2 * DFF]); ffn_convT = din("ffn_convT", [2, 132, 128])
    ffn_w_down = din("ffn_w_down", [2, DFF, D])
    for k, v in cst.items():
        din("c_" + k, v.shape, BF16 if v.dtype != np.float32 else F32)

    yp = nc.dram_tensor("yp", [512, D], F32, kind="ExternalOutput").ap()
    ys = nc.dram_tensor("ys", [LS, D], F32, kind="ExternalOutput").ap()
    nk = nc.dram_tensor("nk", [512, 512], F32, kind="ExternalOutput").ap()
    nv = nc.dram_tensor("nv", [512, 512], F32, kind="ExternalOutput").ap()

    hT = dscr("hT", [D, NT], BF16); hT_r = Res("hT", True)
    qT = dscr("qT", [2560, NT], BF16); qT_r = Res("qT", True)
    vtm = dscr("vtm", [NT, 512], BF16); vtm_r = Res("vtm", True)
    oT = dscr("oT", [36, 128, 16, 128], BF16); oT_r = Res("oT", True)
    hTt = dscr("hTt", [36, 128, 16, 128], BF16); hTt_r = Res("hTt", True)
    xres = [dscr(f"xres{i}", [NT, D], F32) for i in range(4)]
    xres_r = [Res(f"xres{i}", True) for i in range(4)]
    abT = dscr("abT", [2 * DFF, NT], BF16); abT_r = Res("abT", True)
    gT = dscr("gT", [DFF, NT], BF16); gT_r = Res("gT", True)
    gate_scr = dscr("gate_scr", [4, 128, D], F32); gate_r = Res("gate", True)
    ptm = dscr("ptm", [NT, 3 * D], BF16); ptm_r = Res("ptm", True)
    c3 = dscr("c3", [NT, 3 * D], BF16); c3_r = Res("c3", True)
    hsd = dscr("hsd", [2, LS, 2 * D], BF16); hsd_r = Res("hsd", True)
    ghat = dscr("ghat", [2, LS, 2 * D], BF16); ghat_r = Res("ghat", True)
    yhat = dscr("yhat", [2, 2 * LS, D], BF16); yhat_r = Res("yhat", True)
    z1 = dscr("z1", [NT, D], BF16); z1_r = Res("z1", True)
    zz = dscr("zz", [NT, D], BF16); zz_r = Res("zz", True)
    in_r = Res("inputs")

    BB = S.sb("BB", [128, 35200], BF16); BB_r = Res("BB")
    WA = [S.sb(f"WA{i}", [128, 12288], BF16) for i in range(2)]; WA_r = [Res(f"WA{i}") for i in range(2)]
    XT = [S.sb(f"XT{i}", [128, D], F32) for i in range(2)]; XT_r = [Res(f"XT{i}") for i in range(2)]
    XN = S.sb("XN", [128, D], BF16); XN_r = Res("XN")
    JK = S.sb("JK", [128, D], BF16); JK_r = Res("JK")
    HTS = S.sb("HTS", [128, 16, 256], BF16); HTS_r = Res("HTS")
    STG = [S.sb(f"STG{i}", [128, 512], F32) for i in range(6)]; STG_r = [Res(f"STG{i}") for i in range(6)]
    STX = [S.sb(f"STX{i}", [128, 512], F32) for i in range(4)]; STX_r = [Res(f"STX{i}") for i in range(4)]
    SB16 = [S.sb(f"SB16{i}", [128, 512], BF16) for i in range(4)]; SB16_r = [Res(f"SB16{i}") for i in range(4)]
    ACC = S.sb("ACC", [128, NT], F32); ACC_r = Res("ACC")
    idb = S.sb("idb", [128, 128], BF16); idf = S.sb("idf", [128, 128], F32); perm = S.sb("perm", [128, 128], BF16)
    cst_r = Res("consts")
    colsT = S.sb("colsT", [128, 128], F32); colsT_r = Res("colsT")
    cT = S.sb("cT", [128, 32], BF16); cT_r = Res("cT")
    cbc = HTS[:].rearrange("p c t -> p (c t)").rearrange("p (a b) -> p a b", a=32); cbc_r = HTS_r
    modc = S.sb("modc", [128, 2, 96], F32); modc_r = Res("modc")
    gsc = S.sb("gsc", [128, 2, 2, 16], F32); gsc_r = Res("gsc")
    sm = S.sb("sm", [128, 8], F32); sm_r = Res("sm")
    esink = S.sb("esink", [128, 16], F32); esink_r = Res("esink")
    fcv = S.sb("fcv", [128, 3, 44], F32); fcv_r = Res("fcv")
    stg_s = S.sb("stg_s", [128, 128], F32); stg_r = Res("stg_s")
    banks = [S.ps(f"bank{i}", [128, 512], F32) for i in range(8)]
    bank_r = [Res(f"bank{i}") for i in range(8)]
    bi = [0]

    def sub(parent, name):
        r = Res(name)
        r.w = dict(parent.w)
        r.r = dict(parent.r)
        return r

    def join(parent, subs):
        for rr_ in subs:
            for k, v in list(rr_.w.items()) + list(rr_.r.items()):
                if v > parent.r.get(k, 0):
                    parent.r[k] = v

    nbn = [8]

    def nb():
        i = bi[0] % nbn[0]
        bi[0] += 1
        return banks[i], bank_r[i]

    rr = {"stg": 0, "sb16": 0, "wa": 0, "xt": 0, "ev": 0}

    def rot(name, n):
        i = rr[name] % n
        rr[name] += 1
        return i

    def evq():
        return "act" if rot("ev", 2) == 0 else "dve"

    def copy(q, out, in_, reads, writes):
        if q == "act":
            return S.op("act", lambda e: e.activation(out=out, in_=in_, func=AF.Copy), reads=reads, writes=writes)
        return S.op(q, lambda e: e.tensor_copy(out=out, in_=in_), reads=reads, writes=writes)

    S.dma("sp", idb[:], IN["c_ident_b"], writes=[cst_r])
    S.dma("sp", idf[:], IN["c_ident_f"], writes=[cst_r])
    S.dma("sp", perm[:], IN["c_perm"], writes=[cst_r])

    def x_src(layer_in):
        def f(ti):
            if layer_in is None:
                return (xs[ti * 128:(ti + 1) * 128, :] if ti < 32 else xp[(ti - 32) * 128:(ti - 31) * 128, :]), in_r
            return xres[layer_in][ti * 128:(ti + 1) * 128, :], xres_r[layer_in]
        return f

    def mod_phase(i):
        S.dma("sp", stg_s[:], smallv[i], reads=[in_r], writes=[stg_r])
        b, br = nb()
        S.op("pe", lambda e: e.transpose(b[:, 0:128], stg_s[:], idf[:]), reads=[stg_r, cst_r], writes=[br])
        copy("act", colsT[:], b[:, 0:128], [br], [colsT_r])
        S.dma("sp", stg_s[0:32, :], cvec, reads=[in_r], writes=[stg_r])
        S.op("act", lambda e: e.activation(out=stg_s[0:32, :], in_=stg_s[0:32, :], func=AF.Silu),
             reads=[stg_r], writes=[stg_r])
        b2, b2r = nb()
        S.op("pe", lambda e: e.transpose(b2[:, 0:32], stg_s[0:32, :], idf[0:32, 0:32]), reads=[stg_r, cst_r], writes=[b2r])
        copy("act", cT[:], b2[:, 0:32], [b2r], [cT_r])

        def mkbc(e):
            last = None
            for c in range(32):
                last = e.tensor_copy(out=cbc[:, c, :], in_=cT[:, c:c + 1].to_broadcast([128, 128]))
            return last
        S.op("dve", mkbc, reads=[cT_r], writes=[cbc_r])
        mb, mbr = banks[7], bank_r[7]
        nbn[0] = 7
        for blk in range(24):
            wi = rot("wa", 2)
            wv = WA[wi][:, 0:8192].rearrange("p (c w) -> p c w", c=16)

            def ldw(e, blk=blk, wv=wv):
                return [e.dma_start(out=wv[:, c4 * 4:(c4 + 1) * 4, :],
                                    in_=w_mod[i, c4 * 512:(c4 + 1) * 512, blk * 512:(blk + 1) * 512]
                                    .rearrange("(c p) n -> p c n", p=128)) for c4 in range(4)]
            S.op("pool", ldw, reads=[in_r], writes=[WA_r[wi]], dma=4)

            def mm(e, blk=blk, wv=wv):
                last = None
                for n in range(4):
                    ch = blk * 4 + n
                    for k in range(16):
                        last = e.matmul(mb[:, ch * 2:ch * 2 + 2], wv[:, k, n * 128:(n + 1) * 128],
                                        cT[:, k:32:16], start=(k == 0), stop=(k == 15))
                return last
            S.op("pe", mm, reads=[WA_r[wi], cT_r], writes=[mbr])
            which = {2: 0, 5: 1}.get(blk // 4)
            if which is not None:
                cb = (blk % 4) * 512
                for r in range(2):
                    gb, gbr = nb()

                    def mg(e, wv=wv, r=r, gb=gb):
                        last = None
                        for k in range(16):
                            last = e.matmul(gb[:], cbc[:, r * 16 + k, :], wv[:, k, :], start=(k == 0), stop=(k == 15))
                        return last
                    S.op("pe", mg, reads=[WA_r[wi], cbc_r], writes=[gbr])
                    si = rot("stg", 6)
                    s2 = rot("stg", 6)
                    S.dma("sp", STG[s2][:], smallv[i, (blk * 4):(blk * 4 + 4), :].rearrange("a b -> (a b)")
                          .partition_broadcast(128), reads=[in_r], writes=[STG_r[s2]])
                    S.op("dve", lambda e, gb=gb, si=si, s2=s2: e.tensor_tensor(
                        out=STG[si][:], in0=gb[:], in1=STG[s2][:], op=ALU.add),
                        reads=[gbr, STG_r[s2]], writes=[STG_r[si]])
                    S.dma("sp", gate_scr[r * 2 + which, :, cb:cb + 512], STG[si][:],
                          reads=[STG_r[si]], writes=[gate_r])
        nbn[0] = 8
        for r in range(2):
            S.op("dve", lambda e, r=r: e.tensor_tensor(out=modc[:, r, :], in0=mb[:, r:192:2], in1=colsT[:, 0:96],
                                                       op=ALU.add), reads=[mbr, colsT_r], writes=[modc_r])
        for r in range(2):
            for wh in range(2):
                S.op("dve", lambda e, r=r, wh=wh: e.scalar_tensor_tensor(
                    out=gsc[:, r, wh, :], in0=modc[:, r, wh * 48 + 16:wh * 48 + 32], scalar=1.0,
                    in1=colsT[:, 96 + wh * 16:112 + wh * 16], op0=ALU.add, op1=ALU.mult),
                    reads=[modc_r, colsT_r], writes=[gsc_r])

    def norm_T(src, wh, plain_src=None):
        for ti in range(36):
            r = 0 if ti < 32 else 1
            if plain_src is None:
                xi = rot("xt", 2)
                ap, res = src(ti)
                S.dma("sp", XT[xi][:], ap, reads=[res], writes=[XT_r[xi]])
                S.op("act", lambda e, xi=xi: e.activation(out=JK[:], in_=XT[xi][:], func=AF.Square,
                                                          accum_out=sm[:, 0:1]),
                     reads=[XT_r[xi]], writes=[JK_r, sm_r])
                S.op("dve", lambda e: e.tensor_scalar(out=sm[:, 1:2], in0=sm[:, 0:1], scalar1=1.0 / D, scalar2=EPS,
                                                      op0=ALU.mult, op1=ALU.add), reads=[sm_r], writes=[sm_r])
                S.op("act", lambda e: e.activation(out=sm[:, 2:3], in_=sm[:, 1:2], func=AF.Sqrt), reads=[sm_r], writes=[sm_r])
                S.op("dve", lambda e: e.reciprocal(out=sm[:, 3:4], in_=sm[:, 2:3]), reads=[sm_r], writes=[sm_r])
                S.op("dve", lambda e, xi=xi: e.tensor_scalar(out=XN[:], in0=XT[xi][:], scalar1=sm[:, 3:4], scalar2=None,
                                                            op0=ALU.mult), reads=[XT_r[xi], sm_r], writes=[XN_r])
            else:
                pa, pr = plain_src
                S.dma("sp", XN[:], pa[ti * 128:(ti + 1) * 128, :], reads=[pr], writes=[XN_r])
            half = ti % 2
            for g in range(2):
                b, br = nb()
                bv = b[:].bitcast(BF16)

                def tp(e, g=g, bv=bv):
                    last = None
                    for j in range(8):
                        c = g * 8 + j
                        last = e.transpose(bv[:, j * 128:(j + 1) * 128], XN[:, c * 128:(c + 1) * 128], idb[:])
                    return last
                S.op("pe", tp, reads=[XN_r, cst_r], writes=[br])

                def ev(e, g=g, bv=bv, r=r, half=half):
                    last = None
                    for j in range(8):
                        c = g * 8 + j
                        o = HTS[:, c, half * 128:(half + 1) * 128]
                        if plain_src is None:
                            last = e.activation(out=o, in_=bv[:, j * 128:(j + 1) * 128], func=AF.Identity,
                                                scale=gsc[:, r, wh, c:c + 1],
                                                bias=modc[:, r, wh * 48 + c:wh * 48 + c + 1])
                        else:
                            last = e.activation(out=o, in_=bv[:, j * 128:(j + 1) * 128], func=AF.Copy)
                    return last
                S.op("act", ev, reads=[br, gsc_r, modc_r], writes=[HTS_r])
            S.dma("sp", hTt[ti], HTS[:, :, half * 128:(half + 1) * 128], reads=[HTS_r], writes=[hTt_r])
            if half == 1:
                t0 = (ti - 1) * 128

                def st(e, t0=t0):
                    return [e.dma_start(out=hT[q * 512:(q + 1) * 512, t0:t0 + 256].rearrange("(c p) t -> p c t", p=128),
                                        in_=HTS[:, q * 4:(q + 1) * 4, :]) for q in range(4)]
                S.op("sp", st, reads=[HTS_r], writes=[hT_r], dma=4)

    BLKS = [(0, 2048), (2048, 4096), (4096, 4608)]

    def gemm_fm(W, ncols, src, src_r, KC, post):
        for (c0, c1) in BLKS:
            wd = c1 - c0
            bv = BB[:, 0:KC * wd].rearrange("p (c w) -> p c w", c=KC)

            def ldb(e, bv=bv, c0=c0, c1=c1):
                return [e.dma_start(out=bv[:, q * 4:(q + 1) * 4, :],
                                    in_=src[q * 512:(q + 1) * 512, c0:c1].rearrange("(c p) t -> p c t", p=128))
                        for q in range(KC // 4)]
            S.op("sp", ldb, reads=[src_r], writes=[BB_r], dma=KC // 4)
            for cg in range(ncols // 256):
                wi = rot("wa", 2)
                wv = WA[wi][:, 0:KC * 256].rearrange("p (c w) -> p c w", c=KC)

                def ldw(e, wv=wv, cg=cg):
                    return [e.dma_start(out=wv[:, q * 4:(q + 1) * 4, :],
                                        in_=W[q * 512:(q + 1) * 512, cg * 256:(cg + 1) * 256]
                                        .rearrange("(c p) n -> p c n", p=128)) for q in range(KC // 4)]
                S.op("pool", ldw, reads=[in_r], writes=[WA_r[wi]], dma=KC // 4)
                for n in range(2):
                    ci = cg * 2 + n
                    for p0 in range(0, wd, 512):
                        b, br = nb()

                        def mm(e, wv=wv, bv=bv, n=n, p0=p0, b=b):
                            last = None
                            for k in range(KC):
                                last = e.matmul(b[:], wv[:, k, n * 128:(n + 1) * 128], bv[:, k, p0:p0 + 512],
                                                start=(k == 0), stop=(k == KC - 1))
                            return last
                        S.op("pe", mm, reads=[WA_r[wi], BB_r], writes=[br])
                        post(ci, c0 + p0, b, br)

    def post_store_fm(dst, dst_r, rowoff=0):
        def post(ci, t0, b, br):
            si = rot("sb16", 4)
            copy(evq(), SB16[si][:], b[:], [br], [SB16_r[si]])
            S.dma("sp", dst[rowoff + ci * 128:rowoff + (ci + 1) * 128, t0:t0 + 512], SB16[si][:],
                  reads=[SB16_r[si]], writes=[dst_r])
        return post

    def gemm_tm(src, src_r, KC, W, ncols, post, tiles=range(36), tiled=False):
        tiles = list(tiles)
        for jb in range(ncols // 512):
            bv = BB[:, 0:KC * 512].rearrange("p (c w) -> p c w", c=KC)

            def ldb(e, bv=bv, jb=jb):
                return [e.dma_start(out=bv[:, q * 4:(q + 1) * 4, :],
                                    in_=W[q * 512:(q + 1) * 512, jb * 512:(jb + 1) * 512]
                                    .rearrange("(c p) n -> p c n", p=128)) for q in range(KC // 4)]
            S.op("pool", ldb, reads=[in_r], writes=[BB_r], dma=KC // 4)
            for t2 in range(0, len(tiles), 2):
                pair = tiles[t2:t2 + 2]
                wi = rot("wa", 2)
                wv = WA[wi][:, 0:KC * 256].rearrange("p (c w) -> p c w", c=KC)
                t0 = pair[0] * 128

                def lda(e, wv=wv, t0=t0, n=len(pair)):
                    return [e.dma_start(out=wv[:, q * 4:(q + 1) * 4, 0:n * 128],
                                        in_=src[q * 512:(q + 1) * 512, t0:t0 + n * 128]
                                        .rearrange("(c p) t -> p c t", p=128)) for q in range(KC // 4)]
                if tiled:
                    wts_ = [WA[wi][:, pi_ * KC * 128:(pi_ + 1) * KC * 128].rearrange("p (c t) -> p c t", c=KC) for pi_ in range(2)]

                    def lda(e, wts_=wts_, pair=pair):
                        return [e.dma_start(out=wts_[pi_], in_=src[ti_]) for pi_, ti_ in enumerate(pair)]
                    S.op("pool", lda, reads=[src_r], writes=[WA_r[wi]], dma=len(pair))
                else:
                    S.op("pool", lda, reads=[src_r], writes=[WA_r[wi]], dma=KC // 4)
                for pi, ti in enumerate(pair):
                    b, br = nb()

                    def mm(e, wv=wv, bv=bv, pi=pi, b=b, wt_=(wts_[pi] if tiled else None)):
                        last = None
                        for k in range(KC):
                            lh = wt_[:, k, :] if wt_ is not None else wv[:, k, pi * 128:(pi + 1) * 128]
                            last = e.matmul(b[:], lh, bv[:, k, :], start=(k == 0), stop=(k == KC - 1))
                        return last
                    S.op("pe", mm, reads=[WA_r[wi], BB_r], writes=[br])
                    post(ti, jb * 512, b, br)

    def post_residual(xin, which, xout, xout_r, final=False):
        def post(ti, j0, b, br):
            r = 0 if ti < 32 else 1
            g = rot("stg", 6)
            S.dma("sp", STG[g][:], gate_scr[r * 2 + which, :, j0:j0 + 512], reads=[gate_r], writes=[STG_r[g]])
            xi = rot("stg", 6)
            ap, res = xin(ti)
            S.dma("sp", STG[xi][:], ap[:, j0:j0 + 512], reads=[res], writes=[STG_r[xi]])
            S.op("dve", lambda e: e.tensor_tensor(out=STG[g][:], in0=b[:], in1=STG[g][:], op=ALU.mult),
                 reads=[br, STG_r[g]], writes=[STG_r[g]])
            S.op("dve", lambda e: e.tensor_tensor(out=STG[xi][:], in0=STG[xi][:], in1=STG[g][:], op=ALU.add),
                 reads=[STG_r[g], STG_r[xi]], writes=[STG_r[xi]])
            S.dma("sp", xout[ti * 128:(ti + 1) * 128, j0:j0 + 512], STG[xi][:], reads=[STG_r[xi]], writes=[xout_r])
        return post

    def attention():
        KT = BB[:, 0:4 * 4352].rearrange("p (h t) -> p h t", h=4)
        VV = BB[:, 17408:17408 + 34 * 512].rearrange("p (b c) -> p b c", b=34)
        PKT = WA[0][:, 0:2048].rearrange("p (h t) -> p h t", h=4)
        PVV = WA[0][:, 2048:4096].rearrange("p (b c) -> p b c", b=4)
        RC = WA[1][:, 0:4608]
        RS = WA[1][:, 4608:9216]
        MK = WA[1][:, 9216:10240].rearrange("p (a q) -> p a q", a=2)
        S.dma("sp", RC, IN["c_ropeC"], writes=[WA_r[1]])
        S.dma("sp", RS, IN["c_ropeS"], writes=[WA_r[1]])
        S.dma("sp", MK, IN["c_bmask"], writes=[WA_r[1]])
        S.dma("sp", esink[:], sink.partition_broadcast(128), reads=[in_r], writes=[esink_r])
        S.op("act", lambda e: e.activation(out=esink[:], in_=esink[:], func=AF.Exp), reads=[esink_r], writes=[esink_r])
        for h in range(4):
            S.dma("sp", KT[:, h, 0:4096], qT[2048 + h * 128:2048 + (h + 1) * 128, 0:4096], reads=[qT_r], writes=[BB_r])
            S.dma("sp", PKT[:, h, :], qT[2048 + h * 128:2048 + (h + 1) * 128, 4096:4608], reads=[qT_r], writes=[WA_r[0]])
        for h in range(4):
            for p0 in range(0, 4096, 512):
                b, br = nb()
                S.op("pe", lambda e, b=b, h=h, p0=p0: e.matmul(b[:], perm[:], KT[:, h, p0:p0 + 512], start=True, stop=True),
                     reads=[BB_r, cst_r], writes=[br])
                si = rot("stg", 6)
                S.op("dve", lambda e, b=b, si=si, p0=p0: e.tensor_tensor(out=STG[si][:], in0=b[:], in1=RS[:, p0:p0 + 512],
                                                                      op=ALU.mult), reads=[br, WA_r[1]], writes=[STG_r[si]])
                s2 = rot("stg", 6)
                S.op("pool", lambda e, s2=s2, h=h, p0=p0: e.tensor_tensor(out=STG[s2][:], in0=KT[:, h, p0:p0 + 512],
                                                                         in1=RC[:, p0:p0 + 512], op=ALU.mult),
                     reads=[BB_r, WA_r[1]], writes=[STG_r[s2]])
                S.op("dve", lambda e, si=si, s2=s2, h=h, p0=p0: e.tensor_tensor(out=KT[:, h, p0:p0 + 512], in0=STG[si][:],
                                                                               in1=STG[s2][:], op=ALU.add),
                     reads=[STG_r[si], STG_r[s2]], writes=[BB_r])
        for blk in range(2):
            xi = rot("xt", 2)
            S.dma("sp", XT[xi][:, 0:512], ck[blk * 128:(blk + 1) * 128, :], reads=[in_r], writes=[XT_r[xi]])
            S.dma("sp", XT[xi][:, 512:1024], cv[blk * 128:(blk + 1) * 128, :], reads=[in_r], writes=[XT_r[xi]])
            copy("dve", VV[:, 32 + blk, :], XT[xi][:, 512:1024], [XT_r[xi]], [BB_r])
            b, br = nb()

            def tp(e, b=b, xi=xi):
                last = None
                for h in range(4):
                    last = e.transpose(b[:, h * 128:(h + 1) * 128], XT[xi][:, h * 128:(h + 1) * 128], idf[:])
                return last
            S.op("pe", tp, reads=[XT_r[xi], cst_r], writes=[br])
            for h in range(4):
                copy("act", KT[:, h, 4096 + blk * 128:4096 + (blk + 1) * 128], b[:, h * 128:(h + 1) * 128], [br], [BB_r])
        S.dma("sp", VV[:, 0:32, :], vtm[0:4096, :].rearrange("(b p) c -> p b c", p=128), reads=[vtm_r], writes=[BB_r])
        S.dma("sp", PVV, vtm[4096:4608, :].rearrange("(b p) c -> p b c", p=128), reads=[vtm_r], writes=[WA_r[0]])
        QBs = [WA[0][:, 4096 + i * 512:4608 + i * 512].rearrange("p (g q) -> p g q", g=4) for i in range(2)]
        QB_r = [sub(WA_r[0], "QB0"), sub(WA_r[0], "QB1")]
        ones = WA[0][:, 5120:5248]
        ones_r = sub(WA_r[0], "ones")
        PTs = [WA[0][:, 5248 + i * 512:5760 + i * 512] for i in range(12)]
        PT_r = [sub(WA_r[0], f"PT{i}") for i in range(12)]
        pti = [0]
        S.op("dve", lambda e: e.memset(ones, 1.0), writes=[ones_r])
        scale = 128 ** -0.5
        it = 0
        for qb in range(36):
            t0 = qb * 128
            if qb < 32:
                kblocks = [("w", kb) for kb in (qb - 1, qb, qb + 1) if 0 <= kb < 32] + [("c", 0), ("c", 1)]
            else:
                s0 = 32 + ((qb - 32) // 2) * 2
                kblocks = [("p", s0 - 32), ("p", s0 - 31)]
            for g in range(4):
                QB = QBs[it % 2]
                qr = QB_r[it % 2]
                it += 1
                QBf = QB.rearrange("p g q -> p (g q)")
                S.dma("pool", QB, qT[g * 512:(g + 1) * 512, t0:t0 + 128].rearrange("(g p) t -> p g t", p=128),
                      reads=[qT_r], writes=[qr])
                if qb < 32:
                    b, br = nb()
                    S.op("pe", lambda e, b=b, QBf=QBf: e.matmul(b[:], perm[:], QBf, start=True, stop=True),
                         reads=[qr, cst_r], writes=[br])
                    si = rot("stg", 6)
                    s2 = rot("stg", 6)

                    def rp(e, b=b, si=si, s2=s2, QB=QB, t0=t0):
                        last = None
                        for gg in range(4):
                            e.tensor_tensor(out=STG[si][:, gg * 128:(gg + 1) * 128], in0=b[:, gg * 128:(gg + 1) * 128],
                                            in1=RS[:, t0:t0 + 128], op=ALU.mult)
                            last = e.tensor_tensor(out=STG[s2][:, gg * 128:(gg + 1) * 128], in0=QB[:, gg, :],
                                                   in1=RC[:, t0:t0 + 128], op=ALU.mult)
                        return last
                    S.op("dve", rp, reads=[br, qr, WA_r[1]], writes=[STG_r[si], STG_r[s2]])
                    S.op("dve", lambda e, si=si, s2=s2, QBf=QBf: e.tensor_tensor(out=QBf, in0=STG[si][:], in1=STG[s2][:], op=ALU.add),
                         reads=[STG_r[si], STG_r[s2]], writes=[qr])
                nkb = len(kblocks)
                ops_ = []
                for ki, (kind, kb) in enumerate(kblocks):
                    if kind == "w":
                        kt = KT[:, g, kb * 128:(kb + 1) * 128]; vv = VV[:, kb, g * 128:(g + 1) * 128]
                    elif kind == "c":
                        kt = KT[:, g, 4096 + kb * 128:4096 + (kb + 1) * 128]; vv = VV[:, 32 + kb, g * 128:(g + 1) * 128]
                    else:
                        kt = PKT[:, g, kb * 128:(kb + 1) * 128]; vv = PVV[:, kb, g * 128:(g + 1) * 128]
                    pidx = pti[0] % 12
                    pti[0] += 1
                    PT, ptr_ = PTs[pidx], PT_r[pidx]
                    sb_, sbr = nb()
                    S.op("pe", lambda e, sb_=sb_, kt=kt, QBf=QBf: e.matmul(sb_[:], kt, QBf, start=True, stop=True),
                         reads=[BB_r, WA_r[0], qr], writes=[sbr])
                    S.op("act", lambda e, sb_=sb_, PT=PT: e.activation(out=PT, in_=sb_[:], func=AF.Exp, scale=scale),
                         reads=[sbr], writes=[ptr_])
                    if kind == "w" and kb != qb:
                        mi = 0 if kb < qb else 1
                        S.op("dve", lambda e, PT=PT, mi=mi: e.tensor_tensor(out=PT, in0=PT, in1=MK[:, mi, :], op=ALU.mult),
                             reads=[ptr_, WA_r[1]], writes=[ptr_])
                    ops_.append((vv, PT, ptr_))
                ob, obr = nb()
                db, dbr = nb()
                for ki, (vv, PT, ptr_) in enumerate(ops_):
                    S.op("pe", lambda e, ob=ob, vv=vv, PT=PT, ki=ki, nkb=nkb: e.matmul(ob[:], vv, PT, start=(ki == 0),
                                                                                   stop=(ki == nkb - 1)),
                         reads=[BB_r, WA_r[0], ptr_], writes=[obr])
                    S.op("pe", lambda e, db=db, PT=PT, ki=ki, nkb=nkb: e.matmul(db[:], ones, PT, start=(ki == 0),
                                                                            stop=(ki == nkb - 1)),
                         reads=[ones_r, ptr_], writes=[dbr])
                si = rot("stg", 6)

                def dn(e, db=db, si=si, g=g):
                    last = None
                    for gg in range(4):
                        last = e.tensor_scalar(out=STG[si][:, gg * 128:(gg + 1) * 128], in0=db[:, gg * 128:(gg + 1) * 128],
                                               scalar1=esink[:, g * 4 + gg:g * 4 + gg + 1], scalar2=None, op0=ALU.add)
                    return last
                S.op("dve", dn, reads=[dbr, esink_r], writes=[STG_r[si]])
                S.op("act", lambda e, si=si: e.activation(out=STG[si][:], in_=STG[si][:], func=AF.Ln), reads=[STG_r[si]], writes=[STG_r[si]])
                S.op("act", lambda e, si=si: e.activation(out=STG[si][:], in_=STG[si][:], func=AF.Exp, scale=-1.0),
                     reads=[STG_r[si]], writes=[STG_r[si]])
                oi = rot("sb16", 4)
                S.op("dve", lambda e, ob=ob, si=si, oi=oi: e.tensor_tensor(out=SB16[oi][:], in0=ob[:], in1=STG[si][:],
                                                                          op=ALU.mult),
                     reads=[obr, STG_r[si]], writes=[SB16_r[oi]])
                S.dma("sp", oT[qb][:, g * 4:(g + 1) * 4, :],
                      SB16[oi][:].rearrange("p (g q) -> p g q", g=4), reads=[SB16_r[oi]], writes=[oT_r])
        join(WA_r[0], QB_r + PT_r + [ones_r])

    def ffn_act(i):
        S.dma("sp", stg_s[:], ffn_convT[i, 0:128, :], reads=[in_r], writes=[stg_r])
        b, br = nb()
        S.op("pe", lambda e: e.transpose(b[:, 0:128], stg_s[:], idf[:]), reads=[stg_r, cst_r], writes=[br])
        copy("act", fcv[:].rearrange("p a c -> p (a c)")[:, 0:128], b[:, 0:128], [br], [fcv_r])
        S.dma("sp", stg_s[0:4, :], ffn_convT[i, 128:132, :], reads=[in_r], writes=[stg_r])
        b2, b2r = nb()
        S.op("pe", lambda e: e.transpose(b2[:, 0:4], stg_s[0:4, :], idf[0:4, 0:4]), reads=[stg_r, cst_r], writes=[b2r])
        copy("act", fcv[:].rearrange("p a c -> p (a c)")[:, 128:132], b2[:, 0:4], [b2r], [fcv_r])
        bufs = []
        for i2 in range(2):
            bufs.append((BB[:, (3 * i2) * NT:(3 * i2 + 1) * NT], BB[:, (3 * i2 + 1) * NT:(3 * i2 + 2) * NT],
                         BB[:, (3 * i2 + 2) * NT:(3 * i2 + 3) * NT],
                         sub(BB_r, f"fa{i2}"), sub(BB_r, f"fb{i2}"), sub(BB_r, f"fg{i2}")))
        PARTS = [(0, 2304), (2304, 4608)]
        accp_r = [sub(ACC_r, f"accq{i_}") for i_ in range(2)]
        def fa_loads(j):
            AA, BBb, GG, ar, brr, gr = bufs[j % 2]
            S.dma("sp", AA, abT[j * 128:(j + 1) * 128, :], reads=[abT_r], writes=[ar])
            S.dma("sp", BBb, abT[DFF + j * 128:DFF + (j + 1) * 128, :], reads=[abT_r], writes=[brr])
        fa_loads(0)
        for j in range(44):
            AA, BBb, GG, ar, brr, gr = bufs[j % 2]
            if j + 1 < 44:
                fa_loads(j + 1)
            gpr = [sub(gr, f"gq{j}_{i_}") for i_ in range(2)]
            for pi_, (p0, p1) in enumerate(PARTS):
                acr = accp_r[pi_]
                S.op("act", lambda e, j=j, AA=AA, p0=p0, p1=p1: e.activation(out=ACC[:, p0:p1], in_=AA[:, p0:p1], func=AF.Copy,
                                                                          scale=fcv[:, 1, j:j + 1]),
                     reads=[ar, fcv_r], writes=[acr])
                for tap, (lo, hi) in ((0, (1, 0)), (2, (0, 1))):
                    rngs = []
                    for (s0, s1) in SEQS:
                        o0, o1 = max(s0 + lo, p0), min(s1 - hi, p1)
                        if o1 > o0:
                            rngs.append((o0, o1))

                    def taps(e, j=j, AA=AA, tap=tap, lo=lo, hi=hi, rngs=rngs):
                        last = None
                        for (o0, o1) in rngs:
                            last = e.scalar_tensor_tensor(out=ACC[:, o0:o1], in0=AA[:, o0 - lo + hi:o1 - lo + hi],
                                                          scalar=fcv[:, tap, j:j + 1], in1=ACC[:, o0:o1],
                                                          op0=ALU.mult, op1=ALU.add)
                        return last
                    S.op("dve", taps, reads=[ar, fcv_r, acr], writes=[acr])
                S.op("act", lambda e, GG=GG, p0=p0, p1=p1: e.activation(out=GG[:, p0:p1], in_=ACC[:, p0:p1], func=AF.Silu),
                     reads=[acr], writes=[gpr[pi_]])
                S.op("pool", lambda e, GG=GG, BBb=BBb, p0=p0, p1=p1: e.tensor_tensor(out=GG[:, p0:p1], in0=GG[:, p0:p1],
                                                                                   in1=BBb[:, p0:p1], op=ALU.mult),
                     reads=[gpr[pi_], brr], writes=[gpr[pi_]])
                S.dma("sp", gT[j * 128:(j + 1) * 128, p0:p1], GG[:, p0:p1], reads=[gpr[pi_]], writes=[gT_r])
            join(gr, gpr)
        join(ACC_r, accp_r)
        join(BB_r, [x for bf_ in bufs for x in bf_[3:]])

    CVT = [S.sb(f"CVT{i}", [128, 512], BF16) for i in range(6)]
    CVT_r = [Res(f"CVT{i}") for i in range(6)]
    rr["cvt"] = 0
    hyc = S.sb("hyc", [128, 16], F32); hyc_r = Res("hyc")
    wf1 = S.sb("wf1", [33, 64], F32); wf2 = S.sb("wf2", [64, 64], F32); wf_r = Res("wf")
    hsdP = dscr("hsdP", [2, LP, 2 * D], BF16); hsdP_r = Res("hsdP", True)
    ghatP = dscr("ghatP", [2, LP, 2 * D], BF16); ghatP_r = Res("ghatP", True)
    yhatP = dscr("yhatP", [2, 2 * LP, D], BF16); yhatP_r = Res("yhatP", True)

    def post_ptm(ti, j0, b, br):
        si = rot("sb16", 4)
        copy(evq(), SB16[si][:], b[:], [br], [SB16_r[si]])
        S.dma("sp", ptm[ti * 128:(ti + 1) * 128, j0:j0 + 512], SB16[si][:], reads=[SB16_r[si]], writes=[ptm_r])

    def hy_conv3():
        starts = {0, 32, 34}
        ends = {31, 33, 35}
        W2 = 2048
        ins_ = [[BB[:, (k * 2 + i2) * W2:(k * 2 + i2 + 1) * W2] for i2 in range(2)] for k in range(3)]
        in_r_ = [[sub(BB_r, f"cv{k}{i2}") for i2 in range(2)] for k in range(3)]
        wts = [BB[:, 12288 + k * 4096:12288 + (k + 1) * 4096].bitcast(F32) for k in range(3)]
        wt_r = [sub(BB_r, f"cw{k}") for k in range(3)]
        accs = [BB[:, 24576 + i2 * 4096:24576 + (i2 + 1) * 4096].bitcast(F32) for i2 in range(2)]
        acc_r = [sub(BB_r, f"ca{i2}") for i2 in range(2)]
        T1, T2 = ACC[:, 0:W2], ACC[:, W2:2 * W2]
        T1_r, T2_r = sub(ACC_r, "T1"), sub(ACC_r, "T2")
        it = 0
        for cb in range(3):
            c0 = cb * W2
            for k in range(3):
                S.dma("sp", wts[k], hy_conv[k, c0:c0 + W2].partition_broadcast(128), reads=[in_r], writes=[wt_r[k]])
            def loads(ti, i2):
                t0 = ti * 128
                pc, pm, pp = ins_[0][i2], ins_[1][i2], ins_[2][i2]
                pcr, pmr, ppr = in_r_[0][i2], in_r_[1][i2], in_r_[2][i2]
                S.dma("act", pc, ptm[t0:t0 + 128, c0:c0 + W2], reads=[ptm_r], writes=[pcr])
                if ti in starts:
                    S.op("pool", lambda e, pm=pm: e.memset(pm, 0.0), writes=[pmr])
                    S.dma("act", pm[1:128, :], ptm[t0:t0 + 127, c0:c0 + W2], reads=[ptm_r], writes=[pmr])
                else:
                    S.dma("act", pm, ptm[t0 - 1:t0 + 127, c0:c0 + W2], reads=[ptm_r], writes=[pmr])
                if ti in ends:
                    S.op("pool", lambda e, pp=pp: e.memset(pp, 0.0), writes=[ppr])
                    S.dma("act", pp[0:127, :], ptm[t0 + 1:t0 + 128, c0:c0 + W2], reads=[ptm_r], writes=[ppr])
                else:
                    S.dma("act", pp, ptm[t0 + 1:t0 + 129, c0:c0 + W2], reads=[ptm_r], writes=[ppr])
            loads(0, it % 2)
            for ti in range(36):
                t0 = ti * 128
                i2 = it % 2
                it += 1
                if ti + 1 < 36:
                    loads(ti + 1, it % 2)
                pc, pm, pp = ins_[0][i2], ins_[1][i2], ins_[2][i2]
                pcr, pmr, ppr = in_r_[0][i2], in_r_[1][i2], in_r_[2][i2]
                ac, acr = accs[i2], acc_r[i2]
                S.op("dve", lambda e, ac=ac, pc=pc: e.tensor_tensor(out=ac, in0=pc, in1=wts[1], op=ALU.mult),
                     reads=[pcr, wt_r[1]], writes=[acr])
                S.op("pool", lambda e, pm=pm: e.tensor_tensor(out=T1, in0=pm, in1=wts[0], op=ALU.mult),
                     reads=[pmr, wt_r[0]], writes=[T1_r])
                S.op("dve", lambda e, pp=pp: e.tensor_tensor(out=T2, in0=pp, in1=wts[2], op=ALU.mult),
                     reads=[ppr, wt_r[2]], writes=[T2_r])
                S.op("pool", lambda e, ac=ac: e.tensor_tensor(out=ac, in0=ac, in1=T1, op=ALU.add),
                     reads=[acr, T1_r], writes=[acr])
                S.op("dve", lambda e, ac=ac, pc=pc: e.tensor_tensor(out=pc, in0=ac, in1=T2, op=ALU.add),
                     reads=[acr, T2_r], writes=[pcr])
                S.dma("sp", c3[t0:t0 + 128, c0:c0 + W2], pc, reads=[pcr], writes=[c3_r])
        join(BB_r, [x for l in in_r_ for x in l] + wt_r + acc_r)
        join(ACC_r, [T1_r, T2_r])

    def hy_filters(nm, L, hs_dst, hs_dst_r):
        featT = IN["c_featT" + nm]
        FT = ACC[0:33, 0:L]
        S.dma("sp", FT, featT, writes=[ACC_r])
        S.dma("sp", wf1[:], hy_w_f1, reads=[in_r], writes=[wf_r])
        S.dma("sp", wf2[:], hy_w_f2, reads=[in_r], writes=[wf_r])
        S.dma("sp", hyc[0:64, 0:3], hy_small, reads=[in_r], writes=[hyc_r])
        def cols0(e):
            e.tensor_scalar(out=hyc[0:64, 3:4], in0=hyc[0:64, 2:3], scalar1=0.5, scalar2=None, op0=ALU.mult)
            return e.tensor_scalar(out=hyc[0:64, 4:5], in0=hyc[0:64, 2:3], scalar1=0.25, scalar2=None, op0=ALU.mult)
        S.op("dve", cols0, reads=[hyc_r], writes=[hyc_r])

        def cols(e):
            e.tensor_tensor(out=hyc[0:64, 5:6], in0=hyc[0:64, 3:4], in1=hyc[0:64, 0:1], op=ALU.mult)
            e.tensor_tensor(out=hyc[0:64, 6:7], in0=hyc[0:64, 4:5], in1=hyc[0:64, 0:1], op=ALU.mult)
            e.tensor_tensor(out=hyc[0:64, 7:8], in0=hyc[0:64, 3:4], in1=hyc[0:64, 1:2], op=ALU.mult)
            return e.tensor_tensor(out=hyc[0:64, 8:9], in0=hyc[0:64, 4:5], in1=hyc[0:64, 1:2], op=ALU.mult)
        S.op("dve", cols, reads=[hyc_r], writes=[hyc_r])
        H1 = BB[:, 0:8192].bitcast(F32)
        H2 = BB[:, 8192:16384].bitcast(F32)
        H2b = BB[:, 16384:20480]
        W3 = WA[0][0:64, 0:8192]
        S.dma("pool", W3, hy_w_f3, reads=[in_r], writes=[WA_r[0]])

        def sin_layer(lhsT, K, src, src_r, dst, bcol):
            for p0 in range(0, L, 512):
                w = min(512, L - p0)
                b, br = nb()
                S.op("pe", lambda e, b=b, p0=p0, w=w: e.matmul(b[0:64, 0:w], lhsT, src[0:K, p0:p0 + w], start=True, stop=True),
                     reads=[src_r, wf_r], writes=[br])
                s2, s4 = rot("stg", 6), rot("stg", 6)
                S.op("act", lambda e, b=b, s2=s2, w=w: e.activation(out=STG[s2][0:64, 0:w], in_=b[0:64, 0:w], func=AF.Sin,
                                                                   scale=hyc[0:64, 3:4], bias=hyc[0:64, bcol:bcol + 1]),
                     reads=[br, hyc_r], writes=[STG_r[s2]])
                S.op("act", lambda e, b=b, s4=s4, w=w: e.activation(out=STG[s4][0:64, 0:w], in_=b[0:64, 0:w], func=AF.Sin,
                                                                   scale=hyc[0:64, 4:5], bias=hyc[0:64, bcol + 1:bcol + 2]),
                     reads=[br, hyc_r], writes=[STG_r[s4]])
                S.op("dve", lambda e, s4=s4, w=w: e.tensor_tensor(out=STG[s4][0:64, 0:w], in0=STG[s4][0:64, 0:w],
                                                                 in1=STG[s4][0:64, 0:w], op=ALU.mult),
                     reads=[STG_r[s4]], writes=[STG_r[s4]])
                S.op("dve", lambda e, s4=s4, w=w: e.tensor_scalar(out=STG[s4][0:64, 0:w], in0=STG[s4][0:64, 0:w], scalar1=-2.0,
                                                                 scalar2=1.0, op0=ALU.mult, op1=ALU.add),
                     reads=[STG_r[s4]], writes=[STG_r[s4]])
                S.op("dve", lambda e, s2=s2, s4=s4, w=w, p0=p0: e.scalar_tensor_tensor(
                    out=dst[0:64, p0:p0 + w], in0=STG[s2][0:64, 0:w], scalar=2.0, in1=STG[s4][0:64, 0:w],
                    op0=ALU.mult, op1=ALU.mult), reads=[STG_r[s2], STG_r[s4]], writes=[BB_r])
        sin_layer(wf1[:], 33, ACC, ACC_r, H1, 5)
        sin_layer(wf2[:], 64, H1, BB_r, H2, 7)
        copy("dve", H2b[0:64, 0:L], H2[0:64, 0:L], [BB_r], [BB_r])
        decf, decb = IN["c_decf" + nm], IN["c_decb" + nm]
        dtl = [(XT[0][:], XT[1][:], sub(XT_r[0], "dcf0"), sub(XT_r[1], "dcb0")),
               (ACC[:, 0:2048], ACC[:, 2048:4096], sub(ACC_r, "dcf1"), sub(ACC_r, "dcb1"))]
        for tc in range(L // 128):
            dF, dB, dFr, dBr = dtl[tc % 2]
            S.dma("act", dF, decf[tc * 128:(tc + 1) * 128, :], writes=[dFr])
            S.dma("act", dB, decb[tc * 128:(tc + 1) * 128, :], writes=[dBr])
            for o in range(2):
                for db in range(4):
                    cf = o * 2048 + db * 512
                    cs = slice(db * 512, (db + 1) * 512)
                    bf_, bfr = nb()
                    bb_, bbr = nb()
                    S.op("pe", lambda e, bf_=bf_, tc=tc, cf=cf: e.matmul(bf_[:], H2b[0:64, tc * 128:(tc + 1) * 128],
                                                                      W3[:, cf:cf + 512], start=True, stop=True),
                         reads=[BB_r, WA_r[0]], writes=[bfr])
                    S.op("pe", lambda e, bb_=bb_, tc=tc, cf=cf: e.matmul(bb_[:], H2b[0:64, tc * 128:(tc + 1) * 128],
                                                                      W3[:, 4096 + cf:4096 + cf + 512], start=True, stop=True),
                         reads=[BB_r, WA_r[0]], writes=[bbr])
                    d0, d1 = rot("stg", 6), rot("stg", 6)
                    S.op("dve", lambda e, bf_=bf_, d0=d0, dF=dF, cs=cs: e.tensor_tensor(out=STG[d0][:], in0=bf_[:], in1=dF[:, cs], op=ALU.mult),
                         reads=[bfr, dFr], writes=[STG_r[d0]])
                    S.op("dve", lambda e, bb_=bb_, d1=d1, dB=dB, cs=cs: e.tensor_tensor(out=STG[d1][:], in0=bb_[:], in1=dB[:, cs], op=ALU.mult),
                         reads=[bbr, dBr], writes=[STG_r[d1]])
                    c0_, c1_ = rot("cvt", 6), rot("cvt", 6)
                    S.op("pool", lambda e, d0=d0, d1=d1, c0_=c0_: e.tensor_tensor(out=CVT[c0_][:], in0=STG[d0][:], in1=STG[d1][:],
                                                                              op=ALU.add),
                         reads=[STG_r[d0], STG_r[d1]], writes=[CVT_r[c0_]])
                    S.op("pool", lambda e, d0=d0, d1=d1, c1_=c1_: e.tensor_tensor(out=CVT[c1_][:], in0=STG[d1][:], in1=STG[d0][:],
                                                                              op=ALU.subtract),
                         reads=[STG_r[d0], STG_r[d1]], writes=[CVT_r[c1_]])
                    S.dma("sp", hs_dst[0, tc * 128:(tc + 1) * 128, cf:cf + 512], CVT[c0_][:], reads=[CVT_r[c0_]], writes=[hs_dst_r])
                    S.dma("sp", hs_dst[1, tc * 128:(tc + 1) * 128, cf:cf + 512], CVT[c1_][:], reads=[CVT_r[c1_]], writes=[hs_dst_r])
        join(XT_r[0], [dtl[0][2]]); join(XT_r[1], [dtl[0][3]]); join(ACC_r, [dtl[1][2], dtl[1][3]])

    def dft_gemm(A_t, parts, nI, CC, B_r, ncols, JB, post):
        for j0 in range(0, ncols, JB):
            bvs = []
            for pi_, (ioff, bfn) in enumerate(parts):
                if pi_ > 0 and bfn is parts[0][1]:
                    bvs.append(bvs[0])
                    continue
                off = pi_ * CC * JB
                bv = BB[:, off:off + CC * JB].rearrange("p (c w) -> p c w", c=CC)
                src = bfn(j0, JB)

                def ldb(e, bv=bv, src=src):
                    n = max(1, CC // 8)
                    step = CC // n
                    return [e.dma_start(out=bv[:, q * step:(q + 1) * step, :],
                                        in_=src[q * step * 128:(q + 1) * step * 128, :].rearrange("(c p) w -> p c w", p=128))
                            for q in range(n)]
                S.op("sp", ldb, reads=[B_r], writes=[BB_r], dma=max(1, CC // 8))
                bvs.append(bv)
            for i in range(nI):
                wi = rot("wa", 2)
                avs = []
                for pi_, (ioff, bfn) in enumerate(parts):
                    av = WA[wi][:, pi_ * CC * 128:(pi_ + 1) * CC * 128].rearrange("p (c m) -> p c m", c=CC)
                    S.dma("pool", av, A_t[ioff + i], writes=[WA_r[wi]])
                    avs.append(av)
                res = []
                for pi_ in range(len(parts)):
                    bl = []
                    for h0 in range(0, JB, 512):
                        b, br = nb()

                        def mm(e, av=avs[pi_], bv=bvs[pi_], h0=h0, b=b):
                            last = None
                            for c in range(CC):
                                last = e.matmul(b[:], av[:, c, :], bv[:, c, h0:h0 + 512], start=(c == 0), stop=(c == CC - 1))
                            return last
                        S.op("pe", mm, reads=[WA_r[wi], BB_r], writes=[br])
                        bl.append((b, br))
                    res.append(bl)
                post(i, j0, res)

    def hy_seq(nm, L, row0, hs_, hs_r_, gh_, gh_r_, yh_, yh_r_):
        CC = L // 128
        fwd_t, inv_t = IN["c_fwd" + nm], IN["c_inv" + nm]
        for part in range(2):
            def post_g(i, j0, res, part=part):
                for h, (b, br) in enumerate(res[0]):
                    ci = rot("cvt", 6)
                    copy(evq(), CVT[ci][:], b[:], [br], [CVT_r[ci]])
                    S.dma("sp", gh_[part, i * 128:(i + 1) * 128, j0 + h * 512:j0 + (h + 1) * 512], CVT[ci][:],
                          reads=[CVT_r[ci]], writes=[gh_r_])
            dft_gemm(fwd_t, [(part * CC, lambda j0, JB, part=part: hs_[part, :, j0:j0 + JB])], CC, CC, hs_r_, 2 * D, 1024, post_g)
        for o in range(2):
            if o == 0:
                vsrc = lambda j0, JB: c3[row0:row0 + L, 2 * D + j0:2 * D + j0 + JB]
                v_r = c3_r
            else:
                vsrc = lambda j0, JB: z1[row0:row0 + L, j0:j0 + JB]
                v_r = z1_r

            def post_f(i, j0, res, o=o):
                for h in range(len(res[0])):
                    (a, ar), (b, br) = res[0][h], res[1][h]
                    cg_, sg_ = rot("cvt", 6), rot("cvt", 6)
                    col = o * D + j0 + h * 512
                    S.dma("sp", CVT[cg_][:], gh_[0, i * 128:(i + 1) * 128, col:col + 512], reads=[gh_r_], writes=[CVT_r[cg_]])
                    S.dma("sp", CVT[sg_][:], gh_[1, i * 128:(i + 1) * 128, col:col + 512], reads=[gh_r_], writes=[CVT_r[sg_]])
                    t1, t2 = rot("stg", 6), rot("stg", 6)
                    S.op("dve", lambda e, a=a, t1=t1, cg_=cg_: e.tensor_tensor(out=STG[t1][:], in0=a[:], in1=CVT[cg_][:], op=ALU.mult),
                         reads=[ar, CVT_r[cg_]], writes=[STG_r[t1]])
                    S.op("dve", lambda e, b=b, t2=t2, sg_=sg_: e.tensor_tensor(out=STG[t2][:], in0=b[:], in1=CVT[sg_][:], op=ALU.mult),
                         reads=[br, CVT_r[sg_]], writes=[STG_r[t2]])
                    y0 = rot("sb16", 4)
                    S.op("dve", lambda e, t1=t1, t2=t2, y0=y0: e.tensor_tensor(out=SB16[y0][:], in0=STG[t1][:], in1=STG[t2][:], op=ALU.add),
                         reads=[STG_r[t1], STG_r[t2]], writes=[SB16_r[y0]])
                    S.dma("sp", yh_[o, i * 128:(i + 1) * 128, j0 + h * 512:j0 + (h + 1) * 512], SB16[y0][:],
                          reads=[SB16_r[y0]], writes=[yh_r_])
                    t3, t4 = rot("stg", 6), rot("stg", 6)
                    S.op("dve", lambda e, b=b, t3=t3, cg_=cg_: e.tensor_tensor(out=STG[t3][:], in0=b[:], in1=CVT[cg_][:], op=ALU.mult),
                         reads=[br, CVT_r[cg_]], writes=[STG_r[t3]])
                    S.op("dve", lambda e, a=a, t4=t4, sg_=sg_: e.tensor_tensor(out=STG[t4][:], in0=a[:], in1=CVT[sg_][:], op=ALU.mult),
                         reads=[ar, CVT_r[sg_]], writes=[STG_r[t4]])
                    y1 = rot("sb16", 4)
                    S.op("dve", lambda e, t3=t3, t4=t4, y1=y1: e.tensor_tensor(out=SB16[y1][:], in0=STG[t3][:], in1=STG[t4][:],
                                                                             op=ALU.subtract),
                         reads=[STG_r[t3], STG_r[t4]], writes=[SB16_r[y1]])
                    S.dma("sp", yh_[o, L + i * 128:L + (i + 1) * 128, j0 + h * 512:j0 + (h + 1) * 512], SB16[y1][:],
                          reads=[SB16_r[y1]], writes=[yh_r_])
            dft_gemm(fwd_t, [(0, vsrc), (CC, vsrc)], CC, CC, v_r, D, 1024, post_f)

            def post_i(i, j0, res, o=o):
                (y, yr) = res[0][0]
                t0 = row0 + i * 128
                bi_ = rot("stg", 6)
                S.dma("sp", STG[bi_][:], hy_bias[o, j0:j0 + 512].partition_broadcast(128), reads=[in_r], writes=[STG_r[bi_]])
                vv, gg = rot("cvt", 6), rot("cvt", 6)
                if o == 0:
                    S.dma("sp", CVT[vv][:], c3[t0:t0 + 128, 2 * D + j0:2 * D + j0 + 512], reads=[c3_r], writes=[CVT_r[vv]])
                    S.dma("sp", CVT[gg][:], c3[t0:t0 + 128, j0:j0 + 512], reads=[c3_r], writes=[CVT_r[gg]])
                else:
                    S.dma("sp", CVT[vv][:], z1[t0:t0 + 128, j0:j0 + 512], reads=[z1_r], writes=[CVT_r[vv]])
                    S.dma("sp", CVT[gg][:], c3[t0:t0 + 128, D + j0:D + j0 + 512], reads=[c3_r], writes=[CVT_r[gg]])
                S.op("dve", lambda e, bi_=bi_, vv=vv: e.tensor_tensor(out=STG[bi_][:], in0=CVT[vv][:], in1=STG[bi_][:], op=ALU.mult),
                     reads=[CVT_r[vv], STG_r[bi_]], writes=[STG_r[bi_]])
                S.op("dve", lambda e, y=y, bi_=bi_: e.tensor_tensor(out=STG[bi_][:], in0=y[:], in1=STG[bi_][:], op=ALU.add),
                     reads=[yr, STG_r[bi_]], writes=[STG_r[bi_]])
                S.op("dve", lambda e, bi_=bi_, gg=gg, vv=vv: e.tensor_tensor(out=CVT[vv][:], in0=STG[bi_][:], in1=CVT[gg][:], op=ALU.mult),
                     reads=[STG_r[bi_], CVT_r[gg]], writes=[CVT_r[vv]])
                dst, dr = (z1, z1_r) if o == 0 else (zz, zz_r)
                S.dma("sp", dst[t0:t0 + 128, j0:j0 + 512], CVT[vv][:], reads=[CVT_r[vv]], writes=[dr])
            dft_gemm(inv_t, [(0, lambda j0, JB, o=o: yh_[o, :, j0:j0 + JB])], CC, 2 * CC, yh_r_, D, 512, post_i)


    PL = {"fp": [], "bf": [], "fi": 0, "bi": 0, "subs": []}

    def pools_open():
        fp = [(STG[i][:], STG_r[i]) for i in range(1, 6)]
        bfp = [(CVT[i][:], CVT_r[i]) for i in range(6)] + [(SB16[i][:], SB16_r[i]) for i in range(4)]
        subs = []
        for k in range(2):
            for q in range(4):
                r_ = sub(XT_r[k], f"xtp{k}{q}")
                subs.append((XT_r[k], r_))
                fp.append((XT[k][:, q * 512:(q + 1) * 512], r_))
        for q in range(9):
            r_ = sub(ACC_r, f"accp{q}")
            subs.append((ACC_r, r_))
            fp.append((ACC[:, q * 512:(q + 1) * 512], r_))
        hv = HTS[:].rearrange("p c t -> p (c t)")
        for q in range(8):
            r_ = sub(HTS_r, f"htsp{q}")
            subs.append((HTS_r, r_))
            bfp.append((hv[:, q * 512:(q + 1) * 512], r_))
        for (t_, tr_, nm_) in ((XN, XN_r, "xnp"), (JK, JK_r, "jkp")):
            for q in range(4):
                r_ = sub(tr_, f"{nm_}{q}")
                subs.append((tr_, r_))
                bfp.append((t_[:, q * 512:(q + 1) * 512], r_))
        PL["fp"], PL["bf"], PL["subs"] = fp, bfp, subs

    def pools_close():
        for parent, r_ in PL["subs"]:
            join(parent, [r_])
        PL["fp"], PL["bf"], PL["subs"] = [], [], []

    def fpt():
        i = PL["fi"] % len(PL["fp"])
        PL["fi"] += 1
        return PL["fp"][i]

    def bft():
        i = PL["bi"] % len(PL["bf"])
        PL["bi"] += 1
        return PL["bf"][i]

    B1 = dscr("B1", [2, 128, 32, D], BF16); B1_r = Res("B1", True)
    D1 = dscr("D1", [2, 128, 32, D], BF16); D1_r = Res("D1", True)
    G2 = dscr("G2", [2, 128, 32, 2 * D], BF16); G2_r = Res("G2", True)
    fcs = S.sb("fcs", [128, 8, 128], BF16); fcs_r = Res("fcs")
    tws = S.sb("tws", [128, 2, 64], F32); tws_r = Res("tws")

    def fft_consts():
        for i_, nm_ in enumerate(("f1c", "f1s", "i1c", "i1s", "cbm", "sbm", "ncbm", "nsbm")):
            S.dma("sp", fcs[:, i_, :], IN["c_" + nm_], writes=[fcs_r])
        S.dma("sp", tws[:, 0, :], IN["c_tw1"], writes=[tws_r])
        S.dma("sp", tws[:, 1, :], IN["c_tw2"], writes=[tws_r])

    BT = {"t": [], "i": 0, "subs": []}

    def bt_open():
        t, subs = [], []
        for (tile_, tr_, nm_, n_) in ((ACC, ACC_r, "ba", 4), (XT[0], XT_r[0], "bx0", 2), (XT[1], XT_r[1], "bx1", 2)):
            v = tile_[:].bitcast(BF16)
            for q in range(n_):
                r_ = sub(tr_, f"{nm_}{q}")
                subs.append((tr_, r_))
                t.append((v[:, q * 2048:(q + 1) * 2048], r_))
        hv = HTS[:].rearrange("p c t -> p (c t)")
        for q in range(2):
            r_ = sub(HTS_r, f"bh{q}")
            subs.append((HTS_r, r_))
            t.append((hv[:, q * 2048:(q + 1) * 2048], r_))
        for (tile_, tr_, nm_) in ((XN, XN_r, "bxn"), (JK, JK_r, "bjk")):
            r_ = sub(tr_, nm_)
            subs.append((tr_, r_))
            t.append((tile_[:], r_))
        BT["t"], BT["subs"] = t, subs

    def bt_close():
        for parent, r_ in BT["subs"]:
            join(parent, [r_])
        BT["t"], BT["subs"] = [], []

    def btt():
        i = BT["i"] % len(BT["t"])
        BT["i"] += 1
        return BT["t"][i]

    rr["sfp"] = 0

    def sfp():
        i = rot("sfp", 10)
        return (STG[i][:], STG_r[i]) if i < 6 else (STX[i - 6][:], STX_r[i - 6])

    def sbf():
        k = rot("sbf", 10)
        return (CVT[k][:], CVT_r[k]) if k < 6 else (SB16[k - 6][:], SB16_r[k - 6])
    rr["sbf"] = 0

    def twiddle_evac(pb, pbr, qb_, qbr, ti_, col, o1, o1r, o2, o2r):
        cc = tws[:, ti_, col:col + 1]
        ss = tws[:, ti_, 32 + col:32 + col + 1]
        (u1, u1r), (u2, u2r) = sfp(), sfp()
        S.op("act", lambda e: e.activation(out=u1, in_=qb_[:], func=AF.Copy, scale=ss), reads=[qbr, tws_r], writes=[u1r])
        S.op("act", lambda e: e.activation(out=u2, in_=qb_[:], func=AF.Copy, scale=cc), reads=[qbr, tws_r], writes=[u2r])
        S.op("dve", lambda e: e.scalar_tensor_tensor(out=o1, in0=pb[:], scalar=cc, in1=u1, op0=ALU.mult, op1=ALU.subtract),
             reads=[pbr, u1r, tws_r], writes=[o1r])
        S.op("dve", lambda e: e.scalar_tensor_tensor(out=o2, in0=pb[:], scalar=ss, in1=u2, op0=ALU.mult, op1=ALU.add),
             reads=[pbr, u2r, tws_r], writes=[o2r])

    def fft_s1(src_fn, src_r):
        xb = [BB[:, i2 * 16384:(i2 + 1) * 16384].rearrange("p (r c) -> p r c", r=8) for i2 in range(2)]
        xb_r = [sub(BB_r, f"xb{i2}") for i2 in range(2)]
        srcv = src_fn().rearrange("(a r) c -> a r c", r=32)
        for rq in range(4):
            i2 = rq % 2

            def ld(e, i2=i2, rq=rq):
                return [e.dma_start(out=xb[i2][:, q * 2:(q + 1) * 2, :], in_=srcv[:, rq * 8 + q * 2:rq * 8 + (q + 1) * 2, :])
                        for q in range(4)]
            S.op("pool", ld, reads=[src_r], writes=[xb_r[i2]], dma=4)
            for r8 in range(8):
                r = rq * 8 + r8
                (ore, orr), (oim, oir) = btt(), btt()
                for db in range(4):
                    cs = slice(db * 512, (db + 1) * 512)
                    pb, pbr = nb()
                    qb_, qbr = nb()
                    S.op("pe", lambda e, pb=pb, i2=i2, r8=r8, cs=cs: e.matmul(pb[:], fcs[:, 0, :], xb[i2][:, r8, cs], start=True, stop=True),
                         reads=[xb_r[i2], fcs_r], writes=[pbr])
                    S.op("pe", lambda e, qb_=qb_, i2=i2, r8=r8, cs=cs: e.matmul(qb_[:], fcs[:, 1, :], xb[i2][:, r8, cs], start=True, stop=True),
                         reads=[xb_r[i2], fcs_r], writes=[qbr])
                    twiddle_evac(pb, pbr, qb_, qbr, 0, r, ore[:, cs], orr, oim[:, cs], oir)
                S.dma("sp", B1[0, :, r, :], ore, reads=[orr], writes=[B1_r])
                S.dma("sp", B1[1, :, r, :], oim, reads=[oir], writes=[B1_r])
        join(BB_r, xb_r)

    def fft_s2_load(src, src_r, jq, i2, bufs, bufs_r):
        for c_ in range(2):
            v = src[c_].rearrange("(j q) r d -> (q r) j d", q=4)
            dst = bufs[i2][c_]

            def ld(e, v=v, dst=dst, jq=jq):
                return [e.dma_start(out=dst[:, q * 2:(q + 1) * 2, :], in_=v[:, jq * 8 + q * 2:jq * 8 + (q + 1) * 2, :]) for q in range(4)]
            S.op("pool", ld, reads=[src_r], writes=[bufs_r[i2][c_]], dma=4)

    def s2_bufs():
        bufs = [[BB[:, (i2 * 2 + c_) * 8192:(i2 * 2 + c_ + 1) * 8192].rearrange("p (j c) -> p j c", j=4) for c_ in range(2)]
                for i2 in range(2)]
        bufs_r = [[sub(BB_r, f"s2b{i2}{c_}") for c_ in range(2)] for i2 in range(2)]
        return bufs, bufs_r

    def fft_filters(o):
        for part in range(2):
            fft_s1(lambda part=part: hsd[part, :, o * D:(o + 1) * D], hsd_r)
            bufs, bufs_r = s2_bufs()
            for jq in range(8):
                i2 = jq % 2
                for c_ in range(2):
                    v = B1[c_].rearrange("(j q) r d -> (q r) j d", q=4)

                    def ld(e, v=v, dst=bufs[i2][c_], jq=jq):
                        return [e.dma_start(out=dst[:, q:q + 1, :], in_=v[:, jq * 4 + q:jq * 4 + q + 1, :]) for q in range(4)]
                    S.op("pool", ld, reads=[B1_r], writes=[bufs_r[i2][c_]], dma=4)
                bre, bim = bufs[i2]
                for j4 in range(4):
                    j = jq * 4 + j4
                    og, ogr = btt()
                    for db in range(4):
                        cs = slice(db * 512, (db + 1) * 512)
                        b, br = nb()
                        m0, m1 = (4, 7) if part == 0 else (5, 4)
                        S.op("pe", lambda e, b=b, j4=j4, cs=cs, bre=bre, bim=bim, m0=m0, m1=m1: (
                            e.matmul(b[:], fcs[:, m0, :], bre[:, j4, cs], start=True, stop=False),
                            e.matmul(b[:], fcs[:, m1, :], bim[:, j4, cs], start=False, stop=True))[-1],
                            reads=[bufs_r[i2][0], bufs_r[i2][1], fcs_r], writes=[br])
                        copy(evq(), og[:, cs], b[:], [br], [ogr])
                    S.dma("sp", G2[part, :, j, o * D:(o + 1) * D], og, reads=[ogr], writes=[G2_r])
            join(BB_r, [x for l in bufs_r for x in l])

    def fft_conv(o):
        if o == 0:
            vsrc, v_r = (lambda: c3[0:LS, 2 * D:3 * D]), c3_r
        else:
            vsrc, v_r = (lambda: z1[0:LS, :]), z1_r
        fft_s1(vsrc, v_r)
        bufs, bufs_r = s2_bufs()
        d1v = [D1[c_].rearrange("(j q) r d -> j (q r) d", q=4) for c_ in range(2)]

        def g_loads(j):
            (gc, gcr), (gs, gsr) = btt(), btt()
            S.dma("sp", gc, G2[0, :, j, o * D:(o + 1) * D], reads=[G2_r], writes=[gcr])
            S.dma("sp", gs, G2[1, :, j, o * D:(o + 1) * D], reads=[G2_r], writes=[gsr])
            return (gc, gcr), (gs, gsr)
        gnext = None
        for jq in range(8):
            i2 = jq % 2
            for c_ in range(2):
                v = B1[c_].rearrange("(j q) r d -> (q r) j d", q=4)

                def ld(e, v=v, dst=bufs[i2][c_], jq=jq):
                    return [e.dma_start(out=dst[:, q:q + 1, :], in_=v[:, jq * 4 + q:jq * 4 + q + 1, :]) for q in range(4)]
                S.op("pool", ld, reads=[B1_r], writes=[bufs_r[i2][c_]], dma=4)
            bre, bim = bufs[i2]
            for j4 in range(4):
                j = jq * 4 + j4
                if j == 0:
                    gnext = g_loads(0)
                (gc, gcr), (gs, gsr) = gnext
                if j + 1 < 32:
                    gnext = g_loads(j + 1)
                (ore, orr), (oim, oir) = btt(), btt()
                for db in range(4):
                    cs = slice(db * 512, (db + 1) * 512)
                    a, ar = nb()
                    b, br = nb()
                    S.op("pe", lambda e, a=a, j4=j4, cs=cs, bre=bre, bim=bim: (
                        e.matmul(a[:], fcs[:, 4, :], bre[:, j4, cs], start=True, stop=False),
                        e.matmul(a[:], fcs[:, 7, :], bim[:, j4, cs], start=False, stop=True))[-1],
                        reads=[bufs_r[i2][0], bufs_r[i2][1], fcs_r], writes=[ar])
                    S.op("pe", lambda e, b=b, j4=j4, cs=cs, bre=bre, bim=bim: (
                        e.matmul(b[:], fcs[:, 5, :], bre[:, j4, cs], start=True, stop=False),
                        e.matmul(b[:], fcs[:, 4, :], bim[:, j4, cs], start=False, stop=True))[-1],
                        reads=[bufs_r[i2][0], bufs_r[i2][1], fcs_r], writes=[br])
                    (t1, t1r), (t2, t2r), (t3, t3r), (t4, t4r) = sfp(), sfp(), sfp(), sfp()
                    S.op("dve", lambda e, a=a, t1=t1, gc=gc, cs=cs: e.tensor_tensor(out=t1, in0=a[:], in1=gc[:, cs], op=ALU.mult),
                         reads=[ar, gcr], writes=[t1r])
                    S.op("dve", lambda e, b=b, t2=t2, gs=gs, cs=cs: e.tensor_tensor(out=t2, in0=b[:], in1=gs[:, cs], op=ALU.mult),
                         reads=[br, gsr], writes=[t2r])
                    S.op("dve", lambda e, b=b, t3=t3, gc=gc, cs=cs: e.tensor_tensor(out=t3, in0=b[:], in1=gc[:, cs], op=ALU.mult),
                         reads=[br, gcr], writes=[t3r])
                    S.op("dve", lambda e, a=a, t4=t4, gs=gs, cs=cs: e.tensor_tensor(out=t4, in0=a[:], in1=gs[:, cs], op=ALU.mult),
                         reads=[ar, gsr], writes=[t4r])
                    (y0, y0r), (y1, y1r) = sbf(), sbf()
                    S.op("pool", lambda e, t1=t1, t2=t2, y0=y0: e.tensor_tensor(out=y0, in0=t1, in1=t2, op=ALU.add),
                         reads=[t1r, t2r], writes=[y0r])
                    S.op("pool", lambda e, t3=t3, t4=t4, y1=y1: e.tensor_tensor(out=y1, in0=t3, in1=t4, op=ALU.subtract),
                         reads=[t3r, t4r], writes=[y1r])
                    cb_, cbr = nb()
                    db_, dbr = nb()
                    S.op("pe", lambda e, cb_=cb_, y0=y0, y1=y1: (e.matmul(cb_[:], fcs[:, 4, :], y0, start=True, stop=False),
                                                                e.matmul(cb_[:], fcs[:, 5, :], y1, start=False, stop=True))[-1],
                         reads=[y0r, y1r, fcs_r], writes=[cbr])
                    S.op("pe", lambda e, db_=db_, y0=y0, y1=y1: (e.matmul(db_[:], fcs[:, 5, :], y0, start=True, stop=False),
                                                                e.matmul(db_[:], fcs[:, 6, :], y1, start=False, stop=True))[-1],
                         reads=[y0r, y1r, fcs_r], writes=[dbr])
                    twiddle_evac(cb_, cbr, db_, dbr, 1, j, ore[:, cs], orr, oim[:, cs], oir)
                S.dma("sp", d1v[0][j], ore, reads=[orr], writes=[D1_r])
                S.dma("sp", d1v[1][j], oim, reads=[oir], writes=[D1_r])
        join(BB_r, [x for l in bufs_r for x in l])
        dbufs = [[BB[:, (i2 * 2 + c_) * 8192:(i2 * 2 + c_ + 1) * 8192].rearrange("p (r c) -> p r c", r=4) for c_ in range(2)]
                 for i2 in range(2)]
        dbufs_r = [[sub(BB_r, f"d1b{i2}{c_}") for c_ in range(2)] for i2 in range(2)]
        S.dma("sp", WA[1][:, 0:4096].bitcast(F32), hy_bias[o, :].partition_broadcast(128), reads=[in_r], writes=[WA_r[1]])
        biasv = WA[1][:, 0:4096].bitcast(F32)
        c3v = c3[0:LS, :].rearrange("(a r) c -> r a c", r=32)
        z1v = z1[0:LS, :].rearrange("(a r) c -> r a c", r=32)
        zzv = zz[0:LS, :].rearrange("(a r) c -> r a c", r=32)
        for rq in range(8):
            i2 = rq % 2
            for c_ in range(2):
                def ld(e, c_=c_, dst=dbufs[i2][c_], rq=rq):
                    return [e.dma_start(out=dst[:, q:q + 1, :], in_=D1[c_, :, rq * 4 + q:rq * 4 + q + 1, :]) for q in range(4)]
                S.op("pool", ld, reads=[D1_r], writes=[dbufs_r[i2][c_]], dma=4)
            dre, dim_ = dbufs[i2]
            for r4 in range(4):
                r = rq * 4 + r4
                (vv, vvr), (gg, ggr) = btt(), btt()
                if o == 0:
                    S.dma("act", vv, c3v[r, :, 2 * D:3 * D], reads=[c3_r], writes=[vvr])
                    S.dma("act", gg, c3v[r, :, 0:D], reads=[c3_r], writes=[ggr])
                else:
                    S.dma("act", vv, z1v[r, :, :], reads=[z1_r], writes=[vvr])
                    S.dma("act", gg, c3v[r, :, D:2 * D], reads=[c3_r], writes=[ggr])
                for db in range(4):
                    cs = slice(db * 512, (db + 1) * 512)
                    y, yr = nb()
                    S.op("pe", lambda e, y=y, r4=r4, cs=cs, dre=dre, dim_=dim_: (
                        e.matmul(y[:], fcs[:, 2, :], dre[:, r4, cs], start=True, stop=False),
                        e.matmul(y[:], fcs[:, 3, :], dim_[:, r4, cs], start=False, stop=True))[-1],
                        reads=[dbufs_r[i2][0], dbufs_r[i2][1], fcs_r], writes=[yr])
                    t_, tr_ = sfp()
                    S.op("dve", lambda e, t_=t_, vv=vv, cs=cs: e.tensor_tensor(out=t_, in0=vv[:, cs], in1=biasv[:, cs], op=ALU.mult),
                         reads=[vvr, WA_r[1]], writes=[tr_])
                    S.op("dve", lambda e, y=y, t_=t_: e.tensor_tensor(out=t_, in0=y[:], in1=t_, op=ALU.add),
                         reads=[yr, tr_], writes=[tr_])
                    S.op("dve", lambda e, t_=t_, gg=gg, vv=vv, cs=cs: e.tensor_tensor(out=vv[:, cs], in0=t_, in1=gg[:, cs], op=ALU.mult),
                         reads=[tr_, ggr], writes=[vvr])
                dstv, dr = (z1v, z1_r) if o == 0 else (zzv, zz_r)
                S.dma("sp", dstv[r, :, :], vv, reads=[vvr], writes=[dr])
        join(BB_r, [x for l in dbufs_r for x in l])

    def hy_seq_fft():
        fft_consts()
        bt_open()
        for o in range(2):
            fft_filters(o)
        for o in range(2):
            fft_conv(o)
        bt_close()

    phase = [0]

    def P(fn, *a, **k):
        if phase[0] >= DBG_STOP:
            raise _Stop()
        fn(*a, **k)
        phase[0] += 1

    def post_v(ti, j0, b, br):
        si = rot("sb16", 4)
        if ti < 32:
            copy(evq(), SB16[si][:], b[:], [br], [SB16_r[si]])
        else:
            s2 = rot("stg", 6)
            copy("dve", STG[s2][:], b[:], [br], [STG_r[s2]])
            for hh in range(2):
                S.dma("sp", nv[(ti - 32) * 128:(ti - 31) * 128, hh * 256:(hh + 1) * 256], STG[s2][:, hh * 256:(hh + 1) * 256],
                      reads=[STG_r[s2]], writes=[Res("nv", True)])
            copy("act", SB16[si][:], STG[s2][:], [STG_r[s2]], [SB16_r[si]])
        S.dma("sp", vtm[ti * 128:(ti + 1) * 128, :], SB16[si][:], reads=[SB16_r[si]], writes=[vtm_r])

    def post_k(ti, j0, b, br):
        s2 = rot("stg", 6)
        copy(evq(), STG[s2][:], b[:], [br], [STG_r[s2]])
        for hh in range(2):
            S.dma("sp", nk[(ti - 32) * 128:(ti - 31) * 128, hh * 256:(hh + 1) * 256], STG[s2][:, hh * 256:(hh + 1) * 256],
                  reads=[STG_r[s2]], writes=[Res("nk", True)])

    def final_norm(xi_):
        for ti in range(36):
            xi = rot("xt", 2)
            S.dma("sp", XT[xi][:], xres[xi_][ti * 128:(ti + 1) * 128, :], reads=[xres_r[xi_]], writes=[XT_r[xi]])
            S.op("act", lambda e, xi=xi: e.activation(out=JK[:], in_=XT[xi][:], func=AF.Square, accum_out=sm[:, 0:1]),
                 reads=[XT_r[xi]], writes=[JK_r, sm_r])
            S.op("dve", lambda e: e.tensor_scalar(out=sm[:, 1:2], in0=sm[:, 0:1], scalar1=1.0 / D, scalar2=EPS,
                                                  op0=ALU.mult, op1=ALU.add), reads=[sm_r], writes=[sm_r])
            S.op("act", lambda e: e.activation(out=sm[:, 2:3], in_=sm[:, 1:2], func=AF.Sqrt), reads=[sm_r], writes=[sm_r])
            S.op("dve", lambda e: e.reciprocal(out=sm[:, 3:4], in_=sm[:, 2:3]), reads=[sm_r], writes=[sm_r])
            if ti == 0:
                S.dma("sp", ACC[:, 0:D], norm_final.partition_broadcast(128), reads=[in_r], writes=[ACC_r])
            S.op("dve", lambda e, xi=xi: e.scalar_tensor_tensor(out=XT[xi][:], in0=XT[xi][:], scalar=sm[:, 3:4],
                                                               in1=ACC[:, 0:D], op0=ALU.mult, op1=ALU.mult),
                 reads=[XT_r[xi], sm_r, ACC_r], writes=[XT_r[xi]])
            dst = ys[ti * 128:(ti + 1) * 128, :] if ti < 32 else yp[(ti - 32) * 128:(ti - 31) * 128, :]
            for hh in range(4):
                S.dma("sp", dst[:, hh * 512:(hh + 1) * 512], XT[xi][:, hh * 512:(hh + 1) * 512], reads=[XT_r[xi]],
                      writes=[Res("y", True)])

    if os.environ.get("MK_VAR", "") == "nvtest":
        S.op("dve", lambda e: e.memset(STG[0][:], 1.0), writes=[STG_r[0]])
        S.dma("sp", nv[0:128, :], STG[0][:], reads=[STG_r[0]], writes=[Res("nv", True)])
    try:
        P(mod_phase, 0)
        P(norm_T, x_src(None), 0)
        P(gemm_fm, w_qkv, 2560, hT, hT_r, 16, post_store_fm(qT, qT_r))
        P(gemm_tm, hTt, hTt_r, 16, w_qkv[:, 2560:3072], 512, post_v, tiled=True)
        P(gemm_tm, hTt, hTt_r, 16, w_qkv[:, 2048:2560], 512, post_k, tiles=range(32, 36), tiled=True)
        P(attention)
        P(gemm_tm, oT, oT_r, 16, w_o, D, post_residual(x_src(None), 0, xres[0], xres_r[0]), tiled=True)
        P(norm_T, x_src(0), 1)
        P(gemm_fm, ffn_w_up[0], 2 * DFF, hT, hT_r, 16, post_store_fm(abT, abT_r))
        P(ffn_act, 0)
        P(gemm_tm, gT, gT_r, 44, ffn_w_down[0], D, post_residual(x_src(0), 1, xres[1], xres_r[1]))
        P(mod_phase, 1)
        P(norm_T, x_src(1), 0)
        P(gemm_tm, hTt, hTt_r, 16, hy_w_in, 3 * D, post_ptm, tiled=True)
        P(hy_conv3)
        P(hy_filters, "S", LS, hsd, hsd_r)
        if os.environ.get("MK_DFT", "") == "big":
            P(hy_seq, "S", LS, 0, hsd, hsd_r, ghat, ghat_r, yhat, yhat_r)
        else:
            P(hy_seq_fft)
        P(hy_filters, "P", LP, hsdP, hsdP_r)
        P(hy_seq, "P", LP, 4096, hsdP, hsdP_r, ghatP, ghatP_r, yhatP, yhatP_r)
        P(hy_seq, "P", LP, 4352, hsdP, hsdP_r, ghatP, ghatP_r, yhatP, yhatP_r)
        P(norm_T, None, 0, plain_src=(zz, zz_r))
        P(gemm_tm, hTt, hTt_r, 16, hy_w_out, D, post_residual(x_src(1), 0, xres[2], xres_r[2]), tiled=True)
        P(norm_T, x_src(2), 1)
        P(gemm_fm, ffn_w_up[1], 2 * DFF, hT, hT_r, 16, post_store_fm(abT, abT_r))
        P(ffn_act, 1)
        P(gemm_tm, gT, gT_r, 44, ffn_w_down[1], D, post_residual(x_src(2), 1, xres[3], xres_r[3]))
        P(final_norm, 3)
    except _Stop:
        pass
    S.finish()
    S.emit()
    S.st.close()
    return nc


_NC = None


def kernel(x_prompt, x_sample, cache_k, cache_v, c, c_ctx, w_mod, b_mod, norm_mix, norm_ffn, norm_final,
           w_qkv, w_o, attn_sink, hy_w_in, hy_conv, hy_w_f1, hy_b_f1, hy_w_f2, hy_b_f2, hy_w_f3, hy_freq,
           hy_bias, hy_w_out, ffn_w_up, ffn_conv, ffn_w_down):
    global _NC
    f = lambda a: np.ascontiguousarray(np.asarray(a, dtype=np.float32))
    x_prompt, x_sample, cache_k, cache_v, c, c_ctx = map(f, (x_prompt, x_sample, cache_k, cache_v, c, c_ctx))
    cst = _consts()
    if _NC is None:
        _NC = build()
    nc = _NC
    smallv = np.concatenate([f(b_mod).reshape(2, 96, 128), f(norm_mix).reshape(2, 16, 128),
                             f(norm_ffn).reshape(2, 16, 128)], axis=1)
    fc = f(ffn_conv).reshape(2, 3, 44, 128).reshape(2, 132, 128)
    shared = {
        "w_mod": f(w_mod), "smallv": np.ascontiguousarray(smallv), "norm_final": f(norm_final),
        "w_qkv": f(w_qkv)[0], "w_o": f(w_o)[0], "sink": f(attn_sink)[0],
        "hy_w_in": f(hy_w_in)[0], "hy_conv": f(hy_conv)[0], "hy_w_f1": f(hy_w_f1)[0], "hy_w_f2": f(hy_w_f2)[0],
        "hy_w_f3": f(hy_w_f3)[0],
        "hy_small": np.ascontiguousarray(np.stack([f(hy_b_f1)[0], f(hy_b_f2)[0], f(hy_freq)[0]], axis=1)),
        "hy_bias": f(hy_bias)[0], "hy_w_out": f(hy_w_out)[0],
        "ffn_w_up": f(ffn_w_up), "ffn_convT": np.ascontiguousarray(fc), "ffn_w_down": f(ffn_w_down),
    }
    for k, v in cst.items():
        shared["c_" + k] = v
    in_maps = []
    for b in range(8):
        m = dict(shared)
        m["xs"] = x_sample[b]
        m["xp"] = x_prompt[2 * b:2 * b + 2].reshape(512, D)
        m["ck"] = cache_k[b, 0].reshape(256, 512)
        m["cv"] = cache_v[b, 0].reshape(256, 512)
        m["cvec"] = np.ascontiguousarray(np.concatenate([c[b].reshape(16, 128), c_ctx.reshape(16, 128)], axis=0))
        in_maps.append(m)
    res = run_bass_kernel_spmd(nc, in_maps, core_ids=list(range(8)))
    R = res.results
    y_prompt = np.concatenate([R[b]["yp"].reshape(2, 256, D) for b in range(8)], axis=0)
    y_sample = np.stack([R[b]["ys"] for b in range(8)], axis=0)
    nk = np.concatenate([R[b]["nk"].reshape(2, 1, 256, 4, 128) for b in range(8)], axis=0)
    nv = np.concatenate([R[b]["nv"].reshape(2, 1, 256, 4, 128) for b in range(8)], axis=0)
    return (y_prompt.astype(np.float32), y_sample.astype(np.float32), nk.astype(np.float32), nv.astype(np.float32))
```

```python
from contextlib import ExitStack
import math
import numpy as np
import ml_dtypes
import concourse.bass as bass
import concourse.mybir as mybir
from concourse.bass_utils import run_bass_kernel_spmd

F32 = mybir.dt.float32
BF16 = mybir.dt.bfloat16
AF = mybir.ActivationFunctionType
ALU = mybir.AluOpType

D = 2048
NT = 4608
LS = 4096
LP = 256
DFF = 5632
SEQS = [(0, 4096), (4096, 4352), (4352, 4608)]
EPS = 1e-6


class Res:
    __slots__ = ("name", "w", "r", "multi")

    def __init__(self, name, multi=False):
        self.name = name
        self.w = {}
        self.r = {}
        self.multi = multi


def _merge(d, tok):
    k, v = tok
    if v > d.get(k, 0):
        d[k] = v


class Sched:
    CE = ("pe", "act", "dve", "pool")
    ALLQ = ("pe", "act", "dve", "pool", "sp")

    def __init__(self, nc, n_dma_sems=32):
        self.nc = nc
        self.prog = {e: [] for e in self.ALLQ}
        self.cnt = {e: 0 for e in self.CE}
        self.sem = {}
        self.dma_i = 0
        self.nd = n_dma_sems
        self.dma_val = [0] * n_dma_sems
        self.nsw = 32
        self.sw_i = 0
        self.sw_val = [0] * self.nsw
        self.waited = {e: {} for e in self.ALLQ}
        self.st = ExitStack()

    def sb(self, name, shape, dtype=F32):
        return self.st.enter_context(self.nc.sbuf_tensor(name, list(shape), dtype))

    def ps(self, name, shape, dtype=F32):
        return self.st.enter_context(self.nc.psum_tensor(name, list(shape), dtype))

    def op(self, eng, fn, reads=(), writes=(), dma=0):
        waits = {}
        for r in reads:
            for k, v in r.w.items():
                if v > waits.get(k, 0):
                    waits[k] = v
        for w in writes:
            for k, v in w.r.items():
                if v > waits.get(k, 0):
                    waits[k] = v
            if not (w.multi and not w.r):
                for k, v in w.w.items():
                    if v > waits.get(k, 0):
                        waits[k] = v
        if (not dma) and eng == "pe":
            waits.pop(("E", "pe"), None)
        sems = None
        if dma and eng == "pool":
            tok = {}
            sems = []
            for _ in range(dma):
                idx = self.sw_i % self.nsw
                self.sw_i += 1
                prev = self.sw_val[idx]
                if prev > waits.get(("S", idx), 0):
                    waits[("S", idx)] = prev
                self.sw_val[idx] = prev + 16
                tok[("S", idx)] = prev + 16
                sems.append(("S", idx))
        elif dma:
            idx = self.dma_i % self.nd
            self.dma_i += 1
            prev = self.dma_val[idx]
            if prev > waits.get(("D", idx), 0):
                waits[("D", idx)] = prev
            self.dma_val[idx] = prev + 16 * dma
            tok = {("D", idx): prev + 16 * dma}
            sems = [("D", idx)] * dma
        else:
            self.cnt[eng] += 1
            tok = {("E", eng): self.cnt[eng]}
        wl = []
        wd = self.waited[eng]
        for key, val in waits.items():
            if val <= 0 or wd.get(key, 0) >= val:
                continue
            wd[key] = val
            wl.append((key, val))
        self.prog[eng].append((wl, fn, tok, sems))
        for r in reads:
            for k, v in tok.items():
                if v > r.r.get(k, 0):
                    r.r[k] = v
        for w in writes:
            if w.multi and not w.r:
                for k, v in tok.items():
                    if v > w.w.get(k, 0):
                        w.w[k] = v
            else:
                w.w = dict(tok)
                w.r = {}
        return tok

    def dma(self, q, out, in_, reads=(), writes=()):
        return self.op(q, lambda e: [e.dma_start(out=out, in_=in_)], reads=reads, writes=writes, dma=1)

    def finish(self):
        wl = []
        for i in range(self.nd):
            if self.dma_val[i] > 0:
                wl.append((("D", i), self.dma_val[i]))
        for i in range(self.nsw):
            if self.sw_val[i] > 0:
                wl.append((("S", i), self.sw_val[i]))
        for e in self.CE:
            if self.cnt[e] > 0:
                wl.append((("E", e), self.cnt[e]))
        self.prog["sp"].append((wl, None, None, None))

    def emit(self):
        nc = self.nc
        st = self.st
        for e in self.CE:
            self.sem[("E", e)] = st.enter_context(nc.semaphore(f"s_{e}"))
        for i in range(self.nd):
            self.sem[("D", i)] = st.enter_context(nc.semaphore(f"d_{i}"))
        for i in range(self.nsw):
            self.sem[("S", i)] = st.enter_context(nc.semaphore(f"sw_{i}"))
        block = st.enter_context(nc.Block())
        sched = self

        def mk(engname):
            def body(e):
                for wl, fn, tok, sems in sched.prog[engname]:
                    for key, val in wl:
                        e.wait_ge(sched.sem[key], val)
                    if fn is None:
                        continue
                    ins = fn(e)
                    if sems is not None:
                        assert len(ins) == len(sems), (len(ins), len(sems))
                        for i, sk in zip(ins, sems):
                            i.then_inc(sched.sem[sk], 16)
                    else:
                        if isinstance(ins, (list, tuple)):
                            ins = ins[-1]
                        ins.then_inc(sched.sem[("E", engname)], 1)
            return body

        block.sync(mk("sp"))
        block.tensor(mk("pe"))
        block.scalar(mk("act"))
        block.vector(mk("dve"))
        block.gpsimd(mk("pool"))


_CONST = None


def _consts():
    global _CONST
    if _CONST is not None:
        return _CONST
    bf = ml_dtypes.bfloat16
    c = {}
    c["ident_b"] = np.eye(128, dtype=np.float32).astype(bf)
    c["ident_f"] = np.eye(128, dtype=np.float32)
    t = np.arange(LS)
    row = (t // 64).astype(np.float64)
    col = (t % 64).astype(np.float64)
    inv = 10000.0 ** (-np.arange(32, dtype=np.float64) / 32)
    C = np.ones((128, NT), np.float64)
    Sg = np.zeros((128, NT), np.float64)
    perm = np.zeros((128, 128), np.float32)
    for d in range(128):
        half, e = d // 64, d % 64
        j, first = e % 32, e < 32
        ang = (row if half == 0 else col) * inv[j]
        C[d, :LS] = np.cos(ang)
        Sg[d, :LS] = -np.sin(ang) if first else np.sin(ang)
        perm[d + 32 if first else d - 32, d] = 1.0
    c["ropeC"] = C.astype(np.float32).astype(bf)
    c["ropeS"] = Sg.astype(np.float32).astype(bf)
    c["perm"] = perm.astype(bf)
    kl = np.arange(128)[:, None]
    ql = np.arange(128)[None, :]
    m = np.stack([np.tile((kl >= ql), (1, 4)), np.tile((kl <= ql), (1, 4))], axis=1)
    c["bmask"] = m.astype(np.float32).astype(bf)
    for nm, L in (("S", LS), ("P", LP)):
        N = 2 * L
        tt = np.arange(L, dtype=np.float64)[:, None]
        kk = np.arange(L, dtype=np.float64)[None, :]
        th = 2.0 * np.pi * ((tt * (kk + 0.5)) % N) / N
        fwd = np.concatenate([np.cos(th), np.sin(th)], axis=1)
        CCf = L // 128
        inv = fwd.T * (2.0 / N)
        c["fwd" + nm] = np.ascontiguousarray(
            fwd.reshape(CCf, 128, 2 * CCf, 128).transpose(2, 1, 0, 3)).astype(np.float32).astype(bf)
        c["inv" + nm] = np.ascontiguousarray(
            inv.reshape(2 * CCf, 128, CCf, 128).transpose(2, 1, 0, 3)).astype(np.float32).astype(bf)
        tl = np.linspace(0.0, 1.0, L, dtype=np.float32)[:, None]
        w = (2.0 * np.pi * np.arange(L, dtype=np.float32)[:, None] / L).astype(np.float32)
        f = np.linspace(1e-4, 15, 16, dtype=np.float32)[None, :]
        feat = np.concatenate([tl, np.cos(f * w), -np.sin(f * w)], axis=-1).astype(np.float32)
        c["featT" + nm] = np.ascontiguousarray(feat.T)
        deltas = np.linspace(math.log(1e-2) / 1.5, math.log(1e-2) / 0.3, D, dtype=np.float32)
        dec = np.exp(-tl * np.abs(deltas)[None, :]).astype(np.float32)
        decb = dec.copy()
        decb[0, :] = 0.0
        c["decf" + nm] = dec
        c["decb" + nm] = decb
    N = 2 * LS
    a_ = np.arange(128, dtype=np.float64)[:, None]
    k1 = np.arange(128, dtype=np.float64)[None, :]
    phi = 2.0 * np.pi * a_ * (k1 + 0.5) / 256.0
    c["f1c"] = np.cos(phi).astype(np.float32).astype(bf)
    c["f1s"] = np.sin(phi).astype(np.float32).astype(bf)
    c["i1c"] = (np.cos(phi).T * (2.0 / N)).astype(np.float32).astype(bf)
    c["i1s"] = (-np.sin(phi).T * (2.0 / N)).astype(np.float32).astype(bf)
    r_ = np.arange(32, dtype=np.float64)[None, :]
    psi = 2.0 * np.pi * r_ * (np.arange(128, dtype=np.float64)[:, None] + 0.5) / N
    c["tw1"] = np.concatenate([np.cos(psi), np.sin(psi)], axis=1).astype(np.float32)
    chi = 2.0 * np.pi * np.outer(np.arange(32), np.arange(32)) / 32.0
    cbm = np.kron(np.eye(4), np.cos(chi))
    sbm = np.kron(np.eye(4), np.sin(chi))
    c["cbm"] = cbm.astype(np.float32).astype(bf)
    c["sbm"] = sbm.astype(np.float32).astype(bf)
    c["ncbm"] = (-cbm).astype(np.float32).astype(bf)
    c["nsbm"] = (-sbm).astype(np.float32).astype(bf)
    q_ = (np.arange(128) // 32).astype(np.float64)[:, None]
    rr2 = (np.arange(128) % 32).astype(np.float64)[:, None]
    j_ = np.arange(32, dtype=np.float64)[None, :]
    psi2 = 2.0 * np.pi * rr2 * (4.0 * j_ + q_ + 0.5) / N
    c["tw2"] = np.concatenate([np.cos(psi2), np.sin(psi2)], axis=1).astype(np.float32)
    _CONST = c
    return c


import os
DBG_STOP = int(os.environ.get("MK_STOP", "999"))
DBG_OUT = [x for x in os.environ.get("MK_OUT", "").split(",") if x]


class _Stop(Exception):
    pass


def build():
    nc = bass.Bass("TRN2", target_bir_lowering=False)
    S = Sched(nc)
    cst = _consts()
    IN = {}

    def din(name, shape, dt=F32):
        IN[name] = nc.dram_tensor(name, list(shape), dt, kind="ExternalInput").ap()
        return IN[name]

    def dscr(name, shape, dt):
        return nc.dram_tensor(name, list(shape), dt, kind=("ExternalOutput" if name in DBG_OUT else "Internal")).ap()

    xs = din("xs", [LS, D]); xp = din("xp", [512, D])
    ck = din("ck", [256, 512]); cv = din("cv", [256, 512])
    cvec = din("cvec", [32, 128])
    w_mod = din("w_mod", [2, D, 6 * D]); smallv = din("smallv", [2, 128, 128])
    norm_final = din("norm_final", [D])
    w_qkv = din("w_qkv", [D, 3072]); w_o = din("w_o", [D, D]); sink = din("sink", [16])
    hy_w_in = din("hy_w_in", [D, 3 * D]); hy_conv = din("hy_conv", [3, 3 * D])
    hy_w_f1 = din("hy_w_f1", [33, 64]); hy_w_f2 = din("hy_w_f2", [64, 64]); hy_w_f3 = din("hy_w_f3", [64, 4 * D])
    hy_small = din("hy_small", [64, 3])
    hy_bias = din("hy_bias", [2, D]); hy_w_out = din("hy_w_out", [D, D])
    ffn_w_up = din("ffn_w_up", [2, D, 2 * DFF]); ffn_convT = din("ffn_convT", [2, 132, 128])
    ffn_w_down = din("ffn_w_down", [2, DFF, D])
    for k, v in cst.items():
        din("c_" + k, v.shape, BF16 if v.dtype != np.float32 else F32)

    yp = nc.dram_tensor("yp", [512, D], F32, kind="ExternalOutput").ap()
    ys = nc.dram_tensor("ys", [LS, D], F32, kind="ExternalOutput").ap()
    nk = nc.dram_tensor("nk", [512, 512], F32, kind="ExternalOutput").ap()
    nv = nc.dram_tensor("nv", [512, 512], F32, kind="ExternalOutput").ap()

    hT = dscr("hT", [D, NT], BF16); hT_r = Res("hT", True)
    qT = dscr("qT", [2560, NT], BF16); qT_r = Res("qT", True)
    vtm = dscr("vtm", [NT, 512], BF16); vtm_r = Res("vtm", True)
    oT = dscr("oT", [36, 128, 16, 128], BF16); oT_r = Res("oT", True)
    hTt = dscr("hTt", [36, 128, 16, 128], BF16); hTt_r = Res("hTt", True)
    xres = [dscr(f"xres{i}", [NT, D], F32) for i in range(4)]
    xres_r = [Res(f"xres{i}", True) for i in range(4)]
    abT = dscr("abT", [2 * DFF, NT], BF16); abT_r = Res("abT", True)
    gT = dscr("gT", [DFF, NT], BF16); gT_r = Res("gT", True)
    gate_scr = dscr("gate_scr", [4, 128, D], F32); gate_r = Res("gate", True)
    ptm = dscr("ptm", [NT, 3 * D], BF16); ptm_r = Res("ptm", True)
    c3 = dscr("c3", [NT, 3 * D], BF16); c3_r = Res("c3", True)
    hsd = dscr("hsd", [2, LS, 2 * D], BF16); hsd_r = Res("hsd", True)
    ghat = dscr("ghat", [2, LS, 2 * D], BF16); ghat_r = Res("ghat", True)
    yhat = dscr("yhat", [2, 2 * LS, D], BF16); yhat_r = Res("yhat", True)
    z1 = dscr("z1", [NT, D], BF16); z1_r = Res("z1", True)
    zz = dscr("zz", [NT, D], BF16); zz_r = Res("zz", True)
    in_r = Res("inputs")

    BB = S.sb("BB", [128, 35200], BF16); BB_r = Res("BB")
    WA = [S.sb(f"WA{i}", [128, 12288], BF16) for i in range(2)]; WA_r = [Res(f"WA{i}") for i in range(2)]
    XT = [S.sb(f"XT{i}", [128, D], F32) for i in range(2)]; XT_r = [Res(f"XT{i}") for i in range(2)]
    XN = S.sb("XN", [128, D], BF16); XN_r = Res("XN")
    JK = S.sb("JK", [128, D], BF16); JK_r = Res("JK")
    HTS = S.sb("HTS", [128, 16, 256], BF16); HTS_r = Res("HTS")
    STG = [S.sb(f"STG{i}", [128, 512], F32) for i in range(6)]; STG_r = [Res(f"STG{i}") for i in range(6)]
    STX = [S.sb(f"STX{i}", [128, 512], F32) for i in range(4)]; STX_r = [Res(f"STX{i}") for i in range(4)]
    SB16 = [S.sb(f"SB16{i}", [128, 512], BF16) for i in range(4)]; SB16_r = [Res(f"SB16{i}") for i in range(4)]
    ACC = S.sb("ACC", [128, NT], F32); ACC_r = Res("ACC")
    idb = S.sb("idb", [128, 128], BF16); idf = S.sb("idf", [128, 128], F32); perm = S.sb("perm", [128, 128], BF16)
    cst_r = Res("consts")
    colsT = S.sb("colsT", [128, 128], F32); colsT_r = Res("colsT")
    cT = S.sb("cT", [128, 32], BF16); cT_r = Res("cT")
    cbc = HTS[:].rearrange("p c t -> p (c t)").rearrange("p (a b) -> p a b", a=32); cbc_r = HTS_r
    modc = S.sb("modc", [128, 2, 96], F32); modc_r = Res("modc")
    gsc = S.sb("gsc", [128, 2, 2, 16], F32); gsc_r = Res("gsc")
    sm = S.sb("sm", [128, 8], F32); sm_r = Res("sm")
    esink = S.sb("esink", [128, 16], F32); esink_r = Res("esink")
    fcv = S.sb("fcv", [128, 3, 44], F32); fcv_r = Res("fcv")
    stg_s = S.sb("stg_s", [128, 128], F32); stg_r = Res("stg_s")
    banks = [S.ps(f"bank{i}", [128, 512], F32) for i in range(8)]
    bank_r = [Res(f"bank{i}") for i in range(8)]
    bi = [0]

    def sub(parent, name):
        r = Res(name)
        r.w = dict(parent.w)
        r.r = dict(parent.r)
        return r

    def join(parent, subs):
        for rr_ in subs:
            for k, v in list(rr_.w.items()) + list(rr_.r.items()):
                if v > parent.r.get(k, 0):
                    parent.r[k] = v

    nbn = [8]

    def nb():
        i = bi[0] % nbn[0]
        bi[0] += 1
        return banks[i], bank_r[i]

    rr = {"stg": 0, "sb16": 0, "wa": 0, "xt": 0, "ev": 0}

    def rot(name, n):
        i = rr[name] % n
        rr[name] += 1
        return i

    def evq():
        return "act" if rot("ev", 2) == 0 else "dve"

    def copy(q, out, in_, reads, writes):
        if q == "act":
            return S.op("act", lambda e: e.activation(out=out, in_=in_, func=AF.Copy), reads=reads, writes=writes)
        return S.op(q, lambda e: e.tensor_copy(out=out, in_=in_), reads=reads, writes=writes)

    S.dma("sp", idb[:], IN["c_ident_b"], writes=[cst_r])
    S.dma("sp", idf[:], IN["c_ident_f"], writes=[cst_r])
    S.dma("sp", perm[:], IN["c_perm"], writes=[cst_r])

    def x_src(layer_in):
        def f(ti):
            if layer_in is None:
                return (xs[ti * 128:(ti + 1) * 128, :] if ti < 32 else xp[(ti - 32) * 128:(ti - 31) * 128, :]), in_r
            return xres[layer_in][ti * 128:(ti + 1) * 128, :], xres_r[layer_in]
        return f

    def mod_phase(i):
        S.dma("sp", stg_s[:], smallv[i], reads=[in_r], writes=[stg_r])
        b, br = nb()
        S.op("pe", lambda e: e.transpose(b[:, 0:128], stg_s[:], idf[:]), reads=[stg_r, cst_r], writes=[br])
        copy("act", colsT[:], b[:, 0:128], [br], [colsT_r])
        S.dma("sp", stg_s[0:32, :], cvec, reads=[in_r], writes=[stg_r])
        S.op("act", lambda e: e.activation(out=stg_s[0:32, :], in_=stg_s[0:32, :], func=AF.Silu),
             reads=[stg_r], writes=[stg_r])
        b2, b2r = nb()
        S.op("pe", lambda e: e.transpose(b2[:, 0:32], stg_s[0:32, :], idf[0:32, 0:32]), reads=[stg_r, cst_r], writes=[b2r])
        copy("act", cT[:], b2[:, 0:32], [b2r], [cT_r])

        def mkbc(e):
            last = None
            for c in range(32):
                last = e.tensor_copy(out=cbc[:, c, :], in_=cT[:, c:c + 1].to_broadcast([128, 128]))
            return last
        S.op("dve", mkbc, reads=[cT_r], writes=[cbc_r])
        mb, mbr = banks[7], bank_r[7]
        nbn[0] = 7
        for blk in range(24):
            wi = rot("wa", 2)
            wv = WA[wi][:, 0:8192].rearrange("p (c w) -> p c w", c=16)

            def ldw(e, blk=blk, wv=wv):
                return [e.dma_start(out=wv[:, c4 * 4:(c4 + 1) * 4, :],
                                    in_=w_mod[i, c4 * 512:(c4 + 1) * 512, blk * 512:(blk + 1) * 512]
                                    .rearrange("(c p) n -> p c n", p=128)) for c4 in range(4)]
            S.op("pool", ldw, reads=[in_r], writes=[WA_r[wi]], dma=4)

            def mm(e, blk=blk, wv=wv):
                last = None
                for n in range(4):
                    ch = blk * 4 + n
                    for k in range(16):
                        last = e.matmul(mb[:, ch * 2:ch * 2 + 2], wv[:, k, n * 128:(n + 1) * 128],
                                        cT[:, k:32:16], start=(k == 0), stop=(k == 15))
                return last
            S.op("pe", mm, reads=[WA_r[wi], cT_r], writes=[mbr])
            which = {2: 0, 5: 1}.get(blk // 4)
            if which is not None:
                cb = (blk % 4) * 512
                for r in range(2):
                    gb, gbr = nb()

                    def mg(e, wv=wv, r=r, gb=gb):
                        last = None
                        for k in range(16):
                            last = e.matmul(gb[:], cbc[:, r * 16 + k, :], wv[:, k, :], start=(k == 0), stop=(k == 15))
                        return last
                    S.op("pe", mg, reads=[WA_r[wi], cbc_r], writes=[gbr])
                    si = rot("stg", 6)
                    s2 = rot("stg", 6)
                    S.dma("sp", STG[s2][:], smallv[i, (blk * 4):(blk * 4 + 4), :].rearrange("a b -> (a b)")
                          .partition_broadcast(128), reads=[in_r], writes=[STG_r[s2]])
                    S.op("dve", lambda e, gb=gb, si=si, s2=s2: e.tensor_tensor(
                        out=STG[si][:], in0=gb[:], in1=STG[s2][:], op=ALU.add),
                        reads=[gbr, STG_r[s2]], writes=[STG_r[si]])
                    S.dma("sp", gate_scr[r * 2 + which, :, cb:cb + 512], STG[si][:],
                          reads=[STG_r[si]], writes=[gate_r])
        nbn[0] = 8
        for r in range(2):
            S.op("dve", lambda e, r=r: e.tensor_tensor(out=modc[:, r, :], in0=mb[:, r:192:2], in1=colsT[:, 0:96],
                                                       op=ALU.add), reads=[mbr, colsT_r], writes=[modc_r])
        for r in range(2):
            for wh in range(2):
                S.op("dve", lambda e, r=r, wh=wh: e.scalar_tensor_tensor(
                    out=gsc[:, r, wh, :], in0=modc[:, r, wh * 48 + 16:wh * 48 + 32], scalar=1.0,
                    in1=colsT[:, 96 + wh * 16:112 + wh * 16], op0=ALU.add, op1=ALU.mult),
                    reads=[modc_r, colsT_r], writes=[gsc_r])

    def norm_T(src, wh, plain_src=None):
        for ti in range(36):
            r = 0 if ti < 32 else 1
            if plain_src is None:
                xi = rot("xt", 2)
                ap, res = src(ti)
                S.dma("sp", XT[xi][:], ap, reads=[res], writes=[XT_r[xi]])
                S.op("act", lambda e, xi=xi: e.activation(out=JK[:], in_=XT[xi][:], func=AF.Square,
                                                          accum_out=sm[:, 0:1]),
                     reads=[XT_r[xi]], writes=[JK_r, sm_r])
                S.op("dve", lambda e: e.tensor_scalar(out=sm[:, 1:2], in0=sm[:, 0:1], scalar1=1.0 / D, scalar2=EPS,
                                                      op0=ALU.mult, op1=ALU.add), reads=[sm_r], writes=[sm_r])
                S.op("act", lambda e: e.activation(out=sm[:, 2:3], in_=sm[:, 1:2], func=AF.Sqrt), reads=[sm_r], writes=[sm_r])
                S.op("dve", lambda e: e.reciprocal(out=sm[:, 3:4], in_=sm[:, 2:3]), reads=[sm_r], writes=[sm_r])
                S.op("dve", lambda e, xi=xi: e.tensor_scalar(out=XN[:], in0=XT[xi][:], scalar1=sm[:, 3:4], scalar2=None,
                                                            op0=ALU.mult), reads=[XT_r[xi], sm_r], writes=[XN_r])
            else:
                pa, pr = plain_src
                S.dma("sp", XN[:], pa[ti * 128:(ti + 1) * 128, :], reads=[pr], writes=[XN_r])
            half = ti % 2
            for g in range(2):
                b, br = nb()
                bv = b[:].bitcast(BF16)

                def tp(e, g=g, bv=bv):
                    last = None
                    for j in range(8):
                        c = g * 8 + j
                        last = e.transpose(bv[:, j * 128:(j + 1) * 128], XN[:, c * 128:(c + 1) * 128], idb[:])
                    return last
                S.op("pe", tp, reads=[XN_r, cst_r], writes=[br])

                def ev(e, g=g, bv=bv, r=r, half=half):
                    last = None
                    for j in range(8):
                        c = g * 8 + j
                        o = HTS[:, c, half * 128:(half + 1) * 128]
                        if plain_src is None:
                            last = e.activation(out=o, in_=bv[:, j * 128:(j + 1) * 128], func=AF.Identity,
                                                scale=gsc[:, r, wh, c:c + 1],
                                                bias=modc[:, r, wh * 48 + c:wh * 48 + c + 1])
                        else:
                            last = e.activation(out=o, in_=bv[:, j * 128:(j + 1) * 128], func=AF.Copy)
                    return last
                S.op("act", ev, reads=[br, gsc_r, modc_r], writes=[HTS_r])
            S.dma("sp", hTt[ti], HTS[:, :, half * 128:(half + 1) * 128], reads=[HTS_r], writes=[hTt_r])
            if half == 1:
                t0 = (ti - 1) * 128

                def st(e, t0=t0):
                    return [e.dma_start(out=hT[q * 512:(q + 1) * 512, t0:t0 + 256].rearrange("(c p) t -> p c t", p=128),
                                        in_=HTS[:, q * 4:(q + 1) * 4, :]) for q in range(4)]
                S.op("sp", st, reads=[HTS_r], writes=[hT_r], dma=4)

    BLKS = [(0, 2048), (2048, 4096), (4096, 4608)]

    def gemm_fm(W, ncols, src, src_r, KC, post):
        for (c0, c1) in BLKS:
            wd = c1 - c0
            bv = BB[:, 0:KC * wd].rearrange("p (c w) -> p c w", c=KC)

            def ldb(e, bv=bv, c0=c0, c1=c1):
                return [e.dma_start(out=bv[:, q * 4:(q + 1) * 4, :],
                                    in_=src[q * 512:(q + 1) * 512, c0:c1].rearrange("(c p) t -> p c t", p=128))
                        for q in range(KC // 4)]
            S.op("sp", ldb, reads=[src_r], writes=[BB_r], dma=KC // 4)
            for cg in range(ncols // 256):
                wi = rot("wa", 2)
                wv = WA[wi][:, 0:KC * 256].rearrange("p (c w) -> p c w", c=KC)

                def ldw(e, wv=wv, cg=cg):
                    return [e.dma_start(out=wv[:, q * 4:(q + 1) * 4, :],
                                        in_=W[q * 512:(q + 1) * 512, cg * 256:(cg + 1) * 256]
                                        .rearrange("(c p) n -> p c n", p=128)) for q in range(KC // 4)]
                S.op("pool", ldw, reads=[in_r], writes=[WA_r[wi]], dma=KC // 4)
                for n in range(2):
                    ci = cg * 2 + n
                    for p0 in range(0, wd, 512):
                        b, br = nb()

                        def mm(e, wv=wv, bv=bv, n=n, p0=p0, b=b):
                            last = None
                            for k in range(KC):
                                last = e.matmul(b[:], wv[:, k, n * 128:(n + 1) * 128], bv[:, k, p0:p0 + 512],
                                                start=(k == 0), stop=(k == KC - 1))
                            return last
                        S.op("pe", mm, reads=[WA_r[wi], BB_r], writes=[br])
                        post(ci, c0 + p0, b, br)

    def post_store_fm(dst, dst_r, rowoff=0):
        def post(ci, t0, b, br):
            si = rot("sb16", 4)
            copy(evq(), SB16[si][:], b[:], [br], [SB16_r[si]])
            S.dma("sp", dst[rowoff + ci * 128:rowoff + (ci + 1) * 128, t0:t0 + 512], SB16[si][:],
                  reads=[SB16_r[si]], writes=[dst_r])
        return post

    def gemm_tm(src, src_r, KC, W, ncols, post, tiles=range(36), tiled=False):
        tiles = list(tiles)
        for jb in range(ncols // 512):
            bv = BB[:, 0:KC * 512].rearrange("p (c w) -> p c w", c=KC)

            def ldb(e, bv=bv, jb=jb):
                return [e.dma_start(out=bv[:, q * 4:(q + 1) * 4, :],
                                    in_=W[q * 512:(q + 1) * 512, jb * 512:(jb + 1) * 512]
                                    .rearrange("(c p) n -> p c n", p=128)) for q in range(KC // 4)]
            S.op("pool", ldb, reads=[in_r], writes=[BB_r], dma=KC // 4)
            for t2 in range(0, len(tiles), 2):
                pair = tiles[t2:t2 + 2]
                wi = rot("wa", 2)
                wv = WA[wi][:, 0:KC * 256].rearrange("p (c w) -> p c w", c=KC)
                t0 = pair[0] * 128

                def lda(e, wv=wv, t0=t0, n=len(pair)):
                    return [e.dma_start(out=wv[:, q * 4:(q + 1) * 4, 0:n * 128],
                                        in_=src[q * 512:(q + 1) * 512, t0:t0 + n * 128]
                                        .rearrange("(c p) t -> p c t", p=128)) for q in range(KC // 4)]
                if tiled:
                    wts_ = [WA[wi][:, pi_ * KC * 128:(pi_ + 1) * KC * 128].rearrange("p (c t) -> p c t", c=KC) for pi_ in range(2)]

                    def lda(e, wts_=wts_, pair=pair):
                        return [e.dma_start(out=wts_[pi_], in_=src[ti_]) for pi_, ti_ in enumerate(pair)]
                    S.op("pool", lda, reads=[src_r], writes=[WA_r[wi]], dma=len(pair))
                else:
                    S.op("pool", lda, reads=[src_r], writes=[WA_r[wi]], dma=KC // 4)
                for pi, ti in enumerate(pair):
                    b, br = nb()

                    def mm(e, wv=wv, bv=bv, pi=pi, b=b, wt_=(wts_[pi] if tiled else None)):
                        last = None
                        for k in range(KC):
                            lh = wt_[:, k, :] if wt_ is not None else wv[:, k, pi * 128:(pi + 1) * 128]
                            last = e.matmul(b[:], lh, bv[:, k, :], start=(k == 0), stop=(k == KC - 1))
                        return last
                    S.op("pe", mm, reads=[WA_r[wi], BB_r], writes=[br])
                    post(ti, jb * 512, b, br)

    def post_residual(xin, which, xout, xout_r, final=False):
        def post(ti, j0, b, br):
            r = 0 if ti < 32 else 1
            g = rot("stg", 6)
            S.dma("sp", STG[g][:], gate_scr[r * 2 + which, :, j0:j0 + 512], reads=[gate_r], writes=[STG_r[g]])
            xi = rot("stg", 6)
            ap, res = xin(ti)
            S.dma("sp", STG[xi][:], ap[:, j0:j0 + 512], reads=[res], writes=[STG_r[xi]])
            S.op("dve", lambda e: e.tensor_tensor(out=STG[g][:], in0=b[:], in1=STG[g][:], op=ALU.mult),
                 reads=[br, STG_r[g]], writes=[STG_r[g]])
            S.op("dve", lambda e: e.tensor_tensor(out=STG[xi][:], in0=STG[xi][:], in1=STG[g][:], op=ALU.add),
                 reads=[STG_r[g], STG_r[xi]], writes=[STG_r[xi]])
            S.dma("sp", xout[ti * 128:(ti + 1) * 128, j0:j0 + 512], STG[xi][:], reads=[STG_r[xi]], writes=[xout_r])
        return post

    def attention():
        KT = BB[:, 0:4 * 4352].rearrange("p (h t) -> p h t", h=4)
        VV = BB[:, 17408:17408 + 34 * 512].rearrange("p (b c) -> p b c", b=34)
        PKT = WA[0][:, 0:2048].rearrange("p (h t) -> p h t", h=4)
        PVV = WA[0][:, 2048:4096].rearrange("p (b c) -> p b c", b=4)
        RC = WA[1][:, 0:4608]
        RS = WA[1][:, 4608:9216]
        MK = WA[1][:, 9216:10240].rearrange("p (a q) -> p a q", a=2)
        S.dma("sp", RC, IN["c_ropeC"], writes=[WA_r[1]])
        S.dma("sp", RS, IN["c_ropeS"], writes=[WA_r[1]])
        S.dma("sp", MK, IN["c_bmask"], writes=[WA_r[1]])
        S.dma("sp", esink[:], sink.partition_broadcast(128), reads=[in_r], writes=[esink_r])
        S.op("act", lambda e: e.activation(out=esink[:], in_=esink[:], func=AF.Exp), reads=[esink_r], writes=[esink_r])
        for h in range(4):
            S.dma("sp", KT[:, h, 0:4096], qT[2048 + h * 128:2048 + (h + 1) * 128, 0:4096], reads=[qT_r], writes=[BB_r])
            S.dma("sp", PKT[:, h, :], qT[2048 + h * 128:2048 + (h + 1) * 128, 4096:4608], reads=[qT_r], writes=[WA_r[0]])
        for h in range(4):
            for p0 in range(0, 4096, 512):
                b, br = nb()
                S.op("pe", lambda e, b=b, h=h, p0=p0: e.matmul(b[:], perm[:], KT[:, h, p0:p0 + 512], start=True, stop=True),
                     reads=[BB_r, cst_r], writes=[br])
                si = rot("stg", 6)
                S.op("dve", lambda e, b=b, si=si, p0=p0: e.tensor_tensor(out=STG[si][:], in0=b[:], in1=RS[:, p0:p0 + 512],
                                                                      op=ALU.mult), reads=[br, WA_r[1]], writes=[STG_r[si]])
                s2 = rot("stg", 6)
                S.op("pool", lambda e, s2=s2, h=h, p0=p0: e.tensor_tensor(out=STG[s2][:], in0=KT[:, h, p0:p0 + 512],
                                                                         in1=RC[:, p0:p0 + 512], op=ALU.mult),
                     reads=[BB_r, WA_r[1]], writes=[STG_r[s2]])
                S.op("dve", lambda e, si=si, s2=s2, h=h, p0=p0: e.tensor_tensor(out=KT[:, h, p0:p0 + 512], in0=STG[si][:],
                                                                               in1=STG[s2][:], op=ALU.add),
                     reads=[STG_r[si], STG_r[s2]], writes=[BB_r])
        for blk in range(2):
            xi = rot("xt", 2)
            S.dma("sp", XT[xi][:, 0:512], ck[blk * 128:(blk + 1) * 128, :], reads=[in_r], writes=[XT_r[xi]])
            S.dma("sp", XT[xi][:, 512:1024], cv[blk * 128:(blk + 1) * 128, :], reads=[in_r], writes=[XT_r[xi]])
            copy("dve", VV[:, 32 + blk, :], XT[xi][:, 512:1024], [XT_r[xi]], [BB_r])
            b, br = nb()

            def tp(e, b=b, xi=xi):
                last = None
                for h in range(4):
                    last = e.transpose(b[:, h * 128:(h + 1) * 128], XT[xi][:, h * 128:(h + 1) * 128], idf[:])
                return last
            S.op("pe", tp, reads=[XT_r[xi], cst_r], writes=[br])
            for h in range(4):
                copy("act", KT[:, h, 4096 + blk * 128:4096 + (blk + 1) * 128], b[:, h * 128:(h + 1) * 128], [br], [BB_r])
        S.dma("sp", VV[:, 0:32, :], vtm[0:4096, :].rearrange("(b p) c -> p b c", p=128), reads=[vtm_r], writes=[BB_r])
        S.dma("sp", PVV, vtm[4096:4608, :].rearrange("(b p) c -> p b c", p=128), reads=[vtm_r], writes=[WA_r[0]])
        QBs = [WA[0][:, 4096 + i * 512:4608 + i * 512].rearrange("p (g q) -> p g q", g=4) for i in range(2)]
        QB_r = [sub(WA_r[0], "QB0"), sub(WA_r[0], "QB1")]
        ones = WA[0][:, 5120:5248]
        ones_r = sub(WA_r[0], "ones")
        PTs = [WA[0][:, 5248 + i * 512:5760 + i * 512] for i in range(12)]
        PT_r = [sub(WA_r[0], f"PT{i}") for i in range(12)]
        pti = [0]
        S.op("dve", lambda e: e.memset(ones, 1.0), writes=[ones_r])
        scale = 128 ** -0.5
        it = 0
        for qb in range(36):
            t0 = qb * 128
            if qb < 32:
                kblocks = [("w", kb) for kb in (qb - 1, qb, qb + 1) if 0 <= kb < 32] + [("c", 0), ("c", 1)]
            else:
                s0 = 32 + ((qb - 32) // 2) * 2
                kblocks = [("p", s0 - 32), ("p", s0 - 31)]
            for g in range(4):
                QB = QBs[it % 2]
                qr = QB_r[it % 2]
                it += 1
                QBf = QB.rearrange("p g q -> p (g q)")
                S.dma("pool", QB, qT[g * 512:(g + 1) * 512, t0:t0 + 128].rearrange("(g p) t -> p g t", p=128),
                      reads=[qT_r], writes=[qr])
                if qb < 32:
                    b, br = nb()
                    S.op("pe", lambda e, b=b, QBf=QBf: e.matmul(b[:], perm[:], QBf, start=True, stop=True),
                         reads=[qr, cst_r], writes=[br])
                    si = rot("stg", 6)
                    s2 = rot("stg", 6)

                    def rp(e, b=b, si=si, s2=s2, QB=QB, t0=t0):
                        last = None
                        for gg in range(4):
                            e.tensor_tensor(out=STG[si][:, gg * 128:(gg + 1) * 128], in0=b[:, gg * 128:(gg + 1) * 128],
                                            in1=RS[:, t0:t0 + 128], op=ALU.mult)
                            last = e.tensor_tensor(out=STG[s2][:, gg * 128:(gg + 1) * 128], in0=QB[:, gg, :],
                                                   in1=RC[:, t0:t0 + 128], op=ALU.mult)
                        return last
                    S.op("dve", rp, reads=[br, qr, WA_r[1]], writes=[STG_r[si], STG_r[s2]])
                    S.op("dve", lambda e, si=si, s2=s2, QBf=QBf: e.tensor_tensor(out=QBf, in0=STG[si][:], in1=STG[s2][:], op=ALU.add),
                         reads=[STG_r[si], STG_r[s2]], writes=[qr])
                nkb = len(kblocks)
                ops_ = []
                for ki, (kind, kb) in enumerate(kblocks):
                    if kind == "w":
                        kt = KT[:, g, kb * 128:(kb + 1) * 128]; vv = VV[:, kb, g * 128:(g + 1) * 128]
                    elif kind == "c":
                        kt = KT[:, g, 4096 + kb * 128:4096 + (kb + 1) * 128]; vv = VV[:, 32 + kb, g * 128:(g + 1) * 128]
                    else:
                        kt = PKT[:, g, kb * 128:(kb + 1) * 128]; vv = PVV[:, kb, g * 128:(g + 1) * 128]
                    pidx = pti[0] % 12
                    pti[0] += 1
                    PT, ptr_ = PTs[pidx], PT_r[pidx]
                    sb_, sbr = nb()
                    S.op("pe", lambda e, sb_=sb_, kt=kt, QBf=QBf: e.matmul(sb_[:], kt, QBf, start=True, stop=True),
                         reads=[BB_r, WA_r[0], qr], writes=[sbr])
                    S.op("act", lambda e, sb_=sb_, PT=PT: e.activation(out=PT, in_=sb_[:], func=AF.Exp, scale=scale),
                         reads=[sbr], writes=[ptr_])
                    if kind == "w" and kb != qb:
                        mi = 0 if kb < qb else 1
                        S.op("dve", lambda e, PT=PT, mi=mi: e.tensor_tensor(out=PT, in0=PT, in1=MK[:, mi, :], op=ALU.mult),
                             reads=[ptr_, WA_r[1]], writes=[ptr_])
                    ops_.append((vv, PT, ptr_))
                ob, obr = nb()
                db, dbr = nb()
                for ki, (vv, PT, ptr_) in enumerate(ops_):
                    S.op("pe", lambda e, ob=ob, vv=vv, PT=PT, ki=ki, nkb=nkb: e.matmul(ob[:], vv, PT, start=(ki == 0),
                                                                                   stop=(ki == nkb - 1)),
                         reads=[BB_r, WA_r[0], ptr_], writes=[obr])
                    S.op("pe", lambda e, db=db, PT=PT, ki=ki, nkb=nkb: e.matmul(db[:], ones, PT, start=(ki == 0),
                                                                            stop=(ki == nkb - 1)),
                         reads=[ones_r, ptr_], writes=[dbr])
                si = rot("stg", 6)

                def dn(e, db=db, si=si, g=g):
                    last = None
                    for gg in range(4):
                        last = e.tensor_scalar(out=STG[si][:, gg * 128:(gg + 1) * 128], in0=db[:, gg * 128:(gg + 1) * 128],
                                               scalar1=esink[:, g * 4 + gg:g * 4 + gg + 1], scalar2=None, op0=ALU.add)
                    return last
                S.op("dve", dn, reads=[dbr, esink_r], writes=[STG_r[si]])
                S.op("act", lambda e, si=si: e.activation(out=STG[si][:], in_=STG[si][:], func=AF.Ln), reads=[STG_r[si]], writes=[STG_r[si]])
                S.op("act", lambda e, si=si: e.activation(out=STG[si][:], in_=STG[si][:], func=AF.Exp, scale=-1.0),
                     reads=[STG_r[si]], writes=[STG_r[si]])
                oi = rot("sb16", 4)
                S.op("dve", lambda e, ob=ob, si=si, oi=oi: e.tensor_tensor(out=SB16[oi][:], in0=ob[:], in1=STG[si][:],
                                                                          op=ALU.mult),
                     reads=[obr, STG_r[si]], writes=[SB16_r[oi]])
                S.dma("sp", oT[qb][:, g * 4:(g + 1) * 4, :],
                      SB16[oi][:].rearrange("p (g q) -> p g q", g=4), reads=[SB16_r[oi]], writes=[oT_r])
        join(WA_r[0], QB_r + PT_r + [ones_r])

    def ffn_act(i):
        S.dma("sp", stg_s[:], ffn_convT[i, 0:128, :], reads=[in_r], writes=[stg_r])
        b, br = nb()
        S.op("pe", lambda e: e.transpose(b[:, 0:128], stg_s[:], idf[:]), reads=[stg_r, cst_r], writes=[br])
        copy("act", fcv[:].rearrange("p a c -> p (a c)")[:, 0:128], b[:, 0:128], [br], [fcv_r])
        S.dma("sp", stg_s[0:4, :], ffn_convT[i, 128:132, :], reads=[in_r], writes=[stg_r])
        b2, b2r = nb()
        S.op("pe", lambda e: e.transpose(b2[:, 0:4], stg_s[0:4, :], idf[0:4, 0:4]), reads=[stg_r, cst_r], writes=[b2r])
        copy("act", fcv[:].rearrange("p a c -> p (a c)")[:, 128:132], b2[:, 0:4], [b2r], [fcv_r])
        bufs = []
        for i2 in range(2):
            bufs.append((BB[:, (3 * i2) * NT:(3 * i2 + 1) * NT], BB[:, (3 * i2 + 1) * NT:(3 * i2 + 2) * NT],
                         BB[:, (3 * i2 + 2) * NT:(3 * i2 + 3) * NT],
                         sub(BB_r, f"fa{i2}"), sub(BB_r, f"fb{i2}"), sub(BB_r, f"fg{i2}")))
        PARTS = [(0, 2304), (2304, 4608)]
        accp_r = [sub(ACC_r, f"accq{i_}") for i_ in range(2)]
        def fa_loads(j):
            AA, BBb, GG, ar, brr, gr = bufs[j % 2]
            S.dma("sp", AA, abT[j * 128:(j + 1) * 128, :], reads=[abT_r], writes=[ar])
            S.dma("sp", BBb, abT[DFF + j * 128:DFF + (j + 1) * 128, :], reads=[abT_r], writes=[brr])
        fa_loads(0)
        for j in range(44):
            AA, BBb, GG, ar, brr, gr = bufs[j % 2]
            if j + 1 < 44:
                fa_loads(j + 1)
            gpr = [sub(gr, f"gq{j}_{i_}") for i_ in range(2)]
            for pi_, (p0, p1) in enumerate(PARTS):
                S.op("act", lambda e, j=j, AA=AA, p0=p0, p1=p1: e.activation(out=ACC[:, p0:p1], in_=AA[:, p0:p1], func=AF.Copy,
                                                                          scale=fcv[:, 1, j:j + 1]),
                     reads=[ar, fcv_r], writes=[accp_r[pi_]])
            for pi_, (p0, p1) in enumerate(PARTS):
                acr = accp_r[pi_]
                for tap, (lo, hi) in ((0, (1, 0)), (2, (0, 1))):
                    rngs = []
                    for (s0, s1) in SEQS:
                        o0, o1 = max(s0 + lo, p0), min(s1 - hi, p1)
                        if o1 > o0:
                            rngs.append((o0, o1))

                    def taps(e, j=j, AA=AA, tap=tap, lo=lo, hi=hi, rngs=rngs):
                        last = None
                        for (o0, o1) in rngs:
                            last = e.scalar_tensor_tensor(out=ACC[:, o0:o1], in0=AA[:, o0 - lo + hi:o1 - lo + hi],
                                                          scalar=fcv[:, tap, j:j + 1], in1=ACC[:, o0:o1],
                                                          op0=ALU.mult, op1=ALU.add)
                        return last
                    S.op("dve", taps, reads=[ar, fcv_r, acr], writes=[acr])
            for pi_, (p0, p1) in enumerate(PARTS):
                acr = accp_r[pi_]
                S.op("act", lambda e, GG=GG, p0=p0, p1=p1: e.activation(out=GG[:, p0:p1], in_=ACC[:, p0:p1], func=AF.Silu),
                     reads=[acr], writes=[gpr[pi_]])
                S.op("pool", lambda e, GG=GG, BBb=BBb, p0=p0, p1=p1: e.tensor_tensor(out=GG[:, p0:p1], in0=GG[:, p0:p1],
                                                                                   in1=BBb[:, p0:p1], op=ALU.mult),
                     reads=[gpr[pi_], brr], writes=[gpr[pi_]])
                S.dma("sp", gT[j * 128:(j + 1) * 128, p0:p1], GG[:, p0:p1], reads=[gpr[pi_]], writes=[gT_r])
            join(gr, gpr)
        join(ACC_r, accp_r)
        join(BB_r, [x for bf_ in bufs for x in bf_[3:]])

    CVT = [S.sb(f"CVT{i}", [128, 512], BF16) for i in range(6)]
    CVT_r = [Res(f"CVT{i}") for i in range(6)]
    rr["cvt"] = 0
    hyc = S.sb("hyc", [128, 16], F32); hyc_r = Res("hyc")
    wf1 = S.sb("wf1", [33, 64], F32); wf2 = S.sb("wf2", [64, 64], F32); wf_r = Res("wf")
    hsdP = dscr("hsdP", [2, LP, 2 * D], BF16); hsdP_r = Res("hsdP", True)
    ghatP = dscr("ghatP", [2, LP, 2 * D], BF16); ghatP_r = Res("ghatP", True)
    yhatP = dscr("yhatP", [2, 2 * LP, D], BF16); yhatP_r = Res("yhatP", True)

    def post_ptm(ti, j0, b, br):
        si = rot("sb16", 4)
        copy(evq(), SB16[si][:], b[:], [br], [SB16_r[si]])
        S.dma("sp", ptm[ti * 128:(ti + 1) * 128, j0:j0 + 512], SB16[si][:], reads=[SB16_r[si]], writes=[ptm_r])

    def hy_conv3():
        starts = {0, 32, 34}
        ends = {31, 33, 35}
        W2 = 2048
        ins_ = [[BB[:, (k * 2 + i2) * W2:(k * 2 + i2 + 1) * W2] for i2 in range(2)] for k in range(3)]
        in_r_ = [[sub(BB_r, f"cv{k}{i2}") for i2 in range(2)] for k in range(3)]
        wts = [BB[:, 12288 + k * 4096:12288 + (k + 1) * 4096].bitcast(F32) for k in range(3)]
        wt_r = [sub(BB_r, f"cw{k}") for k in range(3)]
        accs = [BB[:, 24576 + i2 * 4096:24576 + (i2 + 1) * 4096].bitcast(F32) for i2 in range(2)]
        acc_r = [sub(BB_r, f"ca{i2}") for i2 in range(2)]
        T1, T2 = ACC[:, 0:W2], ACC[:, W2:2 * W2]
        T1_r, T2_r = sub(ACC_r, "T1"), sub(ACC_r, "T2")
        it = 0
        for cb in range(3):
            c0 = cb * W2
            for k in range(3):
                S.dma("sp", wts[k], hy_conv[k, c0:c0 + W2].partition_broadcast(128), reads=[in_r], writes=[wt_r[k]])
            def loads(ti, i2):
                t0 = ti * 128
                pc, pm, pp = ins_[0][i2], ins_[1][i2], ins_[2][i2]
                pcr, pmr, ppr = in_r_[0][i2], in_r_[1][i2], in_r_[2][i2]
                S.dma("act", pc, ptm[t0:t0 + 128, c0:c0 + W2], reads=[ptm_r], writes=[pcr])
                if ti in starts:
                    S.op("pool", lambda e, pm=pm: e.memset(pm, 0.0), writes=[pmr])
                    S.dma("act", pm[1:128, :], ptm[t0:t0 + 127, c0:c0 + W2], reads=[ptm_r], writes=[pmr])
                else:
                    S.dma("act", pm, ptm[t0 - 1:t0 + 127, c0:c0 + W2], reads=[ptm_r], writes=[pmr])
                if ti in ends:
                    S.op("pool", lambda e, pp=pp: e.memset(pp, 0.0), writes=[ppr])
                    S.dma("act", pp[0:127, :], ptm[t0 + 1:t0 + 128, c0:c0 + W2], reads=[ptm_r], writes=[ppr])
                else:
                    S.dma("act", pp, ptm[t0 + 1:t0 + 129, c0:c0 + W2], reads=[ptm_r], writes=[ppr])
            loads(0, it % 2)
            for ti in range(36):
                t0 = ti * 128
                i2 = it % 2
                it += 1
                if ti + 1 < 36:
                    loads(ti + 1, it % 2)
                pc, pm, pp = ins_[0][i2], ins_[1][i2], ins_[2][i2]
                pcr, pmr, ppr = in_r_[0][i2], in_r_[1][i2], in_r_[2][i2]
                ac, acr = accs[i2], acc_r[i2]
                S.op("dve", lambda e, ac=ac, pc=pc: e.tensor_tensor(out=ac, in0=pc, in1=wts[1], op=ALU.mult),
                     reads=[pcr, wt_r[1]], writes=[acr])
                S.op("pool", lambda e, pm=pm: e.tensor_tensor(out=T1, in0=pm, in1=wts[0], op=ALU.mult),
                     reads=[pmr, wt_r[0]], writes=[T1_r])
                S.op("dve", lambda e, pp=pp: e.tensor_tensor(out=T2, in0=pp, in1=wts[2], op=ALU.mult),
                     reads=[ppr, wt_r[2]], writes=[T2_r])
                S.op("pool", lambda e, ac=ac: e.tensor_tensor(out=ac, in0=ac, in1=T1, op=ALU.add),
                     reads=[acr, T1_r], writes=[acr])
                S.op("dve", lambda e, ac=ac, pc=pc: e.tensor_tensor(out=pc, in0=ac, in1=T2, op=ALU.add),
                     reads=[acr, T2_r], writes=[pcr])
                S.dma("sp", c3[t0:t0 + 128, c0:c0 + W2], pc, reads=[pcr], writes=[c3_r])
        join(BB_r, [x for l in in_r_ for x in l] + wt_r + acc_r)
        join(ACC_r, [T1_r, T2_r])

    def hy_filters(nm, L, hs_dst, hs_dst_r):
        featT = IN["c_featT" + nm]
        FT = ACC[0:33, 0:L]
        S.dma("sp", FT, featT, writes=[ACC_r])
        S.dma("sp", wf1[:], hy_w_f1, reads=[in_r], writes=[wf_r])
        S.dma("sp", wf2[:], hy_w_f2, reads=[in_r], writes=[wf_r])
        S.dma("sp", hyc[0:64, 0:3], hy_small, reads=[in_r], writes=[hyc_r])
        def cols0(e):
            e.tensor_scalar(out=hyc[0:64, 3:4], in0=hyc[0:64, 2:3], scalar1=0.5, scalar2=None, op0=ALU.mult)
            return e.tensor_scalar(out=hyc[0:64, 4:5], in0=hyc[0:64, 2:3], scalar1=0.25, scalar2=None, op0=ALU.mult)
        S.op("dve", cols0, reads=[hyc_r], writes=[hyc_r])

        def cols(e):
            e.tensor_tensor(out=hyc[0:64, 5:6], in0=hyc[0:64, 3:4], in1=hyc[0:64, 0:1], op=ALU.mult)
            e.tensor_tensor(out=hyc[0:64, 6:7], in0=hyc[0:64, 4:5], in1=hyc[0:64, 0:1], op=ALU.mult)
            e.tensor_tensor(out=hyc[0:64, 7:8], in0=hyc[0:64, 3:4], in1=hyc[0:64, 1:2], op=ALU.mult)
            return e.tensor_tensor(out=hyc[0:64, 8:9], in0=hyc[0:64, 4:5], in1=hyc[0:64, 1:2], op=ALU.mult)
        S.op("dve", cols, reads=[hyc_r], writes=[hyc_r])
        H1 = BB[:, 0:8192].bitcast(F32)
        H2 = BB[:, 8192:16384].bitcast(F32)
        H2b = BB[:, 16384:20480]
        W3 = WA[0][0:64, 0:8192]
        S.dma("pool", W3, hy_w_f3, reads=[in_r], writes=[WA_r[0]])

        def sin_layer(lhsT, K, src, src_r, dst, bcol):
            for p0 in range(0, L, 512):
                w = min(512, L - p0)
                b, br = nb()
                S.op("pe", lambda e, b=b, p0=p0, w=w: e.matmul(b[0:64, 0:w], lhsT, src[0:K, p0:p0 + w], start=True, stop=True),
                     reads=[src_r, wf_r], writes=[br])
                s2, s4 = rot("stg", 6), rot("stg", 6)
                S.op("act", lambda e, b=b, s2=s2, w=w: e.activation(out=STG[s2][0:64, 0:w], in_=b[0:64, 0:w], func=AF.Sin,
                                                                   scale=hyc[0:64, 3:4], bias=hyc[0:64, bcol:bcol + 1]),
                     reads=[br, hyc_r], writes=[STG_r[s2]])
                S.op("act", lambda e, b=b, s4=s4, w=w: e.activation(out=STG[s4][0:64, 0:w], in_=b[0:64, 0:w], func=AF.Sin,
                                                                   scale=hyc[0:64, 4:5], bias=hyc[0:64, bcol + 1:bcol + 2]),
                     reads=[br, hyc_r], writes=[STG_r[s4]])
                S.op("dve", lambda e, s4=s4, w=w: e.tensor_tensor(out=STG[s4][0:64, 0:w], in0=STG[s4][0:64, 0:w],
                                                                 in1=STG[s4][0:64, 0:w], op=ALU.mult),
                     reads=[STG_r[s4]], writes=[STG_r[s4]])
                S.op("dve", lambda e, s4=s4, w=w: e.tensor_scalar(out=STG[s4][0:64, 0:w], in0=STG[s4][0:64, 0:w], scalar1=-2.0,
                                                                 scalar2=1.0, op0=ALU.mult, op1=ALU.add),
                     reads=[STG_r[s4]], writes=[STG_r[s4]])
                S.op("dve", lambda e, s2=s2, s4=s4, w=w, p0=p0: e.scalar_tensor_tensor(
                    out=dst[0:64, p0:p0 + w], in0=STG[s2][0:64, 0:w], scalar=2.0, in1=STG[s4][0:64, 0:w],
                    op0=ALU.mult, op1=ALU.mult), reads=[STG_r[s2], STG_r[s4]], writes=[BB_r])
        sin_layer(wf1[:], 33, ACC, ACC_r, H1, 5)
        sin_layer(wf2[:], 64, H1, BB_r, H2, 7)
        copy("dve", H2b[0:64, 0:L], H2[0:64, 0:L], [BB_r], [BB_r])
        decf, decb = IN["c_decf" + nm], IN["c_decb" + nm]
        dtl = [(XT[0][:], XT[1][:], sub(XT_r[0], "dcf0"), sub(XT_r[1], "dcb0")),
               (ACC[:, 0:2048], ACC[:, 2048:4096], sub(ACC_r, "dcf1"), sub(ACC_r, "dcb1"))]
        for tc in range(L // 128):
            dF, dB, dFr, dBr = dtl[tc % 2]
            S.dma("act", dF, decf[tc * 128:(tc + 1) * 128, :], writes=[dFr])
            S.dma("act", dB, decb[tc * 128:(tc + 1) * 128, :], writes=[dBr])
            for o in range(2):
                for db in range(4):
                    cf = o * 2048 + db * 512
                    cs = slice(db * 512, (db + 1) * 512)
                    bf_, bfr = nb()
                    bb_, bbr = nb()
                    S.op("pe", lambda e, bf_=bf_, tc=tc, cf=cf: e.matmul(bf_[:], H2b[0:64, tc * 128:(tc + 1) * 128],
                                                                      W3[:, cf:cf + 512], start=True, stop=True),
                         reads=[BB_r, WA_r[0]], writes=[bfr])
                    S.op("pe", lambda e, bb_=bb_, tc=tc, cf=cf: e.matmul(bb_[:], H2b[0:64, tc * 128:(tc + 1) * 128],
                                                                      W3[:, 4096 + cf:4096 + cf + 512], start=True, stop=True),
                         reads=[BB_r, WA_r[0]], writes=[bbr])
                    d0, d1 = rot("stg", 6), rot("stg", 6)
                    S.op("dve", lambda e, bf_=bf_, d0=d0, dF=dF, cs=cs: e.tensor_tensor(out=STG[d0][:], in0=bf_[:], in1=dF[:, cs], op=ALU.mult),
                         reads=[bfr, dFr], writes=[STG_r[d0]])
                    S.op("dve", lambda e, bb_=bb_, d1=d1, dB=dB, cs=cs: e.tensor_tensor(out=STG[d1][:], in0=bb_[:], in1=dB[:, cs], op=ALU.mult),
                         reads=[bbr, dBr], writes=[STG_r[d1]])
                    c0_, c1_ = rot("cvt", 6), rot("cvt", 6)
                    S.op("pool", lambda e, d0=d0, d1=d1, c0_=c0_: e.tensor_tensor(out=CVT[c0_][:], in0=STG[d0][:], in1=STG[d1][:],
                                                                              op=ALU.add),
                         reads=[STG_r[d0], STG_r[d1]], writes=[CVT_r[c0_]])
                    S.op("pool", lambda e, d0=d0, d1=d1, c1_=c1_: e.tensor_tensor(out=CVT[c1_][:], in0=STG[d1][:], in1=STG[d0][:],
                                                                              op=ALU.subtract),
                         reads=[STG_r[d0], STG_r[d1]], writes=[CVT_r[c1_]])
                    S.dma("sp", hs_dst[0, tc * 128:(tc + 1) * 128, cf:cf + 512], CVT[c0_][:], reads=[CVT_r[c0_]], writes=[hs_dst_r])
                    S.dma("sp", hs_dst[1, tc * 128:(tc + 1) * 128, cf:cf + 512], CVT[c1_][:], reads=[CVT_r[c1_]], writes=[hs_dst_r])
        join(XT_r[0], [dtl[0][2]]); join(XT_r[1], [dtl[0][3]]); join(ACC_r, [dtl[1][2], dtl[1][3]])

    def dft_gemm(A_t, parts, nI, CC, B_r, ncols, JB, post):
        for j0 in range(0, ncols, JB):
            bvs = []
            for pi_, (ioff, bfn) in enumerate(parts):
                if pi_ > 0 and bfn is parts[0][1]:
                    bvs.append(bvs[0])
                    continue
                off = pi_ * CC * JB
                bv = BB[:, off:off + CC * JB].rearrange("p (c w) -> p c w", c=CC)
                src = bfn(j0, JB)

                def ldb(e, bv=bv, src=src):
                    n = max(1, CC // 8)
                    step = CC // n
                    return [e.dma_start(out=bv[:, q * step:(q + 1) * step, :],
                                        in_=src[q * step * 128:(q + 1) * step * 128, :].rearrange("(c p) w -> p c w", p=128))
                            for q in range(n)]
                S.op("sp", ldb, reads=[B_r], writes=[BB_r], dma=max(1, CC // 8))
                bvs.append(bv)
            for i in range(nI):
                wi = rot("wa", 2)
                avs = []
                for pi_, (ioff, bfn) in enumerate(parts):
                    av = WA[wi][:, pi_ * CC * 128:(pi_ + 1) * CC * 128].rearrange("p (c m) -> p c m", c=CC)
                    S.dma("pool", av, A_t[ioff + i], writes=[WA_r[wi]])
                    avs.append(av)
                res = []
                for pi_ in range(len(parts)):
                    bl = []
                    for h0 in range(0, JB, 512):
                        b, br = nb()

                        def mm(e, av=avs[pi_], bv=bvs[pi_], h0=h0, b=b):
                            last = None
                            for c in range(CC):
                                last = e.matmul(b[:], av[:, c, :], bv[:, c, h0:h0 + 512], start=(c == 0), stop=(c == CC - 1))
                            return last
                        S.op("pe", mm, reads=[WA_r[wi], BB_r], writes=[br])
                        bl.append((b, br))
                    res.append(bl)
                post(i, j0, res)

    def hy_seq(nm, L, row0, hs_, hs_r_, gh_, gh_r_, yh_, yh_r_):
        CC = L // 128
        fwd_t, inv_t = IN["c_fwd" + nm], IN["c_inv" + nm]
        for part in range(2):
            def post_g(i, j0, res, part=part):
                for h, (b, br) in enumerate(res[0]):
                    ci = rot("cvt", 6)
                    copy(evq(), CVT[ci][:], b[:], [br], [CVT_r[ci]])
                    S.dma("sp", gh_[part, i * 128:(i + 1) * 128, j0 + h * 512:j0 + (h + 1) * 512], CVT[ci][:],
                          reads=[CVT_r[ci]], writes=[gh_r_])
            dft_gemm(fwd_t, [(part * CC, lambda j0, JB, part=part: hs_[part, :, j0:j0 + JB])], CC, CC, hs_r_, 2 * D, 1024, post_g)
        for o in range(2):
            if o == 0:
                vsrc = lambda j0, JB: c3[row0:row0 + L, 2 * D + j0:2 * D + j0 + JB]
                v_r = c3_r
            else:
                vsrc = lambda j0, JB: z1[row0:row0 + L, j0:j0 + JB]
                v_r = z1_r

            def post_f(i, j0, res, o=o):
                for h in range(len(res[0])):
                    (a, ar), (b, br) = res[0][h], res[1][h]
                    cg_, sg_ = rot("cvt", 6), rot("cvt", 6)
                    col = o * D + j0 + h * 512
                    S.dma("sp", CVT[cg_][:], gh_[0, i * 128:(i + 1) * 128, col:col + 512], reads=[gh_r_], writes=[CVT_r[cg_]])
                    S.dma("sp", CVT[sg_][:], gh_[1, i * 128:(i + 1) * 128, col:col + 512], reads=[gh_r_], writes=[CVT_r[sg_]])
                    t1, t2 = rot("stg", 6), rot("stg", 6)
                    S.op("dve", lambda e, a=a, t1=t1, cg_=cg_: e.tensor_tensor(out=STG[t1][:], in0=a[:], in1=CVT[cg_][:], op=ALU.mult),
                         reads=[ar, CVT_r[cg_]], writes=[STG_r[t1]])
                    S.op("dve", lambda e, b=b, t2=t2, sg_=sg_: e.tensor_tensor(out=STG[t2][:], in0=b[:], in1=CVT[sg_][:], op=ALU.mult),
                         reads=[br, CVT_r[sg_]], writes=[STG_r[t2]])
                    y0 = rot("sb16", 4)
                    S.op("dve", lambda e, t1=t1, t2=t2, y0=y0: e.tensor_tensor(out=SB16[y0][:], in0=STG[t1][:], in1=STG[t2][:], op=ALU.add),
                         reads=[STG_r[t1], STG_r[t2]], writes=[SB16_r[y0]])
                    S.dma("sp", yh_[o, i * 128:(i + 1) * 128, j0 + h * 512:j0 + (h + 1) * 512], SB16[y0][:],
                          reads=[SB16_r[y0]], writes=[yh_r_])
                    t3, t4 = rot("stg", 6), rot("stg", 6)
                    S.op("dve", lambda e, b=b, t3=t3, cg_=cg_: e.tensor_tensor(out=STG[t3][:], in0=b[:], in1=CVT[cg_][:], op=ALU.mult),
                         reads=[br, CVT_r[cg_]], writes=[STG_r[t3]])
                    S.op("dve", lambda e, a=a, t4=t4, sg_=sg_: e.tensor_tensor(out=STG[t4][:], in0=a[:], in1=CVT[sg_][:], op=ALU.mult),
                         reads=[ar, CVT_r[sg_]], writes=[STG_r[t4]])
                    y1 = rot("sb16", 4)
                    S.op("dve", lambda e, t3=t3, t4=t4, y1=y1: e.tensor_tensor(out=SB16[y1][:], in0=STG[t3][:], in1=STG[t4][:],
                                                                             op=ALU.subtract),
                         reads=[STG_r[t3], STG_r[t4]], writes=[SB16_r[y1]])
                    S.dma("sp", yh_[o, L + i * 128:L + (i + 1) * 128, j0 + h * 512:j0 + (h + 1) * 512], SB16[y1][:],
                          reads=[SB16_r[y1]], writes=[yh_r_])
            dft_gemm(fwd_t, [(0, vsrc), (CC, vsrc)], CC, CC, v_r, D, 1024, post_f)

            def post_i(i, j0, res, o=o):
                (y, yr) = res[0][0]
                t0 = row0 + i * 128
                bi_ = rot("stg", 6)
                S.dma("sp", STG[bi_][:], hy_bias[o, j0:j0 + 512].partition_broadcast(128), reads=[in_r], writes=[STG_r[bi_]])
                vv, gg = rot("cvt", 6), rot("cvt", 6)
                if o == 0:
                    S.dma("sp", CVT[vv][:], c3[t0:t0 + 128, 2 * D + j0:2 * D + j0 + 512], reads=[c3_r], writes=[CVT_r[vv]])
                    S.dma("sp", CVT[gg][:], c3[t0:t0 + 128, j0:j0 + 512], reads=[c3_r], writes=[CVT_r[gg]])
                else:
                    S.dma("sp", CVT[vv][:], z1[t0:t0 + 128, j0:j0 + 512], reads=[z1_r], writes=[CVT_r[vv]])
                    S.dma("sp", CVT[gg][:], c3[t0:t0 + 128, D + j0:D + j0 + 512], reads=[c3_r], writes=[CVT_r[gg]])
                S.op("dve", lambda e, bi_=bi_, vv=vv: e.tensor_tensor(out=STG[bi_][:], in0=CVT[vv][:], in1=STG[bi_][:], op=ALU.mult),
                     reads=[CVT_r[vv], STG_r[bi_]], writes=[STG_r[bi_]])
                S.op("dve", lambda e, y=y, bi_=bi_: e.tensor_tensor(out=STG[bi_][:], in0=y[:], in1=STG[bi_][:], op=ALU.add),
                     reads=[yr, STG_r[bi_]], writes=[STG_r[bi_]])
                S.op("dve", lambda e, bi_=bi_, gg=gg, vv=vv: e.tensor_tensor(out=CVT[vv][:], in0=STG[bi_][:], in1=CVT[gg][:], op=ALU.mult),
                     reads=[STG_r[bi_], CVT_r[gg]], writes=[CVT_r[vv]])
                dst, dr = (z1, z1_r) if o == 0 else (zz, zz_r)
                S.dma("sp", dst[t0:t0 + 128, j0:j0 + 512], CVT[vv][:], reads=[CVT_r[vv]], writes=[dr])
            dft_gemm(inv_t, [(0, lambda j0, JB, o=o: yh_[o, :, j0:j0 + JB])], CC, 2 * CC, yh_r_, D, 512, post_i)


    PL = {"fp": [], "bf": [], "fi": 0, "bi": 0, "subs": []}

    def pools_open():
        fp = [(STG[i][:], STG_r[i]) for i in range(1, 6)]
        bfp = [(CVT[i][:], CVT_r[i]) for i in range(6)] + [(SB16[i][:], SB16_r[i]) for i in range(4)]
        subs = []
        for k in range(2):
            for q in range(4):
                r_ = sub(XT_r[k], f"xtp{k}{q}")
                subs.append((XT_r[k], r_))
                fp.append((XT[k][:, q * 512:(q + 1) * 512], r_))
        for q in range(9):
            r_ = sub(ACC_r, f"accp{q}")
            subs.append((ACC_r, r_))
            fp.append((ACC[:, q * 512:(q + 1) * 512], r_))
        hv = HTS[:].rearrange("p c t -> p (c t)")
        for q in range(8):
            r_ = sub(HTS_r, f"htsp{q}")
            subs.append((HTS_r, r_))
            bfp.append((hv[:, q * 512:(q + 1) * 512], r_))
        for (t_, tr_, nm_) in ((XN, XN_r, "xnp"), (JK, JK_r, "jkp")):
            for q in range(4):
                r_ = sub(tr_, f"{nm_}{q}")
                subs.append((tr_, r_))
                bfp.append((t_[:, q * 512:(q + 1) * 512], r_))
        PL["fp"], PL["bf"], PL["subs"] = fp, bfp, subs

    def pools_close():
        for parent, r_ in PL["subs"]:
            join(parent, [r_])
        PL["fp"], PL["bf"], PL["subs"] = [], [], []

    def fpt():
        i = PL["fi"] % len(PL["fp"])
        PL["fi"] += 1
        return PL["fp"][i]

    def bft():
        i = PL["bi"] % len(PL["bf"])
        PL["bi"] += 1
        return PL["bf"][i]

    B1 = dscr("B1", [2, 128, 32, D], BF16); B1_r = Res("B1", True)
    D1 = dscr("D1", [2, 128, 32, D], BF16); D1_r = Res("D1", True)
    G2 = dscr("G2", [2, 128, 32, 2 * D], BF16); G2_r = Res("G2", True)
    fcs = S.sb("fcs", [128, 8, 128], BF16); fcs_r = Res("fcs")
    tws = S.sb("tws", [128, 2, 64], F32); tws_r = Res("tws")

    def fft_consts():
        for i_, nm_ in enumerate(("f1c", "f1s", "i1c", "i1s", "cbm", "sbm", "ncbm", "nsbm")):
            S.dma("sp", fcs[:, i_, :], IN["c_" + nm_], writes=[fcs_r])
        S.dma("sp", tws[:, 0, :], IN["c_tw1"], writes=[tws_r])
        S.dma("sp", tws[:, 1, :], IN["c_tw2"], writes=[tws_r])

    BT = {"t": [], "i": 0, "subs": []}

    def bt_open():
        t, subs = [], []
        for (tile_, tr_, nm_, n_) in ((ACC, ACC_r, "ba", 4), (XT[0], XT_r[0], "bx0", 2), (XT[1], XT_r[1], "bx1", 2)):
            v = tile_[:].bitcast(BF16)
            for q in range(n_):
                r_ = sub(tr_, f"{nm_}{q}")
                subs.append((tr_, r_))
                t.append((v[:, q * 2048:(q + 1) * 2048], r_))
        hv = HTS[:].rearrange("p c t -> p (c t)")
        for q in range(2):
            r_ = sub(HTS_r, f"bh{q}")
            subs.append((HTS_r, r_))
            t.append((hv[:, q * 2048:(q + 1) * 2048], r_))
        for (tile_, tr_, nm_) in ((XN, XN_r, "bxn"), (JK, JK_r, "bjk")):
            r_ = sub(tr_, nm_)
            subs.append((tr_, r_))
            t.append((tile_[:], r_))
        BT["t"], BT["subs"] = t, subs

    def bt_close():
        for parent, r_ in BT["subs"]:
            join(parent, [r_])
        BT["t"], BT["subs"] = [], []

    def btt():
        i = BT["i"] % len(BT["t"])
        BT["i"] += 1
        return BT["t"][i]

    rr["sfp"] = 0

    def sfp():
        i = rot("sfp", 10)
        return (STG[i][:], STG_r[i]) if i < 6 else (STX[i - 6][:], STX_r[i - 6])

    def sbf():
        k = rot("sbf", 10)
        return (CVT[k][:], CVT_r[k]) if k < 6 else (SB16[k - 6][:], SB16_r[k - 6])
    rr["sbf"] = 0

    def twiddle_evac(pb, pbr, qb_, qbr, ti_, col, o1, o1r, o2, o2r):
        cc = tws[:, ti_, col:col + 1]
        ss = tws[:, ti_, 32 + col:32 + col + 1]
        (u1, u1r), (u2, u2r) = sfp(), sfp()
        S.op("act", lambda e: e.activation(out=u1, in_=qb_[:], func=AF.Copy, scale=ss), reads=[qbr, tws_r], writes=[u1r])
        S.op("act", lambda e: e.activation(out=u2, in_=qb_[:], func=AF.Copy, scale=cc), reads=[qbr, tws_r], writes=[u2r])
        S.op("dve", lambda e: e.scalar_tensor_tensor(out=o1, in0=pb[:], scalar=cc, in1=u1, op0=ALU.mult, op1=ALU.subtract),
             reads=[pbr, u1r, tws_r], writes=[o1r])
        S.op("dve", lambda e: e.scalar_tensor_tensor(out=o2, in0=pb[:], scalar=ss, in1=u2, op0=ALU.mult, op1=ALU.add),
             reads=[pbr, u2r, tws_r], writes=[o2r])

    def fft_s1(src_fn, src_r):
        xb = [BB[:, i2 * 16384:(i2 + 1) * 16384].rearrange("p (r c) -> p r c", r=8) for i2 in range(2)]
        xb_r = [sub(BB_r, f"xb{i2}") for i2 in range(2)]
        srcv = src_fn().rearrange("(a r) c -> a r c", r=32)
        for rq in range(4):
            i2 = rq % 2

            def ld(e, i2=i2, rq=rq):
                return [e.dma_start(out=xb[i2][:, q * 2:(q + 1) * 2, :], in_=srcv[:, rq * 8 + q * 2:rq * 8 + (q + 1) * 2, :])
                        for q in range(4)]
            S.op("pool", ld, reads=[src_r], writes=[xb_r[i2]], dma=4)
            for r8 in range(8):
                r = rq * 8 + r8
                (ore, orr), (oim, oir) = btt(), btt()
                for db in range(4):
                    cs = slice(db * 512, (db + 1) * 512)
                    pb, pbr = nb()
                    qb_, qbr = nb()
                    S.op("pe", lambda e, pb=pb, i2=i2, r8=r8, cs=cs: e.matmul(pb[:], fcs[:, 0, :], xb[i2][:, r8, cs], start=True, stop=True),
                         reads=[xb_r[i2], fcs_r], writes=[pbr])
                    S.op("pe", lambda e, qb_=qb_, i2=i2, r8=r8, cs=cs: e.matmul(qb_[:], fcs[:, 1, :], xb[i2][:, r8, cs], start=True, stop=True),
                         reads=[xb_r[i2], fcs_r], writes=[qbr])
                    twiddle_evac(pb, pbr, qb_, qbr, 0, r, ore[:, cs], orr, oim[:, cs], oir)
                S.dma("sp", B1[0, :, r, :], ore, reads=[orr], writes=[B1_r])
                S.dma("sp", B1[1, :, r, :], oim, reads=[oir], writes=[B1_r])
        join(BB_r, xb_r)

    def fft_s2_load(src, src_r, jq, i2, bufs, bufs_r):
        for c_ in range(2):
            v = src[c_].rearrange("(j q) r d -> (q r) j d", q=4)
            dst = bufs[i2][c_]

            def ld(e, v=v, dst=dst, jq=jq):
                return [e.dma_start(out=dst[:, q * 2:(q + 1) * 2, :], in_=v[:, jq * 8 + q * 2:jq * 8 + (q + 1) * 2, :]) for q in range(4)]
            S.op("pool", ld, reads=[src_r], writes=[bufs_r[i2][c_]], dma=4)

    def s2_bufs():
        bufs = [[BB[:, (i2 * 2 + c_) * 8192:(i2 * 2 + c_ + 1) * 8192].rearrange("p (j c) -> p j c", j=4) for c_ in range(2)]
                for i2 in range(2)]
        bufs_r = [[sub(BB_r, f"s2b{i2}{c_}") for c_ in range(2)] for i2 in range(2)]
        return bufs, bufs_r

    def fft_filters(o):
        for part in range(2):
            fft_s1(lambda part=part: hsd[part, :, o * D:(o + 1) * D], hsd_r)
            bufs, bufs_r = s2_bufs()
            for jq in range(8):
                i2 = jq % 2
                for c_ in range(2):
                    v = B1[c_].rearrange("(j q) r d -> (q r) j d", q=4)

                    def ld(e, v=v, dst=bufs[i2][c_], jq=jq):
                        return [e.dma_start(out=dst[:, q:q + 1, :], in_=v[:, jq * 4 + q:jq * 4 + q + 1, :]) for q in range(4)]
                    S.op("pool", ld, reads=[B1_r], writes=[bufs_r[i2][c_]], dma=4)
                bre, bim = bufs[i2]
                for j4 in range(4):
                    j = jq * 4 + j4
                    og, ogr = btt()
                    for db in range(4):
                        cs = slice(db * 512, (db + 1) * 512)
                        b, br = nb()
                        m0, m1 = (4, 7) if part == 0 else (5, 4)
                        S.op("pe", lambda e, b=b, j4=j4, cs=cs, bre=bre, bim=bim, m0=m0, m1=m1: (
                            e.matmul(b[:], fcs[:, m0, :], bre[:, j4, cs], start=True, stop=False),
                            e.matmul(b[:], fcs[:, m1, :], bim[:, j4, cs], start=False, stop=True))[-1],
                            reads=[bufs_r[i2][0], bufs_r[i2][1], fcs_r], writes=[br])
                        copy(evq(), og[:, cs], b[:], [br], [ogr])
                    S.dma("sp", G2[part, :, j, o * D:(o + 1) * D], og, reads=[ogr], writes=[G2_r])
            join(BB_r, [x for l in bufs_r for x in l])

    def fft_conv(o):
        if o == 0:
            vsrc, v_r = (lambda: c3[0:LS, 2 * D:3 * D]), c3_r
        else:
            vsrc, v_r = (lambda: z1[0:LS, :]), z1_r
        fft_s1(vsrc, v_r)
        bufs, bufs_r = s2_bufs()
        d1v = [D1[c_].rearrange("(j q) r d -> j (q r) d", q=4) for c_ in range(2)]

        def g_loads(j):
            (gc, gcr), (gs, gsr) = btt(), btt()
            S.dma("sp", gc, G2[0, :, j, o * D:(o + 1) * D], reads=[G2_r], writes=[gcr])
            S.dma("sp", gs, G2[1, :, j, o * D:(o + 1) * D], reads=[G2_r], writes=[gsr])
            return (gc, gcr), (gs, gsr)
        gnext = None
        pend = [None]
        for jq in range(8):
            i2 = jq % 2
            for c_ in range(2):
                v = B1[c_].rearrange("(j q) r d -> (q r) j d", q=4)

                def ld(e, v=v, dst=bufs[i2][c_], jq=jq):
                    return [e.dma_start(out=dst[:, q:q + 1, :], in_=v[:, jq * 4 + q:jq * 4 + q + 1, :]) for q in range(4)]
                S.op("pool", ld, reads=[B1_r], writes=[bufs_r[i2][c_]], dma=4)
            bre, bim = bufs[i2]
            for j4 in range(4):
                j = jq * 4 + j4
                if j == 0:
                    gnext = g_loads(0)
                (gc, gcr), (gs, gsr) = gnext
                if j + 1 < 32:
                    gnext = g_loads(j + 1)
                (ore, orr), (oim, oir) = btt(), btt()
                for db in range(4):
                    cs = slice(db * 512, (db + 1) * 512)
                    a, ar = nb()
                    b, br = nb()
                    S.op("pe", lambda e, a=a, j4=j4, cs=cs, bre=bre, bim=bim: (
                        e.matmul(a[:], fcs[:, 4, :], bre[:, j4, cs], start=True, stop=False),
                        e.matmul(a[:], fcs[:, 7, :], bim[:, j4, cs], start=False, stop=True))[-1],
                        reads=[bufs_r[i2][0], bufs_r[i2][1], fcs_r], writes=[ar])
                    S.op("pe", lambda e, b=b, j4=j4, cs=cs, bre=bre, bim=bim: (
                        e.matmul(b[:], fcs[:, 5, :], bre[:, j4, cs], start=True, stop=False),
                        e.matmul(b[:], fcs[:, 4, :], bim[:, j4, cs], start=False, stop=True))[-1],
                        reads=[bufs_r[i2][0], bufs_r[i2][1], fcs_r], writes=[br])
                    if pend[0] is not None:
                        pend[0]()
                        pend[0] = None
                    (t1, t1r), (t2, t2r), (t3, t3r), (t4, t4r) = sfp(), sfp(), sfp(), sfp()
                    S.op("dve", lambda e, a=a, t1=t1, gc=gc, cs=cs: e.tensor_tensor(out=t1, in0=a[:], in1=gc[:, cs], op=ALU.mult),
                         reads=[ar, gcr], writes=[t1r])
                    S.op("dve", lambda e, b=b, t2=t2, gs=gs, cs=cs: e.tensor_tensor(out=t2, in0=b[:], in1=gs[:, cs], op=ALU.mult),
                         reads=[br, gsr], writes=[t2r])
                    S.op("dve", lambda e, b=b, t3=t3, gc=gc, cs=cs: e.tensor_tensor(out=t3, in0=b[:], in1=gc[:, cs], op=ALU.mult),
                         reads=[br, gcr], writes=[t3r])
                    S.op("dve", lambda e, a=a, t4=t4, gs=gs, cs=cs: e.tensor_tensor(out=t4, in0=a[:], in1=gs[:, cs], op=ALU.mult),
                         reads=[ar, gsr], writes=[t4r])
                    (y0, y0r), (y1, y1r) = sbf(), sbf()
                    S.op("pool", lambda e, t1=t1, t2=t2, y0=y0: e.tensor_tensor(out=y0, in0=t1, in1=t2, op=ALU.add),
                         reads=[t1r, t2r], writes=[y0r])
                    S.op("pool", lambda e, t3=t3, t4=t4, y1=y1: e.tensor_tensor(out=y1, in0=t3, in1=t4, op=ALU.subtract),
                         reads=[t3r, t4r], writes=[y1r])

                    def second(y0=y0, y0r=y0r, y1=y1, y1r=y1r, j=j, cs=cs, ore=ore, orr=orr, oim=oim, oir=oir, last=(db == 3)):
                        cb_, cbr = nb()
                        db_, dbr = nb()
                        S.op("pe", lambda e: (e.matmul(cb_[:], fcs[:, 4, :], y0, start=True, stop=False),
                                              e.matmul(cb_[:], fcs[:, 5, :], y1, start=False, stop=True))[-1],
                             reads=[y0r, y1r, fcs_r], writes=[cbr])
                        S.op("pe", lambda e: (e.matmul(db_[:], fcs[:, 5, :], y0, start=True, stop=False),
                                              e.matmul(db_[:], fcs[:, 6, :], y1, start=False, stop=True))[-1],
                             reads=[y0r, y1r, fcs_r], writes=[dbr])
                        twiddle_evac(cb_, cbr, db_, dbr, 1, j, ore[:, cs], orr, oim[:, cs], oir)
                        if last:
                            S.dma("sp", d1v[0][j], ore, reads=[orr], writes=[D1_r])
                            S.dma("sp", d1v[1][j], oim, reads=[oir], writes=[D1_r])
                    pend[0] = second
        if pend[0] is not None:
            pend[0]()
            pend[0] = None
        join(BB_r, [x for l in bufs_r for x in l])
        dbufs = [[BB[:, (i2 * 2 + c_) * 8192:(i2 * 2 + c_ + 1) * 8192].rearrange("p (r c) -> p r c", r=4) for c_ in range(2)]
                 for i2 in range(2)]
        dbufs_r = [[sub(BB_r, f"d1b{i2}{c_}") for c_ in range(2)] for i2 in range(2)]
        S.dma("sp", WA[1][:, 0:4096].bitcast(F32), hy_bias[o, :].partition_broadcast(128), reads=[in_r], writes=[WA_r[1]])
        biasv = WA[1][:, 0:4096].bitcast(F32)
        c3v = c3[0:LS, :].rearrange("(a r) c -> r a c", r=32)
        z1v = z1[0:LS, :].rearrange("(a r) c -> r a c", r=32)
        zzv = zz[0:LS, :].rearrange("(a r) c -> r a c", r=32)
        for rq in range(8):
            i2 = rq % 2
            for c_ in range(2):
                def ld(e, c_=c_, dst=dbufs[i2][c_], rq=rq):
                    return [e.dma_start(out=dst[:, q:q + 1, :], in_=D1[c_, :, rq * 4 + q:rq * 4 + q + 1, :]) for q in range(4)]
                S.op("pool", ld, reads=[D1_r], writes=[dbufs_r[i2][c_]], dma=4)
            dre, dim_ = dbufs[i2]
            for r4 in range(4):
                r = rq * 4 + r4
                (vv, vvr), (gg, ggr) = btt(), btt()
                if o == 0:
                    S.dma("act", vv, c3v[r, :, 2 * D:3 * D], reads=[c3_r], writes=[vvr])
                    S.dma("act", gg, c3v[r, :, 0:D], reads=[c3_r], writes=[ggr])
                else:
                    S.dma("act", vv, z1v[r, :, :], reads=[z1_r], writes=[vvr])
                    S.dma("act", gg, c3v[r, :, D:2 * D], reads=[c3_r], writes=[ggr])
                for db in range(4):
                    cs = slice(db * 512, (db + 1) * 512)
                    y, yr = nb()
                    S.op("pe", lambda e, y=y, r4=r4, cs=cs, dre=dre, dim_=dim_: (
                        e.matmul(y[:], fcs[:, 2, :], dre[:, r4, cs], start=True, stop=False),
                        e.matmul(y[:], fcs[:, 3, :], dim_[:, r4, cs], start=False, stop=True))[-1],
                        reads=[dbufs_r[i2][0], dbufs_r[i2][1], fcs_r], writes=[yr])
                    t_, tr_ = sfp()
                    S.op("dve", lambda e, t_=t_, vv=vv, cs=cs: e.tensor_tensor(out=t_, in0=vv[:, cs], in1=biasv[:, cs], op=ALU.mult),
                         reads=[vvr, WA_r[1]], writes=[tr_])
                    S.op("dve", lambda e, y=y, t_=t_: e.tensor_tensor(out=t_, in0=y[:], in1=t_, op=ALU.add),
                         reads=[yr, tr_], writes=[tr_])
                    S.op("dve", lambda e, t_=t_, gg=gg, vv=vv, cs=cs: e.tensor_tensor(out=vv[:, cs], in0=t_, in1=gg[:, cs], op=ALU.mult),
                         reads=[tr_, ggr], writes=[vvr])
                dstv, dr = (z1v, z1_r) if o == 0 else (zzv, zz_r)
                S.dma("sp", dstv[r, :, :], vv, reads=[vvr], writes=[dr])
        join(BB_r, [x for l in dbufs_r for x in l])

    def hy_seq_fft():
        fft_consts()
        bt_open()
        for o in range(2):
            fft_filters(o)
        for o in range(2):
            fft_conv(o)
        bt_close()

    phase = [0]

    def P(fn, *a, **k):
        if phase[0] >= DBG_STOP:
            raise _Stop()
        fn(*a, **k)
        phase[0] += 1

    def post_v(ti, j0, b, br):
        si = rot("sb16", 4)
        if ti < 32:
            copy(evq(), SB16[si][:], b[:], [br], [SB16_r[si]])
        else:
            s2 = rot("stg", 6)
            copy("dve", STG[s2][:], b[:], [br], [STG_r[s2]])
            for hh in range(2):
                S.dma("sp", nv[(ti - 32) * 128:(ti - 31) * 128, hh * 256:(hh + 1) * 256], STG[s2][:, hh * 256:(hh + 1) * 256],
                      reads=[STG_r[s2]], writes=[Res("nv", True)])
            copy("act", SB16[si][:], STG[s2][:], [STG_r[s2]], [SB16_r[si]])
        S.dma("sp", vtm[ti * 128:(ti + 1) * 128, :], SB16[si][:], reads=[SB16_r[si]], writes=[vtm_r])

    def post_k(ti, j0, b, br):
        s2 = rot("stg", 6)
        copy(evq(), STG[s2][:], b[:], [br], [STG_r[s2]])
        for hh in range(2):
            S.dma("sp", nk[(ti - 32) * 128:(ti - 31) * 128, hh * 256:(hh + 1) * 256], STG[s2][:, hh * 256:(hh + 1) * 256],
                  reads=[STG_r[s2]], writes=[Res("nk", True)])

    def final_norm(xi_):
        for ti in range(36):
            xi = rot("xt", 2)
            S.dma("sp", XT[xi][:], xres[xi_][ti * 128:(ti + 1) * 128, :], reads=[xres_r[xi_]], writes=[XT_r[xi]])
            S.op("act", lambda e, xi=xi: e.activation(out=JK[:], in_=XT[xi][:], func=AF.Square, accum_out=sm[:, 0:1]),
                 reads=[XT_r[xi]], writes=[JK_r, sm_r])
            S.op("dve", lambda e: e.tensor_scalar(out=sm[:, 1:2], in0=sm[:, 0:1], scalar1=1.0 / D, scalar2=EPS,
                                                  op0=ALU.mult, op1=ALU.add), reads=[sm_r], writes=[sm_r])
            S.op("act", lambda e: e.activation(out=sm[:, 2:3], in_=sm[:, 1:2], func=AF.Sqrt), reads=[sm_r], writes=[sm_r])
            S.op("dve", lambda e: e.reciprocal(out=sm[:, 3:4], in_=sm[:, 2:3]), reads=[sm_r], writes=[sm_r])
            if ti == 0:
                S.dma("sp", ACC[:, 0:D], norm_final.partition_broadcast(128), reads=[in_r], writes=[ACC_r])
            S.op("dve", lambda e, xi=xi: e.scalar_tensor_tensor(out=XT[xi][:], in0=XT[xi][:], scalar=sm[:, 3:4],
                                                               in1=ACC[:, 0:D], op0=ALU.mult, op1=ALU.mult),
                 reads=[XT_r[xi], sm_r, ACC_r], writes=[XT_r[xi]])
            dst = ys[ti * 128:(ti + 1) * 128, :] if ti < 32 else yp[(ti - 32) * 128:(ti - 31) * 128, :]
            for hh in range(4):
                S.dma("sp", dst[:, hh * 512:(hh + 1) * 512], XT[xi][:, hh * 512:(hh + 1) * 512], reads=[XT_r[xi]],
                      writes=[Res("y", True)])

    if os.environ.get("MK_VAR", "") == "nvtest":
        S.op("dve", lambda e: e.memset(STG[0][:], 1.0), writes=[STG_r[0]])
        S.dma("sp", nv[0:128, :], STG[0][:], reads=[STG_r[0]], writes=[Res("nv", True)])
    try:
        P(mod_phase, 0)
        P(norm_T, x_src(None), 0)
        P(gemm_fm, w_qkv, 2560, hT, hT_r, 16, post_store_fm(qT, qT_r))
        P(gemm_tm, hTt, hTt_r, 16, w_qkv[:, 2560:3072], 512, post_v, tiled=True)
        P(gemm_tm, hTt, hTt_r, 16, w_qkv[:, 2048:2560], 512, post_k, tiles=range(32, 36), tiled=True)
        P(attention)
        P(gemm_tm, oT, oT_r, 16, w_o, D, post_residual(x_src(None), 0, xres[0], xres_r[0]), tiled=True)
        P(norm_T, x_src(0), 1)
        P(gemm_fm, ffn_w_up[0], 2 * DFF, hT, hT_r, 16, post_store_fm(abT, abT_r))
        P(ffn_act, 0)
        P(gemm_tm, gT, gT_r, 44, ffn_w_down[0], D, post_residual(x_src(0), 1, xres[1], xres_r[1]))
        P(mod_phase, 1)
        P(norm_T, x_src(1), 0)
        P(gemm_tm, hTt, hTt_r, 16, hy_w_in, 3 * D, post_ptm, tiled=True)
        P(hy_conv3)
        P(hy_filters, "S", LS, hsd, hsd_r)
        if os.environ.get("MK_DFT", "") == "big":
            P(hy_seq, "S", LS, 0, hsd, hsd_r, ghat, ghat_r, yhat, yhat_r)
        else:
            P(hy_seq_fft)
        P(hy_filters, "P", LP, hsdP, hsdP_r)
        P(hy_seq, "P", LP, 4096, hsdP, hsdP_r, ghatP, ghatP_r, yhatP, yhatP_r)
        P(hy_seq, "P", LP, 4352, hsdP, hsdP_r, ghatP, ghatP_r, yhatP, yhatP_r)
        P(norm_T, None, 0, plain_src=(zz, zz_r))
        P(gemm_tm, hTt, hTt_r, 16, hy_w_out, D, post_residual(x_src(1), 0, xres[2], xres_r[2]), tiled=True)
        P(norm_T, x_src(2), 1)
        P(gemm_fm, ffn_w_up[1], 2 * DFF, hT, hT_r, 16, post_store_fm(abT, abT_r))
        P(ffn_act, 1)
        P(gemm_tm, gT, gT_r, 44, ffn_w_down[1], D, post_residual(x_src(2), 1, xres[3], xres_r[3]))
        P(final_norm, 3)
    except _Stop:
        pass
    S.finish()
    S.emit()
    S.st.close()
    return nc


_NC = None


def kernel(x_prompt, x_sample, cache_k, cache_v, c, c_ctx, w_mod, b_mod, norm_mix, norm_ffn, norm_final,
           w_qkv, w_o, attn_sink, hy_w_in, hy_conv, hy_w_f1, hy_b_f1, hy_w_f2, hy_b_f2, hy_w_f3, hy_freq,
           hy_bias, hy_w_out, ffn_w_up, ffn_conv, ffn_w_down):
    global _NC
    f = lambda a: np.ascontiguousarray(np.asarray(a, dtype=np.float32))
    x_prompt, x_sample, cache_k, cache_v, c, c_ctx = map(f, (x_prompt, x_sample, cache_k, cache_v, c, c_ctx))
    cst = _consts()
    if _NC is None:
        _NC = build()
    nc = _NC
    smallv = np.concatenate([f(b_mod).reshape(2, 96, 128), f(norm_mix).reshape(2, 16, 128),
                             f(norm_ffn).reshape(2, 16, 128)], axis=1)
    fc = f(ffn_conv).reshape(2, 3, 44, 128).reshape(2, 132, 128)
    shared = {
        "w_mod": f(w_mod), "smallv": np.ascontiguousarray(smallv), "norm_final": f(norm_final),
        "w_qkv": f(w_qkv)[0], "w_o": f(w_o)[0], "sink": f(attn_sink)[0],
        "hy_w_in": f(hy_w_in)[0], "hy_conv": f(hy_conv)[0], "hy_w_f1": f(hy_w_f1)[0], "hy_w_f2": f(hy_w_f2)[0],
        "hy_w_f3": f(hy_w_f3)[0],
        "hy_small": np.ascontiguousarray(np.stack([f(hy_b_f1)[0], f(hy_b_f2)[0], f(hy_freq)[0]], axis=1)),
        "hy_bias": f(hy_bias)[0], "hy_w_out": f(hy_w_out)[0],
        "ffn_w_up": f(ffn_w_up), "ffn_convT": np.ascontiguousarray(fc), "ffn_w_down": f(ffn_w_down),
    }
    for k, v in cst.items():
        shared["c_" + k] = v
    in_maps = []
    for b in range(8):
        m = dict(shared)
        m["xs"] = x_sample[b]
        m["xp"] = x_prompt[2 * b:2 * b + 2].reshape(512, D)
        m["ck"] = cache_k[b, 0].reshape(256, 512)
        m["cv"] = cache_v[b, 0].reshape(256, 512)
        m["cvec"] = np.ascontiguousarray(np.concatenate([c[b].reshape(16, 128), c_ctx.reshape(16, 128)], axis=0))
        in_maps.append(m)
    res = run_bass_kernel_spmd(nc, in_maps, core_ids=list(range(8)))
    R = res.results
    y_prompt = np.concatenate([R[b]["yp"].reshape(2, 256, D) for b in range(8)], axis=0)
    y_sample = np.stack([R[b]["ys"] for b in range(8)], axis=0)
    nk = np.concatenate([R[b]["nk"].reshape(2, 1, 256, 4, 128) for b in range(8)], axis=0)
    nv = np.concatenate([R[b]["nv"].reshape(2, 1, 256, 4, 128) for b in range(8)], axis=0)
    return (y_prompt.astype(np.float32), y_sample.astype(np.float32), nk.astype(np.float32), nv.astype(np.float32))
```

```python
from contextlib import ExitStack
import math
import numpy as np
import ml_dtypes
import concourse.bass as bass
import concourse.mybir as mybir
from concourse.bass_utils import run_bass_kernel_spmd

F32 = mybir.dt.float32
BF16 = mybir.dt.bfloat16
AF = mybir.ActivationFunctionType
ALU = mybir.AluOpType

D = 2048
NT = 4608
LS = 4096
LP = 256
DFF = 5632
SEQS = [(0, 4096), (4096, 4352), (4352, 4608)]
EPS = 1e-6


class Res:
    __slots__ = ("name", "w", "r", "multi")

    def __init__(self, name, multi=False):
        self.name = name
        self.w = {}
        self.r = {}
        self.multi = multi


def _merge(d, tok):
    k, v = tok
    if v > d.get(k, 0):
        d[k] = v


class Sched:
    CE = ("pe", "act", "dve", "pool")
    ALLQ = ("pe", "act", "dve", "pool", "sp")

    def __init__(self, nc, n_dma_sems=32):
        self.nc = nc
        self.prog = {e: [] for e in self.ALLQ}
        self.cnt = {e: 0 for e in self.CE}
        self.sem = {}
        self.dma_i = 0
        self.nd = n_dma_sems
        self.dma_val = [0] * n_dma_sems
        self.nsw = 32
        self.sw_i = 0
        self.sw_val = [0] * self.nsw
        self.waited = {e: {} for e in self.ALLQ}
        self.st = ExitStack()

    def sb(self, name, shape, dtype=F32):
        return self.st.enter_context(self.nc.sbuf_tensor(name, list(shape), dtype))

    def ps(self, name, shape, dtype=F32):
        return self.st.enter_context(self.nc.psum_tensor(name, list(shape), dtype))

    def op(self, eng, fn, reads=(), writes=(), dma=0):
        waits = {}
        for r in reads:
            for k, v in r.w.items():
                if v > waits.get(k, 0):
                    waits[k] = v
        for w in writes:
            for k, v in w.r.items():
                if v > waits.get(k, 0):
                    waits[k] = v
            if not (w.multi and not w.r):
                for k, v in w.w.items():
                    if v > waits.get(k, 0):
                        waits[k] = v
        if (not dma) and eng == "pe":
            waits.pop(("E", "pe"), None)
        sems = None
        if dma and eng == "pool":
            tok = {}
            sems = []
            for _ in range(dma):
                idx = self.sw_i % self.nsw
                self.sw_i += 1
                prev = self.sw_val[idx]
                if prev > waits.get(("S", idx), 0):
                    waits[("S", idx)] = prev
                self.sw_val[idx] = prev + 16
                tok[("S", idx)] = prev + 16
                sems.append(("S", idx))
        elif dma:
            idx = self.dma_i % self.nd
            self.dma_i += 1
            prev = self.dma_val[idx]
            if prev > waits.get(("D", idx), 0):
                waits[("D", idx)] = prev
            self.dma_val[idx] = prev + 16 * dma
            tok = {("D", idx): prev + 16 * dma}
            sems = [("D", idx)] * dma
        else:
            self.cnt[eng] += 1
            tok = {("E", eng): self.cnt[eng]}
        wl = []
        wd = self.waited[eng]
        for key, val in waits.items():
            if val <= 0 or wd.get(key, 0) >= val:
                continue
            wd[key] = val
            wl.append((key, val))
        self.prog[eng].append((wl, fn, tok, sems))
        for r in reads:
            for k, v in tok.items():
                if v > r.r.get(k, 0):
                    r.r[k] = v
        for w in writes:
            if w.multi and not w.r:
                for k, v in tok.items():
                    if v > w.w.get(k, 0):
                        w.w[k] = v
            else:
                w.w = dict(tok)
                w.r = {}
        return tok

    def dma(self, q, out, in_, reads=(), writes=()):
        return self.op(q, lambda e: [e.dma_start(out=out, in_=in_)], reads=reads, writes=writes, dma=1)

    def finish(self):
        wl = []
        for i in range(self.nd):
            if self.dma_val[i] > 0:
                wl.append((("D", i), self.dma_val[i]))
        for i in range(self.nsw):
            if self.sw_val[i] > 0:
                wl.append((("S", i), self.sw_val[i]))
        for e in self.CE:
            if self.cnt[e] > 0:
                wl.append((("E", e), self.cnt[e]))
        self.prog["sp"].append((wl, None, None, None))

    def emit(self):
        nc = self.nc
        st = self.st
        for e in self.CE:
            self.sem[("E", e)] = st.enter_context(nc.semaphore(f"s_{e}"))
        for i in range(self.nd):
            self.sem[("D", i)] = st.enter_context(nc.semaphore(f"d_{i}"))
        for i in range(self.nsw):
            self.sem[("S", i)] = st.enter_context(nc.semaphore(f"sw_{i}"))
        block = st.enter_context(nc.Block())
        sched = self

        def mk(engname):
            def body(e):
                for wl, fn, tok, sems in sched.prog[engname]:
                    for key, val in wl:
                        e.wait_ge(sched.sem[key], val)
                    if fn is None:
                        continue
                    ins = fn(e)
                    if sems is not None:
                        assert len(ins) == len(sems), (len(ins), len(sems))
                        for i, sk in zip(ins, sems):
                            i.then_inc(sched.sem[sk], 16)
                    else:
                        if isinstance(ins, (list, tuple)):
                            ins = ins[-1]
                        ins.then_inc(sched.sem[("E", engname)], 1)
            return body

        block.sync(mk("sp"))
        block.tensor(mk("pe"))
        block.scalar(mk("act"))
        block.vector(mk("dve"))
        block.gpsimd(mk("pool"))


_CONST = None


def _consts():
    global _CONST
    if _CONST is not None:
        return _CONST
    bf = ml_dtypes.bfloat16
    c = {}
    c["ident_b"] = np.eye(128, dtype=np.float32).astype(bf)
    c["ident_f"] = np.eye(128, dtype=np.float32)
    t = np.arange(LS)
    row = (t // 64).astype(np.float64)
    col = (t % 64).astype(np.float64)
    inv = 10000.0 ** (-np.arange(32, dtype=np.float64) / 32)
    C = np.ones((128, NT), np.float64)
    Sg = np.zeros((128, NT), np.float64)
    perm = np.zeros((128, 128), np.float32)
    for d in range(128):
        half, e = d // 64, d % 64
        j, first = e % 32, e < 32
        ang = (row if half == 0 else col) * inv[j]
        C[d, :LS] = np.cos(ang)
        Sg[d, :LS] = -np.sin(ang) if first else np.sin(ang)
        perm[d + 32 if first else d - 32, d] = 1.0
    c["ropeC"] = C.astype(np.float32).astype(bf)
    c["ropeS"] = Sg.astype(np.float32).astype(bf)
    c["perm"] = perm.astype(bf)
    kl = np.arange(128)[:, None]
    ql = np.arange(128)[None, :]
    m = np.stack([np.tile((kl >= ql), (1, 4)), np.tile((kl <= ql), (1, 4))], axis=1)
    c["bmask"] = m.astype(np.float32).astype(bf)
    for nm, L in (("S", LS), ("P", LP)):
        N = 2 * L
        tt = np.arange(L, dtype=np.float64)[:, None]
        kk = np.arange(L, dtype=np.float64)[None, :]
        th = 2.0 * np.pi * ((tt * (kk + 0.5)) % N) / N
        fwd = np.concatenate([np.cos(th), np.sin(th)], axis=1)
        CCf = L // 128
        inv = fwd.T * (2.0 / N)
        c["fwd" + nm] = np.ascontiguousarray(
            fwd.reshape(CCf, 128, 2 * CCf, 128).transpose(2, 1, 0, 3)).astype(np.float32).astype(bf)
        c["inv" + nm] = np.ascontiguousarray(
            inv.reshape(2 * CCf, 128, CCf, 128).transpose(2, 1, 0, 3)).astype(np.float32).astype(bf)
        tl = np.linspace(0.0, 1.0, L, dtype=np.float32)[:, None]
        w = (2.0 * np.pi * np.arange(L, dtype=np.float32)[:, None] / L).astype(np.float32)
        f = np.linspace(1e-4, 15, 16, dtype=np.float32)[None, :]
        feat = np.concatenate([tl, np.cos(f * w), -np.sin(f * w)], axis=-1).astype(np.float32)
        c["featT" + nm] = np.ascontiguousarray(feat.T)
        deltas = np.linspace(math.log(1e-2) / 1.5, math.log(1e-2) / 0.3, D, dtype=np.float32)
        dec = np.exp(-tl * np.abs(deltas)[None, :]).astype(np.float32)
        decb = dec.copy()
        decb[0, :] = 0.0
        c["decf" + nm] = dec
        c["decb" + nm] = decb
    N = 2 * LS
    a_ = np.arange(128, dtype=np.float64)[:, None]
    k1 = np.arange(128, dtype=np.float64)[None, :]
    phi = 2.0 * np.pi * a_ * (k1 + 0.5) / 256.0
    c["f1c"] = np.cos(phi).astype(np.float32).astype(bf)
    c["f1s"] = np.sin(phi).astype(np.float32).astype(bf)
    c["i1c"] = (np.cos(phi).T * (2.0 / N)).astype(np.float32).astype(bf)
    c["i1s"] = (-np.sin(phi).T * (2.0 / N)).astype(np.float32).astype(bf)
    r_ = np.arange(32, dtype=np.float64)[None, :]
    psi = 2.0 * np.pi * r_ * (np.arange(128, dtype=np.float64)[:, None] + 0.5) / N
    c["tw1"] = np.concatenate([np.cos(psi), np.sin(psi)], axis=1).astype(np.float32)
    chi = 2.0 * np.pi * np.outer(np.arange(32), np.arange(32)) / 32.0
    cbm = np.kron(np.eye(4), np.cos(chi))
    sbm = np.kron(np.eye(4), np.sin(chi))
    c["cbm"] = cbm.astype(np.float32).astype(bf)
    c["sbm"] = sbm.astype(np.float32).astype(bf)
    c["ncbm"] = (-cbm).astype(np.float32).astype(bf)
    c["nsbm"] = (-sbm).astype(np.float32).astype(bf)
    q_ = (np.arange(128) // 32).astype(np.float64)[:, None]
    rr2 = (np.arange(128) % 32).astype(np.float64)[:, None]
    j_ = np.arange(32, dtype=np.float64)[None, :]
    psi2 = 2.0 * np.pi * rr2 * (4.0 * j_ + q_ + 0.5) / N
    c["tw2"] = np.concatenate([np.cos(psi2), np.sin(psi2)], axis=1).astype(np.float32)
    _CONST = c
    return c


import os
DBG_STOP = int(os.environ.get("MK_STOP", "999"))
DBG_OUT = [x for x in os.environ.get("MK_OUT", "").split(",") if x]


class _Stop(Exception):
    pass


def build():
    nc = bass.Bass("TRN2", target_bir_lowering=False)
    S = Sched(nc)
    cst = _consts()
    IN = {}

    def din(name, shape, dt=F32):
        IN[name] = nc.dram_tensor(name, list(shape), dt, kind="ExternalInput").ap()
        return IN[name]

    def dscr(name, shape, dt):
        return nc.dram_tensor(name, list(shape), dt, kind=("ExternalOutput" if name in DBG_OUT else "Internal")).ap()

    xs = din("xs", [LS, D]); xp = din("xp", [512, D])
    ck = din("ck", [256, 512]); cv = din("cv", [256, 512])
    cvec = din("cvec", [32, 128])
    w_mod = din("w_mod", [2, D, 6 * D]); smallv = din("smallv", [2, 128, 128])
    norm_final = din("norm_final", [D])
    w_qkv = din("w_qkv", [D, 3072]); w_o = din("w_o", [D, D]); sink = din("sink", [16])
    hy_w_in = din("hy_w_in", [D, 3 * D]); hy_conv = din("hy_conv", [3, 3 * D])
    hy_w_f1 = din("hy_w_f1", [33, 64]); hy_w_f2 = din("hy_w_f2", [64, 64]); hy_w_f3 = din("hy_w_f3", [64, 4 * D])
    hy_small = din("hy_small", [64, 3])
    hy_bias = din("hy_bias", [2, D]); hy_w_out = din("hy_w_out", [D, D])
    ffn_w_up = din("ffn_w_up", [2, D, 2 * DFF]); ffn_convT = din("ffn_convT", [2, 132, 128])
    ffn_w_down = din("ffn_w_down", [2, DFF, D])
    for k, v in cst.items():
        din("c_" + k, v.shape, BF16 if v.dtype != np.float32 else F32)

    yp = nc.dram_tensor("yp", [512, D], F32, kind="ExternalOutput").ap()
    ys = nc.dram_tensor("ys", [LS, D], F32, kind="ExternalOutput").ap()
    nk = nc.dram_tensor("nk", [512, 512], F32, kind="ExternalOutput").ap()
    nv = nc.dram_tensor("nv", [512, 512], F32, kind="ExternalOutput").ap()

    hT = dscr("hT", [D, NT], BF16); hT_r = Res("hT", True)
    qT = dscr("qT", [2560, NT], BF16); qT_r = Res("qT", True)
    vtm = dscr("vtm", [NT, 512], BF16); vtm_r = Res("vtm", True)
    oT = dscr("oT", [36, 128, 16, 128], BF16); oT_r = Res("oT", True)
    hTt = dscr("hTt", [36, 128, 16, 128], BF16); hTt_r = Res("hTt", True)
    xres = [dscr(f"xres{i}", [NT, D], F32) for i in range(4)]
    xres_r = [Res(f"xres{i}", True) for i in range(4)]
    abT = dscr("abT", [2 * DFF, NT], BF16); abT_r = Res("abT", True)
    gT = dscr("gT", [DFF, NT], BF16); gT_r = Res("gT", True)
    gate_scr = dscr("gate_scr", [4, 128, D], F32); gate_r = Res("gate", True)
    ptm = dscr("ptm", [NT, 3 * D], BF16); ptm_r = Res("ptm", True)
    c3 = dscr("c3", [NT, 3 * D], BF16); c3_r = Res("c3", True)
    hsd = dscr("hsd", [2, LS, 2 * D], BF16); hsd_r = Res("hsd", True)
    ghat = dscr("ghat", [2, LS, 2 * D], BF16); ghat_r = Res("ghat", True)
    yhat = dscr("yhat", [2, 2 * LS, D], BF16); yhat_r = Res("yhat", True)
    z1 = dscr("z1", [NT, D], BF16); z1_r = Res("z1", True)
    zz = dscr("zz", [NT, D], BF16); zz_r = Res("zz", True)
    in_r = Res("inputs")

    BB = S.sb("BB", [128, 35200], BF16); BB_r = Res("BB")
    WA = [S.sb(f"WA{i}", [128, 12288], BF16) for i in range(2)]; WA_r = [Res(f"WA{i}") for i in range(2)]
    XT = [S.sb(f"XT{i}", [128, D], F32) for i in range(2)]; XT_r = [Res(f"XT{i}") for i in range(2)]
    XN = S.sb("XN", [128, D], BF16); XN_r = Res("XN")
    JK = S.sb("JK", [128, D], BF16); JK_r = Res("JK")
    HTS = S.sb("HTS", [128, 16, 256], BF16); HTS_r = Res("HTS")
    STG = [S.sb(f"STG{i}", [128, 512], F32) for i in range(6)]; STG_r = [Res(f"STG{i}") for i in range(6)]
    STX = [S.sb(f"STX{i}", [128, 512], F32) for i in range(4)]; STX_r = [Res(f"STX{i}") for i in range(4)]
    SB16 = [S.sb(f"SB16{i}", [128, 512], BF16) for i in range(4)]; SB16_r = [Res(f"SB16{i}") for i in range(4)]
    ACC = S.sb("ACC", [128, NT], F32); ACC_r = Res("ACC")
    idb = S.sb("idb", [128, 128], BF16); idf = S.sb("idf", [128, 128], F32); perm = S.sb("perm", [128, 128], BF16)
    cst_r = Res("consts")
    colsT = S.sb("colsT", [128, 128], F32); colsT_r = Res("colsT")
    cT = S.sb("cT", [128, 32], BF16); cT_r = Res("cT")
    cbc = HTS[:].rearrange("p c t -> p (c t)").rearrange("p (a b) -> p a b", a=32); cbc_r = HTS_r
    modc = S.sb("modc", [128, 2, 96], F32); modc_r = Res("modc")
    gsc = S.sb("gsc", [128, 2, 2, 16], F32); gsc_r = Res("gsc")
    sm = S.sb("sm", [128, 8], F32); sm_r = Res("sm")
    esink = S.sb("esink", [128, 16], F32); esink_r = Res("esink")
    fcv = S.sb("fcv", [128, 3, 44], F32); fcv_r = Res("fcv")
    stg_s = S.sb("stg_s", [128, 128], F32); stg_r = Res("stg_s")
    banks = [S.ps(f"bank{i}", [128, 512], F32) for i in range(8)]
    bank_r = [Res(f"bank{i}") for i in range(8)]
    bi = [0]

    def sub(parent, name):
        r = Res(name)
        r.w = dict(parent.w)
        r.r = dict(parent.r)
        return r

    def join(parent, subs):
        for rr_ in subs:
            for k, v in list(rr_.w.items()) + list(rr_.r.items()):
                if v > parent.r.get(k, 0):
                    parent.r[k] = v

    nbn = [8]

    def nb():
        i = bi[0] % nbn[0]
        bi[0] += 1
        return banks[i], bank_r[i]

    rr = {"stg": 0, "sb16": 0, "wa": 0, "xt": 0, "ev": 0}

    def rot(name, n):
        i = rr[name] % n
        rr[name] += 1
        return i

    def evq():
        return "act" if rot("ev", 2) == 0 else "dve"

    def copy(q, out, in_, reads, writes):
        if q == "act":
            return S.op("act", lambda e: e.activation(out=out, in_=in_, func=AF.Copy), reads=reads, writes=writes)
        return S.op(q, lambda e: e.tensor_copy(out=out, in_=in_), reads=reads, writes=writes)

    S.dma("sp", idb[:], IN["c_ident_b"], writes=[cst_r])
    S.dma("sp", idf[:], IN["c_ident_f"], writes=[cst_r])
    S.dma("sp", perm[:], IN["c_perm"], writes=[cst_r])

    def x_src(layer_in):
        def f(ti):
            if layer_in is None:
                return (xs[ti * 128:(ti + 1) * 128, :] if ti < 32 else xp[(ti - 32) * 128:(ti - 31) * 128, :]), in_r
            return xres[layer_in][ti * 128:(ti + 1) * 128, :], xres_r[layer_in]
        return f

    def mod_phase(i):
        S.dma("sp", stg_s[:], smallv[i], reads=[in_r], writes=[stg_r])
        b, br = nb()
        S.op("pe", lambda e: e.transpose(b[:, 0:128], stg_s[:], idf[:]), reads=[stg_r, cst_r], writes=[br])
        copy("act", colsT[:], b[:, 0:128], [br], [colsT_r])
        S.dma("sp", stg_s[0:32, :], cvec, reads=[in_r], writes=[stg_r])
        S.op("act", lambda e: e.activation(out=stg_s[0:32, :], in_=stg_s[0:32, :], func=AF.Silu),
             reads=[stg_r], writes=[stg_r])
        b2, b2r = nb()
        S.op("pe", lambda e: e.transpose(b2[:, 0:32], stg_s[0:32, :], idf[0:32, 0:32]), reads=[stg_r, cst_r], writes=[b2r])
        copy("act", cT[:], b2[:, 0:32], [b2r], [cT_r])

        def mkbc(e):
            last = None
            for c in range(32):
                last = e.tensor_copy(out=cbc[:, c, :], in_=cT[:, c:c + 1].to_broadcast([128, 128]))
            return last
        S.op("dve", mkbc, reads=[cT_r], writes=[cbc_r])
        mb, mbr = banks[7], bank_r[7]
        nbn[0] = 7
        for blk in range(24):
            wi = rot("wa", 2)
            wv = WA[wi][:, 0:8192].rearrange("p (c w) -> p c w", c=16)

            def ldw(e, blk=blk, wv=wv):
                return [e.dma_start(out=wv[:, c4 * 4:(c4 + 1) * 4, :],
                                    in_=w_mod[i, c4 * 512:(c4 + 1) * 512, blk * 512:(blk + 1) * 512]
                                    .rearrange("(c p) n -> p c n", p=128)) for c4 in range(4)]
            S.op("pool", ldw, reads=[in_r], writes=[WA_r[wi]], dma=4)

            def mm(e, blk=blk, wv=wv):
                last = None
                for n in range(4):
                    ch = blk * 4 + n
                    for k in range(16):
                        last = e.matmul(mb[:, ch * 2:ch * 2 + 2], wv[:, k, n * 128:(n + 1) * 128],
                                        cT[:, k:32:16], start=(k == 0), stop=(k == 15))
                return last
            S.op("pe", mm, reads=[WA_r[wi], cT_r], writes=[mbr])
            which = {2: 0, 5: 1}.get(blk // 4)
            if which is not None:
                cb = (blk % 4) * 512
                for r in range(2):
                    gb, gbr = nb()

                    def mg(e, wv=wv, r=r, gb=gb):
                        last = None
                        for k in range(16):
                            last = e.matmul(gb[:], cbc[:, r * 16 + k, :], wv[:, k, :], start=(k == 0), stop=(k == 15))
                        return last
                    S.op("pe", mg, reads=[WA_r[wi], cbc_r], writes=[gbr])
                    si = rot("stg", 6)
                    s2 = rot("stg", 6)
                    S.dma("sp", STG[s2][:], smallv[i, (blk * 4):(blk * 4 + 4), :].rearrange("a b -> (a b)")
                          .partition_broadcast(128), reads=[in_r], writes=[STG_r[s2]])
                    S.op("dve", lambda e, gb=gb, si=si, s2=s2: e.tensor_tensor(
                        out=STG[si][:], in0=gb[:], in1=STG[s2][:], op=ALU.add),
                        reads=[gbr, STG_r[s2]], writes=[STG_r[si]])
                    S.dma("sp", gate_scr[r * 2 + which, :, cb:cb + 512], STG[si][:],
                          reads=[STG_r[si]], writes=[gate_r])
        nbn[0] = 8
        for r in range(2):
            S.op("dve", lambda e, r=r: e.tensor_tensor(out=modc[:, r, :], in0=mb[:, r:192:2], in1=colsT[:, 0:96],
                                                       op=ALU.add), reads=[mbr, colsT_r], writes=[modc_r])
        for r in range(2):
            for wh in range(2):
                S.op("dve", lambda e, r=r, wh=wh: e.scalar_tensor_tensor(
                    out=gsc[:, r, wh, :], in0=modc[:, r, wh * 48 + 16:wh * 48 + 32], scalar=1.0,
                    in1=colsT[:, 96 + wh * 16:112 + wh * 16], op0=ALU.add, op1=ALU.mult),
                    reads=[modc_r, colsT_r], writes=[gsc_r])

    def norm_T(src, wh, plain_src=None):
        for ti in range(36):
            r = 0 if ti < 32 else 1
            if plain_src is None:
                xi = rot("xt", 2)
                ap, res = src(ti)
                S.dma("sp", XT[xi][:], ap, reads=[res], writes=[XT_r[xi]])
                S.op("act", lambda e, xi=xi: e.activation(out=JK[:], in_=XT[xi][:], func=AF.Square,
                                                          accum_out=sm[:, 0:1]),
                     reads=[XT_r[xi]], writes=[JK_r, sm_r])
                S.op("dve", lambda e: e.tensor_scalar(out=sm[:, 1:2], in0=sm[:, 0:1], scalar1=1.0 / D, scalar2=EPS,
                                                      op0=ALU.mult, op1=ALU.add), reads=[sm_r], writes=[sm_r])
                S.op("act", lambda e: e.activation(out=sm[:, 2:3], in_=sm[:, 1:2], func=AF.Sqrt), reads=[sm_r], writes=[sm_r])
                S.op("dve", lambda e: e.reciprocal(out=sm[:, 3:4], in_=sm[:, 2:3]), reads=[sm_r], writes=[sm_r])
                S.op("dve", lambda e, xi=xi: e.tensor_scalar(out=XN[:], in0=XT[xi][:], scalar1=sm[:, 3:4], scalar2=None,
                                                            op0=ALU.mult), reads=[XT_r[xi], sm_r], writes=[XN_r])
            else:
                pa, pr = plain_src
                S.dma("sp", XN[:], pa[ti * 128:(ti + 1) * 128, :], reads=[pr], writes=[XN_r])
            half = ti % 2
            for g in range(2):
                b, br = nb()
                bv = b[:].bitcast(BF16)

                def tp(e, g=g, bv=bv):
                    last = None
                    for j in range(8):
                        c = g * 8 + j
                        last = e.transpose(bv[:, j * 128:(j + 1) * 128], XN[:, c * 128:(c + 1) * 128], idb[:])
                    return last
                S.op("pe", tp, reads=[XN_r, cst_r], writes=[br])

                def ev(e, g=g, bv=bv, r=r, half=half):
                    last = None
                    for j in range(8):
                        c = g * 8 + j
                        o = HTS[:, c, half * 128:(half + 1) * 128]
                        if plain_src is None:
                            last = e.activation(out=o, in_=bv[:, j * 128:(j + 1) * 128], func=AF.Identity,
                                                scale=gsc[:, r, wh, c:c + 1],
                                                bias=modc[:, r, wh * 48 + c:wh * 48 + c + 1])
                        else:
                            last = e.activation(out=o, in_=bv[:, j * 128:(j + 1) * 128], func=AF.Copy)
                    return last
                S.op("act", ev, reads=[br, gsc_r, modc_r], writes=[HTS_r])
            S.dma("sp", hTt[ti], HTS[:, :, half * 128:(half + 1) * 128], reads=[HTS_r], writes=[hTt_r])
            if half == 1:
                t0 = (ti - 1) * 128

                def st(e, t0=t0):
                    return [e.dma_start(out=hT[q * 512:(q + 1) * 512, t0:t0 + 256].rearrange("(c p) t -> p c t", p=128),
                                        in_=HTS[:, q * 4:(q + 1) * 4, :]) for q in range(4)]
                S.op("sp", st, reads=[HTS_r], writes=[hT_r], dma=4)

    BLKS = [(0, 2048), (2048, 4096), (4096, 4608)]

    def gemm_fm(W, ncols, src, src_r, KC, post):
        for (c0, c1) in BLKS:
            wd = c1 - c0
            bv = BB[:, 0:KC * wd].rearrange("p (c w) -> p c w", c=KC)

            def ldb(e, bv=bv, c0=c0, c1=c1):
                return [e.dma_start(out=bv[:, q * 4:(q + 1) * 4, :],
                                    in_=src[q * 512:(q + 1) * 512, c0:c1].rearrange("(c p) t -> p c t", p=128))
                        for q in range(KC // 4)]
            S.op("sp", ldb, reads=[src_r], writes=[BB_r], dma=KC // 4)
            for cg in range(ncols // 256):
                wi = rot("wa", 2)
                wv = WA[wi][:, 0:KC * 256].rearrange("p (c w) -> p c w", c=KC)

                def ldw(e, wv=wv, cg=cg):
                    return [e.dma_start(out=wv[:, q * 4:(q + 1) * 4, :],
                                        in_=W[q * 512:(q + 1) * 512, cg * 256:(cg + 1) * 256]
                                        .rearrange("(c p) n -> p c n", p=128)) for q in range(KC // 4)]
                S.op("pool", ldw, reads=[in_r], writes=[WA_r[wi]], dma=KC // 4)
                for n in range(2):
                    ci = cg * 2 + n
                    for p0 in range(0, wd, 512):
                        b, br = nb()

                        def mm(e, wv=wv, bv=bv, n=n, p0=p0, b=b):
                            last = None
                            for k in range(KC):
                                last = e.matmul(b[:], wv[:, k, n * 128:(n + 1) * 128], bv[:, k, p0:p0 + 512],
                                                start=(k == 0), stop=(k == KC - 1))
                            return last
                        S.op("pe", mm, reads=[WA_r[wi], BB_r], writes=[br])
                        post(ci, c0 + p0, b, br)

    def post_store_fm(dst, dst_r, rowoff=0):
        def post(ci, t0, b, br):
            si = rot("sb16", 4)
            copy(evq(), SB16[si][:], b[:], [br], [SB16_r[si]])
            S.dma("sp", dst[rowoff + ci * 128:rowoff + (ci + 1) * 128, t0:t0 + 512], SB16[si][:],
                  reads=[SB16_r[si]], writes=[dst_r])
        return post

    def gemm_tm(src, src_r, KC, W, ncols, post, tiles=range(36), tiled=False):
        tiles = list(tiles)
        for jb in range(ncols // 512):
            bv = BB[:, 0:KC * 512].rearrange("p (c w) -> p c w", c=KC)

            def ldb(e, bv=bv, jb=jb):
                return [e.dma_start(out=bv[:, q * 4:(q + 1) * 4, :],
                                    in_=W[q * 512:(q + 1) * 512, jb * 512:(jb + 1) * 512]
                                    .rearrange("(c p) n -> p c n", p=128)) for q in range(KC // 4)]
            S.op("pool", ldb, reads=[in_r], writes=[BB_r], dma=KC // 4)
            for t2 in range(0, len(tiles), 2):
                pair = tiles[t2:t2 + 2]
                wi = rot("wa", 2)
                wv = WA[wi][:, 0:KC * 256].rearrange("p (c w) -> p c w", c=KC)
                t0 = pair[0] * 128

                def lda(e, wv=wv, t0=t0, n=len(pair)):
                    return [e.dma_start(out=wv[:, q * 4:(q + 1) * 4, 0:n * 128],
                                        in_=src[q * 512:(q + 1) * 512, t0:t0 + n * 128]
                                        .rearrange("(c p) t -> p c t", p=128)) for q in range(KC // 4)]
                if tiled:
                    wts_ = [WA[wi][:, pi_ * KC * 128:(pi_ + 1) * KC * 128].rearrange("p (c t) -> p c t", c=KC) for pi_ in range(2)]

                    def lda(e, wts_=wts_, pair=pair):
                        return [e.dma_start(out=wts_[pi_], in_=src[ti_]) for pi_, ti_ in enumerate(pair)]
                    S.op("pool", lda, reads=[src_r], writes=[WA_r[wi]], dma=len(pair))
                else:
                    S.op("pool", lda, reads=[src_r], writes=[WA_r[wi]], dma=KC // 4)
                for pi, ti in enumerate(pair):
                    b, br = nb()

                    def mm(e, wv=wv, bv=bv, pi=pi, b=b, wt_=(wts_[pi] if tiled else None)):
                        last = None
                        for k in range(KC):
                            lh = wt_[:, k, :] if wt_ is not None else wv[:, k, pi * 128:(pi + 1) * 128]
                            last = e.matmul(b[:], lh, bv[:, k, :], start=(k == 0), stop=(k == KC - 1))
                        return last
                    S.op("pe", mm, reads=[WA_r[wi], BB_r], writes=[br])
                    post(ti, jb * 512, b, br)

    def post_residual(xin, which, xout, xout_r, final=False):
        def post(ti, j0, b, br):
            r = 0 if ti < 32 else 1
            g = rot("stg", 6)
            S.dma("sp", STG[g][:], gate_scr[r * 2 + which, :, j0:j0 + 512], reads=[gate_r], writes=[STG_r[g]])
            xi = rot("stg", 6)
            ap, res = xin(ti)
            S.dma("sp", STG[xi][:], ap[:, j0:j0 + 512], reads=[res], writes=[STG_r[xi]])
            S.op("dve", lambda e: e.tensor_tensor(out=STG[g][:], in0=b[:], in1=STG[g][:], op=ALU.mult),
                 reads=[br, STG_r[g]], writes=[STG_r[g]])
            S.op("dve", lambda e: e.tensor_tensor(out=STG[xi][:], in0=STG[xi][:], in1=STG[g][:], op=ALU.add),
                 reads=[STG_r[g], STG_r[xi]], writes=[STG_r[xi]])
            S.dma("sp", xout[ti * 128:(ti + 1) * 128, j0:j0 + 512], STG[xi][:], reads=[STG_r[xi]], writes=[xout_r])
        return post

    def attention():
        KT = BB[:, 0:4 * 4352].rearrange("p (h t) -> p h t", h=4)
        VV = BB[:, 17408:17408 + 34 * 512].rearrange("p (b c) -> p b c", b=34)
        PKT = WA[0][:, 0:2048].rearrange("p (h t) -> p h t", h=4)
        PVV = WA[0][:, 2048:4096].rearrange("p (b c) -> p b c", b=4)
        RC = WA[1][:, 0:4608]
        RS = WA[1][:, 4608:9216]
        MK = WA[1][:, 9216:10240].rearrange("p (a q) -> p a q", a=2)
        S.dma("sp", RC, IN["c_ropeC"], writes=[WA_r[1]])
        S.dma("sp", RS, IN["c_ropeS"], writes=[WA_r[1]])
        S.dma("sp", MK, IN["c_bmask"], writes=[WA_r[1]])
        S.dma("sp", esink[:], sink.partition_broadcast(128), reads=[in_r], writes=[esink_r])
        S.op("act", lambda e: e.activation(out=esink[:], in_=esink[:], func=AF.Exp), reads=[esink_r], writes=[esink_r])
        for h in range(4):
            S.dma("sp", KT[:, h, 0:4096], qT[2048 + h * 128:2048 + (h + 1) * 128, 0:4096], reads=[qT_r], writes=[BB_r])
            S.dma("sp", PKT[:, h, :], qT[2048 + h * 128:2048 + (h + 1) * 128, 4096:4608], reads=[qT_r], writes=[WA_r[0]])
        for h in range(4):
            for p0 in range(0, 4096, 512):
                b, br = nb()
                S.op("pe", lambda e, b=b, h=h, p0=p0: e.matmul(b[:], perm[:], KT[:, h, p0:p0 + 512], start=True, stop=True),
                     reads=[BB_r, cst_r], writes=[br])
                si = rot("stg", 6)
                S.op("dve", lambda e, b=b, si=si, p0=p0: e.tensor_tensor(out=STG[si][:], in0=b[:], in1=RS[:, p0:p0 + 512],
                                                                      op=ALU.mult), reads=[br, WA_r[1]], writes=[STG_r[si]])
                s2 = rot("stg", 6)
                S.op("pool", lambda e, s2=s2, h=h, p0=p0: e.tensor_tensor(out=STG[s2][:], in0=KT[:, h, p0:p0 + 512],
                                                                         in1=RC[:, p0:p0 + 512], op=ALU.mult),
                     reads=[BB_r, WA_r[1]], writes=[STG_r[s2]])
                S.op("dve", lambda e, si=si, s2=s2, h=h, p0=p0: e.tensor_tensor(out=KT[:, h, p0:p0 + 512], in0=STG[si][:],
                                                                               in1=STG[s2][:], op=ALU.add),
                     reads=[STG_r[si], STG_r[s2]], writes=[BB_r])
        for blk in range(2):
            xi = rot("xt", 2)
            S.dma("sp", XT[xi][:, 0:512], ck[blk * 128:(blk + 1) * 128, :], reads=[in_r], writes=[XT_r[xi]])
            S.dma("sp", XT[xi][:, 512:1024], cv[blk * 128:(blk + 1) * 128, :], reads=[in_r], writes=[XT_r[xi]])
            copy("dve", VV[:, 32 + blk, :], XT[xi][:, 512:1024], [XT_r[xi]], [BB_r])
            b, br = nb()

            def tp(e, b=b, xi=xi):
                last = None
                for h in range(4):
                    last = e.transpose(b[:, h * 128:(h + 1) * 128], XT[xi][:, h * 128:(h + 1) * 128], idf[:])
                return last
            S.op("pe", tp, reads=[XT_r[xi], cst_r], writes=[br])
            for h in range(4):
                copy("act", KT[:, h, 4096 + blk * 128:4096 + (blk + 1) * 128], b[:, h * 128:(h + 1) * 128], [br], [BB_r])
        S.dma("sp", VV[:, 0:32, :], vtm[0:4096, :].rearrange("(b p) c -> p b c", p=128), reads=[vtm_r], writes=[BB_r])
        S.dma("sp", PVV, vtm[4096:4608, :].rearrange("(b p) c -> p b c", p=128), reads=[vtm_r], writes=[WA_r[0]])
        QBs = [WA[0][:, 4096 + i * 512:4608 + i * 512].rearrange("p (g q) -> p g q", g=4) for i in range(2)]
        QB_r = [sub(WA_r[0], "QB0"), sub(WA_r[0], "QB1")]
        ones = WA[0][:, 5120:5248]
        ones_r = sub(WA_r[0], "ones")
        PTs = [WA[0][:, 5248 + i * 512:5760 + i * 512] for i in range(12)]
        PT_r = [sub(WA_r[0], f"PT{i}") for i in range(12)]
        pti = [0]
        S.op("dve", lambda e: e.memset(ones, 1.0), writes=[ones_r])
        scale = 128 ** -0.5
        it = 0
        for qb in range(36):
            t0 = qb * 128
            if qb < 32:
                kblocks = [("w", kb) for kb in (qb - 1, qb, qb + 1) if 0 <= kb < 32] + [("c", 0), ("c", 1)]
            else:
                s0 = 32 + ((qb - 32) // 2) * 2
                kblocks = [("p", s0 - 32), ("p", s0 - 31)]
            for g in range(4):
                QB = QBs[it % 2]
                qr = QB_r[it % 2]
                it += 1
                QBf = QB.rearrange("p g q -> p (g q)")
                S.dma("pool", QB, qT[g * 512:(g + 1) * 512, t0:t0 + 128].rearrange("(g p) t -> p g t", p=128),
                      reads=[qT_r], writes=[qr])
                if qb < 32:
                    b, br = nb()
                    S.op("pe", lambda e, b=b, QBf=QBf: e.matmul(b[:], perm[:], QBf, start=True, stop=True),
                         reads=[qr, cst_r], writes=[br])
                    si = rot("stg", 6)
                    s2 = rot("stg", 6)

                    def rp(e, b=b, si=si, s2=s2, QB=QB, t0=t0):
                        last = None
                        for gg in range(4):
                            e.tensor_tensor(out=STG[si][:, gg * 128:(gg + 1) * 128], in0=b[:, gg * 128:(gg + 1) * 128],
                                            in1=RS[:, t0:t0 + 128], op=ALU.mult)
                            last = e.tensor_tensor(out=STG[s2][:, gg * 128:(gg + 1) * 128], in0=QB[:, gg, :],
                                                   in1=RC[:, t0:t0 + 128], op=ALU.mult)
                        return last
                    S.op("dve", rp, reads=[br, qr, WA_r[1]], writes=[STG_r[si], STG_r[s2]])
                    S.op("dve", lambda e, si=si, s2=s2, QBf=QBf: e.tensor_tensor(out=QBf, in0=STG[si][:], in1=STG[s2][:], op=ALU.add),
                         reads=[STG_r[si], STG_r[s2]], writes=[qr])
                nkb = len(kblocks)
                ops_ = []
                for ki, (kind, kb) in enumerate(kblocks):
                    if kind == "w":
                        kt = KT[:, g, kb * 128:(kb + 1) * 128]; vv = VV[:, kb, g * 128:(g + 1) * 128]
                    elif kind == "c":
                        kt = KT[:, g, 4096 + kb * 128:4096 + (kb + 1) * 128]; vv = VV[:, 32 + kb, g * 128:(g + 1) * 128]
                    else:
                        kt = PKT[:, g, kb * 128:(kb + 1) * 128]; vv = PVV[:, kb, g * 128:(g + 1) * 128]
                    pidx = pti[0] % 12
                    pti[0] += 1
                    PT, ptr_ = PTs[pidx], PT_r[pidx]
                    sb_, sbr = nb()
                    S.op("pe", lambda e, sb_=sb_, kt=kt, QBf=QBf: e.matmul(sb_[:], kt, QBf, start=True, stop=True),
                         reads=[BB_r, WA_r[0], qr], writes=[sbr])
                    S.op("act", lambda e, sb_=sb_, PT=PT: e.activation(out=PT, in_=sb_[:], func=AF.Exp, scale=scale),
                         reads=[sbr], writes=[ptr_])
                    if kind == "w" and kb != qb:
                        mi = 0 if kb < qb else 1
                        S.op("dve", lambda e, PT=PT, mi=mi: e.tensor_tensor(out=PT, in0=PT, in1=MK[:, mi, :], op=ALU.mult),
                             reads=[ptr_, WA_r[1]], writes=[ptr_])
                    ops_.append((vv, PT, ptr_))
                ob, obr = nb()
                db, dbr = nb()
                for ki, (vv, PT, ptr_) in enumerate(ops_):
                    S.op("pe", lambda e, ob=ob, vv=vv, PT=PT, ki=ki, nkb=nkb: e.matmul(ob[:], vv, PT, start=(ki == 0),
                                                                                   stop=(ki == nkb - 1)),
                         reads=[BB_r, WA_r[0], ptr_], writes=[obr])
                    S.op("pe", lambda e, db=db, PT=PT, ki=ki, nkb=nkb: e.matmul(db[:], ones, PT, start=(ki == 0),
                                                                            stop=(ki == nkb - 1)),
                         reads=[ones_r, ptr_], writes=[dbr])
                si = rot("stg", 6)

                def dn(e, db=db, si=si, g=g):
                    last = None
                    for gg in range(4):
                        last = e.tensor_scalar(out=STG[si][:, gg * 128:(gg + 1) * 128], in0=db[:, gg * 128:(gg + 1) * 128],
                                               scalar1=esink[:, g * 4 + gg:g * 4 + gg + 1], scalar2=None, op0=ALU.add)
                    return last
                S.op("dve", dn, reads=[dbr, esink_r], writes=[STG_r[si]])
                S.op("act", lambda e, si=si: e.activation(out=STG[si][:], in_=STG[si][:], func=AF.Ln), reads=[STG_r[si]], writes=[STG_r[si]])
                S.op("act", lambda e, si=si: e.activation(out=STG[si][:], in_=STG[si][:], func=AF.Exp, scale=-1.0),
                     reads=[STG_r[si]], writes=[STG_r[si]])
                oi = rot("sb16", 4)
                S.op("dve", lambda e, ob=ob, si=si, oi=oi: e.tensor_tensor(out=SB16[oi][:], in0=ob[:], in1=STG[si][:],
                                                                          op=ALU.mult),
                     reads=[obr, STG_r[si]], writes=[SB16_r[oi]])
                S.dma("sp", oT[qb][:, g * 4:(g + 1) * 4, :],
                      SB16[oi][:].rearrange("p (g q) -> p g q", g=4), reads=[SB16_r[oi]], writes=[oT_r])
        join(WA_r[0], QB_r + PT_r + [ones_r])

    def ffn_act(i):
        S.dma("sp", stg_s[:], ffn_convT[i, 0:128, :], reads=[in_r], writes=[stg_r])
        b, br = nb()
        S.op("pe", lambda e: e.transpose(b[:, 0:128], stg_s[:], idf[:]), reads=[stg_r, cst_r], writes=[br])
        copy("act", fcv[:].rearrange("p a c -> p (a c)")[:, 0:128], b[:, 0:128], [br], [fcv_r])
        S.dma("sp", stg_s[0:4, :], ffn_convT[i, 128:132, :], reads=[in_r], writes=[stg_r])
        b2, b2r = nb()
        S.op("pe", lambda e: e.transpose(b2[:, 0:4], stg_s[0:4, :], idf[0:4, 0:4]), reads=[stg_r, cst_r], writes=[b2r])
        copy("act", fcv[:].rearrange("p a c -> p (a c)")[:, 128:132], b2[:, 0:4], [b2r], [fcv_r])
        bufs = []
        for i2 in range(2):
            bufs.append((BB[:, (3 * i2) * NT:(3 * i2 + 1) * NT], BB[:, (3 * i2 + 1) * NT:(3 * i2 + 2) * NT],
                         BB[:, (3 * i2 + 2) * NT:(3 * i2 + 3) * NT],
                         sub(BB_r, f"fa{i2}"), sub(BB_r, f"fb{i2}"), sub(BB_r, f"fg{i2}")))
        PARTS = [(0, 2304), (2304, 4608)]
        accp_r = [sub(ACC_r, f"accq{i_}") for i_ in range(2)]
        def fa_loads(j):
            AA, BBb, GG, ar, brr, gr = bufs[j % 2]
            S.dma("sp", AA, abT[j * 128:(j + 1) * 128, :], reads=[abT_r], writes=[ar])
            S.dma("sp", BBb, abT[DFF + j * 128:DFF + (j + 1) * 128, :], reads=[abT_r], writes=[brr])
        fa_loads(0)
        for j in range(44):
            AA, BBb, GG, ar, brr, gr = bufs[j % 2]
            if j + 1 < 44:
                fa_loads(j + 1)
            gpr = [sub(gr, f"gq{j}_{i_}") for i_ in range(2)]
            for pi_, (p0, p1) in enumerate(PARTS):
                S.op("act", lambda e, j=j, AA=AA, p0=p0, p1=p1: e.activation(out=ACC[:, p0:p1], in_=AA[:, p0:p1], func=AF.Copy,
                                                                          scale=fcv[:, 1, j:j + 1]),
                     reads=[ar, fcv_r], writes=[accp_r[pi_]])
            for pi_, (p0, p1) in enumerate(PARTS):
                acr = accp_r[pi_]
                for tap, (lo, hi) in ((0, (1, 0)), (2, (0, 1))):
                    rngs = []
                    for (s0, s1) in SEQS:
                        o0, o1 = max(s0 + lo, p0), min(s1 - hi, p1)
                        if o1 > o0:
                            rngs.append((o0, o1))

                    def taps(e, j=j, AA=AA, tap=tap, lo=lo, hi=hi, rngs=rngs):
                        last = None
                        for (o0, o1) in rngs:
                            last = e.scalar_tensor_tensor(out=ACC[:, o0:o1], in0=AA[:, o0 - lo + hi:o1 - lo + hi],
                                                          scalar=fcv[:, tap, j:j + 1], in1=ACC[:, o0:o1],
                                                          op0=ALU.mult, op1=ALU.add)
                        return last
                    S.op("dve", taps, reads=[ar, fcv_r, acr], writes=[acr])
            for pi_, (p0, p1) in enumerate(PARTS):
                acr = accp_r[pi_]
                S.op("act", lambda e, GG=GG, p0=p0, p1=p1: e.activation(out=GG[:, p0:p1], in_=ACC[:, p0:p1], func=AF.Silu),
                     reads=[acr], writes=[gpr[pi_]])
                S.op("pool", lambda e, GG=GG, BBb=BBb, p0=p0, p1=p1: e.tensor_tensor(out=GG[:, p0:p1], in0=GG[:, p0:p1],
                                                                                   in1=BBb[:, p0:p1], op=ALU.mult),
                     reads=[gpr[pi_], brr], writes=[gpr[pi_]])
                S.dma("sp", gT[j * 128:(j + 1) * 128, p0:p1], GG[:, p0:p1], reads=[gpr[pi_]], writes=[gT_r])
            join(gr, gpr)
        join(ACC_r, accp_r)
        join(BB_r, [x for bf_ in bufs for x in bf_[3:]])

    CVT = [S.sb(f"CVT{i}", [128, 512], BF16) for i in range(6)]
    CVT_r = [Res(f"CVT{i}") for i in range(6)]
    rr["cvt"] = 0
    hyc = S.sb("hyc", [128, 16], F32); hyc_r = Res("hyc")
    wf1 = S.sb("wf1", [33, 64], F32); wf2 = S.sb("wf2", [64, 64], F32); wf_r = Res("wf")
    hsdP = dscr("hsdP", [2, LP, 2 * D], BF16); hsdP_r = Res("hsdP", True)
    ghatP = dscr("ghatP", [2, LP, 2 * D], BF16); ghatP_r = Res("ghatP", True)
    yhatP = dscr("yhatP", [2, 2 * LP, D], BF16); yhatP_r = Res("yhatP", True)

    def post_ptm(ti, j0, b, br):
        si = rot("sb16", 4)
        copy(evq(), SB16[si][:], b[:], [br], [SB16_r[si]])
        S.dma("sp", ptm[ti * 128:(ti + 1) * 128, j0:j0 + 512], SB16[si][:], reads=[SB16_r[si]], writes=[ptm_r])

    def hy_conv3():
        starts = {0, 32, 34}
        ends = {31, 33, 35}
        W2 = 2048
        ins_ = [[BB[:, (k * 2 + i2) * W2:(k * 2 + i2 + 1) * W2] for i2 in range(2)] for k in range(3)]
        in_r_ = [[sub(BB_r, f"cv{k}{i2}") for i2 in range(2)] for k in range(3)]
        wts = [BB[:, 12288 + k * 4096:12288 + (k + 1) * 4096].bitcast(F32) for k in range(3)]
        wt_r = [sub(BB_r, f"cw{k}") for k in range(3)]
        accs = [BB[:, 24576 + i2 * 4096:24576 + (i2 + 1) * 4096].bitcast(F32) for i2 in range(2)]
        acc_r = [sub(BB_r, f"ca{i2}") for i2 in range(2)]
        T1, T2 = ACC[:, 0:W2], ACC[:, W2:2 * W2]
        T1_r, T2_r = sub(ACC_r, "T1"), sub(ACC_r, "T2")
        it = 0
        for cb in range(3):
            c0 = cb * W2
            for k in range(3):
                S.dma("sp", wts[k], hy_conv[k, c0:c0 + W2].partition_broadcast(128), reads=[in_r], writes=[wt_r[k]])
            def loads(ti, i2):
                t0 = ti * 128
                pc, pm, pp = ins_[0][i2], ins_[1][i2], ins_[2][i2]
                pcr, pmr, ppr = in_r_[0][i2], in_r_[1][i2], in_r_[2][i2]
                S.dma("act", pc, ptm[t0:t0 + 128, c0:c0 + W2], reads=[ptm_r], writes=[pcr])
                if ti in starts:
                    S.op("pool", lambda e, pm=pm: e.memset(pm, 0.0), writes=[pmr])
                    S.dma("act", pm[1:128, :], ptm[t0:t0 + 127, c0:c0 + W2], reads=[ptm_r], writes=[pmr])
                else:
                    S.dma("act", pm, ptm[t0 - 1:t0 + 127, c0:c0 + W2], reads=[ptm_r], writes=[pmr])
                if ti in ends:
                    S.op("pool", lambda e, pp=pp: e.memset(pp, 0.0), writes=[ppr])
                    S.dma("act", pp[0:127, :], ptm[t0 + 1:t0 + 128, c0:c0 + W2], reads=[ptm_r], writes=[ppr])
                else:
                    S.dma("act", pp, ptm[t0 + 1:t0 + 129, c0:c0 + W2], reads=[ptm_r], writes=[ppr])
            loads(0, it % 2)
            for ti in range(36):
                t0 = ti * 128
                i2 = it % 2
                it += 1
                if ti + 1 < 36:
                    loads(ti + 1, it % 2)
                pc, pm, pp = ins_[0][i2], ins_[1][i2], ins_[2][i2]
                pcr, pmr, ppr = in_r_[0][i2], in_r_[1][i2], in_r_[2][i2]
                ac, acr = accs[i2], acc_r[i2]
                S.op("dve", lambda e, ac=ac, pc=pc: e.tensor_tensor(out=ac, in0=pc, in1=wts[1], op=ALU.mult),
                     reads=[pcr, wt_r[1]], writes=[acr])
                S.op("pool", lambda e, pm=pm: e.tensor_tensor(out=T1, in0=pm, in1=wts[0], op=ALU.mult),
                     reads=[pmr, wt_r[0]], writes=[T1_r])
                S.op("dve", lambda e, pp=pp: e.tensor_tensor(out=T2, in0=pp, in1=wts[2], op=ALU.mult),
                     reads=[ppr, wt_r[2]], writes=[T2_r])
                S.op("pool", lambda e, ac=ac: e.tensor_tensor(out=ac, in0=ac, in1=T1, op=ALU.add),
                     reads=[acr, T1_r], writes=[acr])
                S.op("dve", lambda e, ac=ac, pc=pc: e.tensor_tensor(out=pc, in0=ac, in1=T2, op=ALU.add),
                     reads=[acr, T2_r], writes=[pcr])
                S.dma("sp", c3[t0:t0 + 128, c0:c0 + W2], pc, reads=[pcr], writes=[c3_r])
        join(BB_r, [x for l in in_r_ for x in l] + wt_r + acc_r)
        join(ACC_r, [T1_r, T2_r])

    def hy_filters(nm, L, hs_dst, hs_dst_r):
        featT = IN["c_featT" + nm]
        FT = ACC[0:33, 0:L]
        S.dma("sp", FT, featT, writes=[ACC_r])
        S.dma("sp", wf1[:], hy_w_f1, reads=[in_r], writes=[wf_r])
        S.dma("sp", wf2[:], hy_w_f2, reads=[in_r], writes=[wf_r])
        S.dma("sp", hyc[0:64, 0:3], hy_small, reads=[in_r], writes=[hyc_r])
        def cols0(e):
            e.tensor_scalar(out=hyc[0:64, 3:4], in0=hyc[0:64, 2:3], scalar1=0.5, scalar2=None, op0=ALU.mult)
            return e.tensor_scalar(out=hyc[0:64, 4:5], in0=hyc[0:64, 2:3], scalar1=0.25, scalar2=None, op0=ALU.mult)
        S.op("dve", cols0, reads=[hyc_r], writes=[hyc_r])

        def cols(e):
            e.tensor_tensor(out=hyc[0:64, 5:6], in0=hyc[0:64, 3:4], in1=hyc[0:64, 0:1], op=ALU.mult)
            e.tensor_tensor(out=hyc[0:64, 6:7], in0=hyc[0:64, 4:5], in1=hyc[0:64, 0:1], op=ALU.mult)
            e.tensor_tensor(out=hyc[0:64, 7:8], in0=hyc[0:64, 3:4], in1=hyc[0:64, 1:2], op=ALU.mult)
            return e.tensor_tensor(out=hyc[0:64, 8:9], in0=hyc[0:64, 4:5], in1=hyc[0:64, 1:2], op=ALU.mult)
        S.op("dve", cols, reads=[hyc_r], writes=[hyc_r])
        H1 = BB[:, 0:8192].bitcast(F32)
        H2 = BB[:, 8192:16384].bitcast(F32)
        H2b = BB[:, 16384:20480]
        W3 = WA[0][0:64, 0:8192]
        S.dma("pool", W3, hy_w_f3, reads=[in_r], writes=[WA_r[0]])

        def sin_layer(lhsT, K, src, src_r, dst, bcol):
            for p0 in range(0, L, 512):
                w = min(512, L - p0)
                b, br = nb()
                S.op("pe", lambda e, b=b, p0=p0, w=w: e.matmul(b[0:64, 0:w], lhsT, src[0:K, p0:p0 + w], start=True, stop=True),
                     reads=[src_r, wf_r], writes=[br])
                s2, s4 = rot("stg", 6), rot("stg", 6)
                S.op("act", lambda e, b=b, s2=s2, w=w: e.activation(out=STG[s2][0:64, 0:w], in_=b[0:64, 0:w], func=AF.Sin,
                                                                   scale=hyc[0:64, 3:4], bias=hyc[0:64, bcol:bcol + 1]),
                     reads=[br, hyc_r], writes=[STG_r[s2]])
                S.op("act", lambda e, b=b, s4=s4, w=w: e.activation(out=STG[s4][0:64, 0:w], in_=b[0:64, 0:w], func=AF.Sin,
                                                                   scale=hyc[0:64, 4:5], bias=hyc[0:64, bcol + 1:bcol + 2]),
                     reads=[br, hyc_r], writes=[STG_r[s4]])
                S.op("dve", lambda e, s4=s4, w=w: e.tensor_tensor(out=STG[s4][0:64, 0:w], in0=STG[s4][0:64, 0:w],
                                                                 in1=STG[s4][0:64, 0:w], op=ALU.mult),
                     reads=[STG_r[s4]], writes=[STG_r[s4]])
                S.op("dve", lambda e, s4=s4, w=w: e.tensor_scalar(out=STG[s4][0:64, 0:w], in0=STG[s4][0:64, 0:w], scalar1=-2.0,
                                                                 scalar2=1.0, op0=ALU.mult, op1=ALU.add),
                     reads=[STG_r[s4]], writes=[STG_r[s4]])
                S.op("dve", lambda e, s2=s2, s4=s4, w=w, p0=p0: e.scalar_tensor_tensor(
                    out=dst[0:64, p0:p0 + w], in0=STG[s2][0:64, 0:w], scalar=2.0, in1=STG[s4][0:64, 0:w],
                    op0=ALU.mult, op1=ALU.mult), reads=[STG_r[s2], STG_r[s4]], writes=[BB_r])
        sin_layer(wf1[:], 33, ACC, ACC_r, H1, 5)
        sin_layer(wf2[:], 64, H1, BB_r, H2, 7)
        copy("dve", H2b[0:64, 0:L], H2[0:64, 0:L], [BB_r], [BB_r])
        decf, decb = IN["c_decf" + nm], IN["c_decb" + nm]
        dtl = [(XT[0][:], XT[1][:], sub(XT_r[0], "dcf0"), sub(XT_r[1], "dcb0")),
               (ACC[:, 0:2048], ACC[:, 2048:4096], sub(ACC_r, "dcf1"), sub(ACC_r, "dcb1"))]
        for tc in range(L // 128):
            dF, dB, dFr, dBr = dtl[tc % 2]
            S.dma("act", dF, decf[tc * 128:(tc + 1) * 128, :], writes=[dFr])
            S.dma("act", dB, decb[tc * 128:(tc + 1) * 128, :], writes=[dBr])
            for o in range(2):
                for db in range(4):
                    cf = o * 2048 + db * 512
                    cs = slice(db * 512, (db + 1) * 512)
                    bf_, bfr = nb()
                    bb_, bbr = nb()
                    S.op("pe", lambda e, bf_=bf_, tc=tc, cf=cf: e.matmul(bf_[:], H2b[0:64, tc * 128:(tc + 1) * 128],
                                                                      W3[:, cf:cf + 512], start=True, stop=True),
                         reads=[BB_r, WA_r[0]], writes=[bfr])
                    S.op("pe", lambda e, bb_=bb_, tc=tc, cf=cf: e.matmul(bb_[:], H2b[0:64, tc * 128:(tc + 1) * 128],
                                                                      W3[:, 4096 + cf:4096 + cf + 512], start=True, stop=True),
                         reads=[BB_r, WA_r[0]], writes=[bbr])
                    d0, d1 = rot("stg", 6), rot("stg", 6)
                    S.op("dve", lambda e, bf_=bf_, d0=d0, dF=dF, cs=cs: e.tensor_tensor(out=STG[d0][:], in0=bf_[:], in1=dF[:, cs], op=ALU.mult),
                         reads=[bfr, dFr], writes=[STG_r[d0]])
                    S.op("dve", lambda e, bb_=bb_, d1=d1, dB=dB, cs=cs: e.tensor_tensor(out=STG[d1][:], in0=bb_[:], in1=dB[:, cs], op=ALU.mult),
                         reads=[bbr, dBr], writes=[STG_r[d1]])
                    c0_, c1_ = rot("cvt", 6), rot("cvt", 6)
                    S.op("pool", lambda e, d0=d0, d1=d1, c0_=c0_: e.tensor_tensor(out=CVT[c0_][:], in0=STG[d0][:], in1=STG[d1][:],
                                                                              op=ALU.add),
                         reads=[STG_r[d0], STG_r[d1]], writes=[CVT_r[c0_]])
                    S.op("pool", lambda e, d0=d0, d1=d1, c1_=c1_: e.tensor_tensor(out=CVT[c1_][:], in0=STG[d1][:], in1=STG[d0][:],
                                                                              op=ALU.subtract),
                         reads=[STG_r[d0], STG_r[d1]], writes=[CVT_r[c1_]])
                    S.dma("sp", hs_dst[0, tc * 128:(tc + 1) * 128, cf:cf + 512], CVT[c0_][:], reads=[CVT_r[c0_]], writes=[hs_dst_r])
                    S.dma("sp", hs_dst[1, tc * 128:(tc + 1) * 128, cf:cf + 512], CVT[c1_][:], reads=[CVT_r[c1_]], writes=[hs_dst_r])
        join(XT_r[0], [dtl[0][2]]); join(XT_r[1], [dtl[0][3]]); join(ACC_r, [dtl[1][2], dtl[1][3]])

    def dft_gemm(A_t, parts, nI, CC, B_r, ncols, JB, post):
        for j0 in range(0, ncols, JB):
            bvs = []
            for pi_, (ioff, bfn) in enumerate(parts):
                if pi_ > 0 and bfn is parts[0][1]:
                    bvs.append(bvs[0])
                    continue
                off = pi_ * CC * JB
                bv = BB[:, off:off + CC * JB].rearrange("p (c w) -> p c w", c=CC)
                src = bfn(j0, JB)

                def ldb(e, bv=bv, src=src):
                    n = max(1, CC // 8)
                    step = CC // n
                    return [e.dma_start(out=bv[:, q * step:(q + 1) * step, :],
                                        in_=src[q * step * 128:(q + 1) * step * 128, :].rearrange("(c p) w -> p c w", p=128))
                            for q in range(n)]
                S.op("sp", ldb, reads=[B_r], writes=[BB_r], dma=max(1, CC // 8))
                bvs.append(bv)
            for i in range(nI):
                wi = rot("wa", 2)
                avs = []
                for pi_, (ioff, bfn) in enumerate(parts):
                    av = WA[wi][:, pi_ * CC * 128:(pi_ + 1) * CC * 128].rearrange("p (c m) -> p c m", c=CC)
                    S.dma("pool", av, A_t[ioff + i], writes=[WA_r[wi]])
                    avs.append(av)
                res = []
                for pi_ in range(len(parts)):
                    bl = []
                    for h0 in range(0, JB, 512):
                        b, br = nb()

                        def mm(e, av=avs[pi_], bv=bvs[pi_], h0=h0, b=b):
                            last = None
                            for c in range(CC):
                                last = e.matmul(b[:], av[:, c, :], bv[:, c, h0:h0 + 512], start=(c == 0), stop=(c == CC - 1))
                            return last
                        S.op("pe", mm, reads=[WA_r[wi], BB_r], writes=[br])
                        bl.append((b, br))
                    res.append(bl)
                post(i, j0, res)

    def hy_seq(nm, L, row0, hs_, hs_r_, gh_, gh_r_, yh_, yh_r_, spectra=True):
        CC = L // 128
        fwd_t, inv_t = IN["c_fwd" + nm], IN["c_inv" + nm]
        for part in (range(2) if spectra else ()):
            def post_g(i, j0, res, part=part):
                for h, (b, br) in enumerate(res[0]):
                    ci = rot("cvt", 6)
                    copy(evq(), CVT[ci][:], b[:], [br], [CVT_r[ci]])
                    S.dma("sp", gh_[part, i * 128:(i + 1) * 128, j0 + h * 512:j0 + (h + 1) * 512], CVT[ci][:],
                          reads=[CVT_r[ci]], writes=[gh_r_])
            dft_gemm(fwd_t, [(part * CC, lambda j0, JB, part=part: hs_[part, :, j0:j0 + JB])], CC, CC, hs_r_, 2 * D, 1024, post_g)
        for o in range(2):
            if o == 0:
                vsrc = lambda j0, JB: c3[row0:row0 + L, 2 * D + j0:2 * D + j0 + JB]
                v_r = c3_r
            else:
                vsrc = lambda j0, JB: z1[row0:row0 + L, j0:j0 + JB]
                v_r = z1_r

            def post_f(i, j0, res, o=o):
                for h in range(len(res[0])):
                    (a, ar), (b, br) = res[0][h], res[1][h]
                    cg_, sg_ = rot("cvt", 6), rot("cvt", 6)
                    col = o * D + j0 + h * 512
                    S.dma("sp", CVT[cg_][:], gh_[0, i * 128:(i + 1) * 128, col:col + 512], reads=[gh_r_], writes=[CVT_r[cg_]])
                    S.dma("sp", CVT[sg_][:], gh_[1, i * 128:(i + 1) * 128, col:col + 512], reads=[gh_r_], writes=[CVT_r[sg_]])
                    t1, t2 = rot("stg", 6), rot("stg", 6)
                    S.op("dve", lambda e, a=a, t1=t1, cg_=cg_: e.tensor_tensor(out=STG[t1][:], in0=a[:], in1=CVT[cg_][:], op=ALU.mult),
                         reads=[ar, CVT_r[cg_]], writes=[STG_r[t1]])
                    S.op("dve", lambda e, b=b, t2=t2, sg_=sg_: e.tensor_tensor(out=STG[t2][:], in0=b[:], in1=CVT[sg_][:], op=ALU.mult),
                         reads=[br, CVT_r[sg_]], writes=[STG_r[t2]])
                    y0 = rot("sb16", 4)
                    S.op("dve", lambda e, t1=t1, t2=t2, y0=y0: e.tensor_tensor(out=SB16[y0][:], in0=STG[t1][:], in1=STG[t2][:], op=ALU.add),
                         reads=[STG_r[t1], STG_r[t2]], writes=[SB16_r[y0]])
                    S.dma("sp", yh_[o, i * 128:(i + 1) * 128, j0 + h * 512:j0 + (h + 1) * 512], SB16[y0][:],
                          reads=[SB16_r[y0]], writes=[yh_r_])
                    t3, t4 = rot("stg", 6), rot("stg", 6)
                    S.op("dve", lambda e, b=b, t3=t3, cg_=cg_: e.tensor_tensor(out=STG[t3][:], in0=b[:], in1=CVT[cg_][:], op=ALU.mult),
                         reads=[br, CVT_r[cg_]], writes=[STG_r[t3]])
                    S.op("dve", lambda e, a=a, t4=t4, sg_=sg_: e.tensor_tensor(out=STG[t4][:], in0=a[:], in1=CVT[sg_][:], op=ALU.mult),
                         reads=[ar, CVT_r[sg_]], writes=[STG_r[t4]])
                    y1 = rot("sb16", 4)
                    S.op("dve", lambda e, t3=t3, t4=t4, y1=y1: e.tensor_tensor(out=SB16[y1][:], in0=STG[t3][:], in1=STG[t4][:],
                                                                             op=ALU.subtract),
                         reads=[STG_r[t3], STG_r[t4]], writes=[SB16_r[y1]])
                    S.dma("sp", yh_[o, L + i * 128:L + (i + 1) * 128, j0 + h * 512:j0 + (h + 1) * 512], SB16[y1][:],
                          reads=[SB16_r[y1]], writes=[yh_r_])
            dft_gemm(fwd_t, [(0, vsrc), (CC, vsrc)], CC, CC, v_r, D, 1024, post_f)

            def post_i(i, j0, res, o=o):
                (y, yr) = res[0][0]
                t0 = row0 + i * 128
                bi_ = rot("stg", 6)
                S.dma("sp", STG[bi_][:], hy_bias[o, j0:j0 + 512].partition_broadcast(128), reads=[in_r], writes=[STG_r[bi_]])
                vv, gg = rot("cvt", 6), rot("cvt", 6)
                if o == 0:
                    S.dma("sp", CVT[vv][:], c3[t0:t0 + 128, 2 * D + j0:2 * D + j0 + 512], reads=[c3_r], writes=[CVT_r[vv]])
                    S.dma("sp", CVT[gg][:], c3[t0:t0 + 128, j0:j0 + 512], reads=[c3_r], writes=[CVT_r[gg]])
                else:
                    S.dma("sp", CVT[vv][:], z1[t0:t0 + 128, j0:j0 + 512], reads=[z1_r], writes=[CVT_r[vv]])
                    S.dma("sp", CVT[gg][:], c3[t0:t0 + 128, D + j0:D + j0 + 512], reads=[c3_r], writes=[CVT_r[gg]])
                S.op("dve", lambda e, bi_=bi_, vv=vv: e.tensor_tensor(out=STG[bi_][:], in0=CVT[vv][:], in1=STG[bi_][:], op=ALU.mult),
                     reads=[CVT_r[vv], STG_r[bi_]], writes=[STG_r[bi_]])
                S.op("dve", lambda e, y=y, bi_=bi_: e.tensor_tensor(out=STG[bi_][:], in0=y[:], in1=STG[bi_][:], op=ALU.add),
                     reads=[yr, STG_r[bi_]], writes=[STG_r[bi_]])
                S.op("dve", lambda e, bi_=bi_, gg=gg, vv=vv: e.tensor_tensor(out=CVT[vv][:], in0=STG[bi_][:], in1=CVT[gg][:], op=ALU.mult),
                     reads=[STG_r[bi_], CVT_r[gg]], writes=[CVT_r[vv]])
                dst, dr = (z1, z1_r) if o == 0 else (zz, zz_r)
                S.dma("sp", dst[t0:t0 + 128, j0:j0 + 512], CVT[vv][:], reads=[CVT_r[vv]], writes=[dr])
            dft_gemm(inv_t, [(0, lambda j0, JB, o=o: yh_[o, :, j0:j0 + JB])], CC, 2 * CC, yh_r_, D, 512, post_i)


    PL = {"fp": [], "bf": [], "fi": 0, "bi": 0, "subs": []}

    def pools_open():
        fp = [(STG[i][:], STG_r[i]) for i in range(1, 6)]
        bfp = [(CVT[i][:], CVT_r[i]) for i in range(6)] + [(SB16[i][:], SB16_r[i]) for i in range(4)]
        subs = []
        for k in range(2):
            for q in range(4):
                r_ = sub(XT_r[k], f"xtp{k}{q}")
                subs.append((XT_r[k], r_))
                fp.append((XT[k][:, q * 512:(q + 1) * 512], r_))
        for q in range(9):
            r_ = sub(ACC_r, f"accp{q}")
            subs.append((ACC_r, r_))
            fp.append((ACC[:, q * 512:(q + 1) * 512], r_))
        hv = HTS[:].rearrange("p c t -> p (c t)")
        for q in range(8):
            r_ = sub(HTS_r, f"htsp{q}")
            subs.append((HTS_r, r_))
            bfp.append((hv[:, q * 512:(q + 1) * 512], r_))
        for (t_, tr_, nm_) in ((XN, XN_r, "xnp"), (JK, JK_r, "jkp")):
            for q in range(4):
                r_ = sub(tr_, f"{nm_}{q}")
                subs.append((tr_, r_))
                bfp.append((t_[:, q * 512:(q + 1) * 512], r_))
        PL["fp"], PL["bf"], PL["subs"] = fp, bfp, subs

    def pools_close():
        for parent, r_ in PL["subs"]:
            join(parent, [r_])
        PL["fp"], PL["bf"], PL["subs"] = [], [], []

    def fpt():
        i = PL["fi"] % len(PL["fp"])
        PL["fi"] += 1
        return PL["fp"][i]

    def bft():
        i = PL["bi"] % len(PL["bf"])
        PL["bi"] += 1
        return PL["bf"][i]

    B1 = dscr("B1", [2, 128, 32, D], BF16); B1_r = Res("B1", True)
    D1 = dscr("D1", [2, 128, 32, D], BF16); D1_r = Res("D1", True)
    G2 = dscr("G2", [2, 128, 32, 2 * D], BF16); G2_r = Res("G2", True)
    fcs = S.sb("fcs", [128, 8, 128], BF16); fcs_r = Res("fcs")
    tws = S.sb("tws", [128, 2, 64], F32); tws_r = Res("tws")

    def fft_consts():
        for i_, nm_ in enumerate(("f1c", "f1s", "i1c", "i1s", "cbm", "sbm", "ncbm", "nsbm")):
            S.dma("sp", fcs[:, i_, :], IN["c_" + nm_], writes=[fcs_r])
        S.dma("sp", tws[:, 0, :], IN["c_tw1"], writes=[tws_r])
        S.dma("sp", tws[:, 1, :], IN["c_tw2"], writes=[tws_r])

    BT = {"t": [], "i": 0, "subs": []}

    def bt_open():
        t, subs = [], []
        for (tile_, tr_, nm_, n_) in ((ACC, ACC_r, "ba", 4), (XT[0], XT_r[0], "bx0", 2), (XT[1], XT_r[1], "bx1", 2)):
            v = tile_[:].bitcast(BF16)
            for q in range(n_):
                r_ = sub(tr_, f"{nm_}{q}")
                subs.append((tr_, r_))
                t.append((v[:, q * 2048:(q + 1) * 2048], r_))
        hv = HTS[:].rearrange("p c t -> p (c t)")
        for q in range(2):
            r_ = sub(HTS_r, f"bh{q}")
            subs.append((HTS_r, r_))
            t.append((hv[:, q * 2048:(q + 1) * 2048], r_))
        for (tile_, tr_, nm_) in ((XN, XN_r, "bxn"), (JK, JK_r, "bjk")):
            r_ = sub(tr_, nm_)
            subs.append((tr_, r_))
            t.append((tile_[:], r_))
        BT["t"], BT["subs"] = t, subs

    def bt_close():
        for parent, r_ in BT["subs"]:
            join(parent, [r_])
        BT["t"], BT["subs"] = [], []

    def btt():
        i = BT["i"] % len(BT["t"])
        BT["i"] += 1
        return BT["t"][i]

    rr["sfp"] = 0

    def sfp():
        i = rot("sfp", 10)
        return (STG[i][:], STG_r[i]) if i < 6 else (STX[i - 6][:], STX_r[i - 6])

    def sbf():
        k = rot("sbf", 10)
        return (CVT[k][:], CVT_r[k]) if k < 6 else (SB16[k - 6][:], SB16_r[k - 6])
    rr["sbf"] = 0

    def twiddle_evac(pb, pbr, qb_, qbr, ti_, col, o1, o1r, o2, o2r):
        cc = tws[:, ti_, col:col + 1]
        ss = tws[:, ti_, 32 + col:32 + col + 1]
        (u1, u1r), (u2, u2r) = sfp(), sfp()
        S.op("act", lambda e: e.activation(out=u1, in_=qb_[:], func=AF.Copy, scale=ss), reads=[qbr, tws_r], writes=[u1r])
        S.op("act", lambda e: e.activation(out=u2, in_=qb_[:], func=AF.Copy, scale=cc), reads=[qbr, tws_r], writes=[u2r])
        S.op("dve", lambda e: e.scalar_tensor_tensor(out=o1, in0=pb[:], scalar=cc, in1=u1, op0=ALU.mult, op1=ALU.subtract),
             reads=[pbr, u1r, tws_r], writes=[o1r])
        S.op("dve", lambda e: e.scalar_tensor_tensor(out=o2, in0=pb[:], scalar=ss, in1=u2, op0=ALU.mult, op1=ALU.add),
             reads=[pbr, u2r, tws_r], writes=[o2r])

    def fft_s1(src_fn, src_r):
        xb = [BB[:, i2 * 16384:(i2 + 1) * 16384].rearrange("p (r c) -> p r c", r=8) for i2 in range(2)]
        xb_r = [sub(BB_r, f"xb{i2}") for i2 in range(2)]
        srcv = src_fn().rearrange("(a r) c -> a r c", r=32)
        for rq in range(4):
            i2 = rq % 2

            def ld(e, i2=i2, rq=rq):
                return [e.dma_start(out=xb[i2][:, q * 2:(q + 1) * 2, :], in_=srcv[:, rq * 8 + q * 2:rq * 8 + (q + 1) * 2, :])
                        for q in range(4)]
            S.op("pool", ld, reads=[src_r], writes=[xb_r[i2]], dma=4)
            for r8 in range(8):
                r = rq * 8 + r8
                (ore, orr), (oim, oir) = btt(), btt()
                for db in range(4):
                    cs = slice(db * 512, (db + 1) * 512)
                    pb, pbr = nb()
                    qb_, qbr = nb()
                    S.op("pe", lambda e, pb=pb, i2=i2, r8=r8, cs=cs: e.matmul(pb[:], fcs[:, 0, :], xb[i2][:, r8, cs], start=True, stop=True),
                         reads=[xb_r[i2], fcs_r], writes=[pbr])
                    S.op("pe", lambda e, qb_=qb_, i2=i2, r8=r8, cs=cs: e.matmul(qb_[:], fcs[:, 1, :], xb[i2][:, r8, cs], start=True, stop=True),
                         reads=[xb_r[i2], fcs_r], writes=[qbr])
                    twiddle_evac(pb, pbr, qb_, qbr, 0, r, ore[:, cs], orr, oim[:, cs], oir)
                S.dma("sp", B1[0, :, r, :], ore, reads=[orr], writes=[B1_r])
                S.dma("sp", B1[1, :, r, :], oim, reads=[oir], writes=[B1_r])
        join(BB_r, xb_r)

    def fft_s2_load(src, src_r, jq, i2, bufs, bufs_r):
        for c_ in range(2):
            v = src[c_].rearrange("(j q) r d -> (q r) j d", q=4)
            dst = bufs[i2][c_]

            def ld(e, v=v, dst=dst, jq=jq):
                return [e.dma_start(out=dst[:, q * 2:(q + 1) * 2, :], in_=v[:, jq * 8 + q * 2:jq * 8 + (q + 1) * 2, :]) for q in range(4)]
            S.op("pool", ld, reads=[src_r], writes=[bufs_r[i2][c_]], dma=4)

    def s2_bufs():
        bufs = [[BB[:, (i2 * 2 + c_) * 8192:(i2 * 2 + c_ + 1) * 8192].rearrange("p (j c) -> p j c", j=4) for c_ in range(2)]
                for i2 in range(2)]
        bufs_r = [[sub(BB_r, f"s2b{i2}{c_}") for c_ in range(2)] for i2 in range(2)]
        return bufs, bufs_r

    def fft_filters(o):
        for part in range(2):
            fft_s1(lambda part=part: hsd[part, :, o * D:(o + 1) * D], hsd_r)
            bufs, bufs_r = s2_bufs()
            for jq in range(8):
                i2 = jq % 2
                for c_ in range(2):
                    v = B1[c_].rearrange("(j q) r d -> (q r) j d", q=4)

                    def ld(e, v=v, dst=bufs[i2][c_], jq=jq):
                        return [e.dma_start(out=dst[:, q:q + 1, :], in_=v[:, jq * 4 + q:jq * 4 + q + 1, :]) for q in range(4)]
                    S.op("pool", ld, reads=[B1_r], writes=[bufs_r[i2][c_]], dma=4)
                bre, bim = bufs[i2]
                for j4 in range(4):
                    j = jq * 4 + j4
                    og, ogr = btt()
                    for db in range(4):
                        cs = slice(db * 512, (db + 1) * 512)
                        b, br = nb()
                        m0, m1 = (4, 7) if part == 0 else (5, 4)
                        S.op("pe", lambda e, b=b, j4=j4, cs=cs, bre=bre, bim=bim, m0=m0, m1=m1: (
                            e.matmul(b[:], fcs[:, m0, :], bre[:, j4, cs], start=True, stop=False),
                            e.matmul(b[:], fcs[:, m1, :], bim[:, j4, cs], start=False, stop=True))[-1],
                            reads=[bufs_r[i2][0], bufs_r[i2][1], fcs_r], writes=[br])
                        copy(evq(), og[:, cs], b[:], [br], [ogr])
                    S.dma("sp", G2[part, :, j, o * D:(o + 1) * D], og, reads=[ogr], writes=[G2_r])
            join(BB_r, [x for l in bufs_r for x in l])

    def fft_conv(o):
        if o == 0:
            vsrc, v_r = (lambda: c3[0:LS, 2 * D:3 * D]), c3_r
        else:
            vsrc, v_r = (lambda: z1[0:LS, :]), z1_r
        fft_s1(vsrc, v_r)
        bufs, bufs_r = s2_bufs()
        d1v = [D1[c_].rearrange("(j q) r d -> j (q r) d", q=4) for c_ in range(2)]

        def g_loads(j):
            (gc, gcr), (gs, gsr) = btt(), btt()
            S.dma("sp", gc, G2[0, :, j, o * D:(o + 1) * D], reads=[G2_r], writes=[gcr])
            S.dma("sp", gs, G2[1, :, j, o * D:(o + 1) * D], reads=[G2_r], writes=[gsr])
            return (gc, gcr), (gs, gsr)
        gnext = None
        pend = [None]
        for jq in range(8):
            i2 = jq % 2
            for c_ in range(2):
                v = B1[c_].rearrange("(j q) r d -> (q r) j d", q=4)

                def ld(e, v=v, dst=bufs[i2][c_], jq=jq):
                    return [e.dma_start(out=dst[:, q:q + 1, :], in_=v[:, jq * 4 + q:jq * 4 + q + 1, :]) for q in range(4)]
                S.op("pool", ld, reads=[B1_r], writes=[bufs_r[i2][c_]], dma=4)
            bre, bim = bufs[i2]
            for j4 in range(4):
                j = jq * 4 + j4
                if j == 0:
                    gnext = g_loads(0)
                (gc, gcr), (gs, gsr) = gnext
                if j + 1 < 32:
                    gnext = g_loads(j + 1)
                (ore, orr), (oim, oir) = btt(), btt()
                for db in range(4):
                    cs = slice(db * 512, (db + 1) * 512)
                    a, ar = nb()
                    b, br = nb()
                    S.op("pe", lambda e, a=a, j4=j4, cs=cs, bre=bre, bim=bim: (
                        e.matmul(a[:], fcs[:, 4, :], bre[:, j4, cs], start=True, stop=False),
                        e.matmul(a[:], fcs[:, 7, :], bim[:, j4, cs], start=False, stop=True))[-1],
                        reads=[bufs_r[i2][0], bufs_r[i2][1], fcs_r], writes=[ar])
                    S.op("pe", lambda e, b=b, j4=j4, cs=cs, bre=bre, bim=bim: (
                        e.matmul(b[:], fcs[:, 5, :], bre[:, j4, cs], start=True, stop=False),
                        e.matmul(b[:], fcs[:, 4, :], bim[:, j4, cs], start=False, stop=True))[-1],
                        reads=[bufs_r[i2][0], bufs_r[i2][1], fcs_r], writes=[br])
                    if pend[0] is not None:
                        pend[0]()
                        pend[0] = None
                    (t1, t1r), (t2, t2r), (t3, t3r), (t4, t4r) = sfp(), sfp(), sfp(), sfp()
                    S.op("dve", lambda e, a=a, t1=t1, gc=gc, cs=cs: e.tensor_tensor(out=t1, in0=a[:], in1=gc[:, cs], op=ALU.mult),
                         reads=[ar, gcr], writes=[t1r])
                    S.op("dve", lambda e, b=b, t2=t2, gs=gs, cs=cs: e.tensor_tensor(out=t2, in0=b[:], in1=gs[:, cs], op=ALU.mult),
                         reads=[br, gsr], writes=[t2r])
                    S.op("dve", lambda e, b=b, t3=t3, gc=gc, cs=cs: e.tensor_tensor(out=t3, in0=b[:], in1=gc[:, cs], op=ALU.mult),
                         reads=[br, gcr], writes=[t3r])
                    S.op("dve", lambda e, a=a, t4=t4, gs=gs, cs=cs: e.tensor_tensor(out=t4, in0=a[:], in1=gs[:, cs], op=ALU.mult),
                         reads=[ar, gsr], writes=[t4r])
                    (y0, y0r), (y1, y1r) = sbf(), sbf()
                    S.op("pool", lambda e, t1=t1, t2=t2, y0=y0: e.tensor_tensor(out=y0, in0=t1, in1=t2, op=ALU.add),
                         reads=[t1r, t2r], writes=[y0r])
                    S.op("pool", lambda e, t3=t3, t4=t4, y1=y1: e.tensor_tensor(out=y1, in0=t3, in1=t4, op=ALU.subtract),
                         reads=[t3r, t4r], writes=[y1r])

                    def second(y0=y0, y0r=y0r, y1=y1, y1r=y1r, j=j, cs=cs, ore=ore, orr=orr, oim=oim, oir=oir, last=(db == 3)):
                        cb_, cbr = nb()
                        db_, dbr = nb()
                        S.op("pe", lambda e: (e.matmul(cb_[:], fcs[:, 4, :], y0, start=True, stop=False),
                                              e.matmul(cb_[:], fcs[:, 5, :], y1, start=False, stop=True))[-1],
                             reads=[y0r, y1r, fcs_r], writes=[cbr])
                        S.op("pe", lambda e: (e.matmul(db_[:], fcs[:, 5, :], y0, start=True, stop=False),
                                              e.matmul(db_[:], fcs[:, 6, :], y1, start=False, stop=True))[-1],
                             reads=[y0r, y1r, fcs_r], writes=[dbr])
                        twiddle_evac(cb_, cbr, db_, dbr, 1, j, ore[:, cs], orr, oim[:, cs], oir)
                        if last:
                            S.dma("sp", d1v[0][j], ore, reads=[orr], writes=[D1_r])
                            S.dma("sp", d1v[1][j], oim, reads=[oir], writes=[D1_r])
                    pend[0] = second
        if pend[0] is not None:
            pend[0]()
            pend[0] = None
        join(BB_r, [x for l in bufs_r for x in l])
        dbufs = [[BB[:, (i2 * 2 + c_) * 8192:(i2 * 2 + c_ + 1) * 8192].rearrange("p (r c) -> p r c", r=4) for c_ in range(2)]
                 for i2 in range(2)]
        dbufs_r = [[sub(BB_r, f"d1b{i2}{c_}") for c_ in range(2)] for i2 in range(2)]
        S.dma("sp", WA[1][:, 0:4096].bitcast(F32), hy_bias[o, :].partition_broadcast(128), reads=[in_r], writes=[WA_r[1]])
        biasv = WA[1][:, 0:4096].bitcast(F32)
        c3v = c3[0:LS, :].rearrange("(a r) c -> r a c", r=32)
        z1v = z1[0:LS, :].rearrange("(a r) c -> r a c", r=32)
        zzv = zz[0:LS, :].rearrange("(a r) c -> r a c", r=32)
        for rq in range(8):
            i2 = rq % 2
            for c_ in range(2):
                def ld(e, c_=c_, dst=dbufs[i2][c_], rq=rq):
                    return [e.dma_start(out=dst[:, q:q + 1, :], in_=D1[c_, :, rq * 4 + q:rq * 4 + q + 1, :]) for q in range(4)]
                S.op("pool", ld, reads=[D1_r], writes=[dbufs_r[i2][c_]], dma=4)
            dre, dim_ = dbufs[i2]
            for r4 in range(4):
                r = rq * 4 + r4
                (vv, vvr), (gg, ggr) = btt(), btt()
                if o == 0:
                    S.dma("act", vv, c3v[r, :, 2 * D:3 * D], reads=[c3_r], writes=[vvr])
                    S.dma("act", gg, c3v[r, :, 0:D], reads=[c3_r], writes=[ggr])
                else:
                    S.dma("act", vv, z1v[r, :, :], reads=[z1_r], writes=[vvr])
                    S.dma("act", gg, c3v[r, :, D:2 * D], reads=[c3_r], writes=[ggr])
                for db in range(4):
                    cs = slice(db * 512, (db + 1) * 512)
                    y, yr = nb()
                    S.op("pe", lambda e, y=y, r4=r4, cs=cs, dre=dre, dim_=dim_: (
                        e.matmul(y[:], fcs[:, 2, :], dre[:, r4, cs], start=True, stop=False),
                        e.matmul(y[:], fcs[:, 3, :], dim_[:, r4, cs], start=False, stop=True))[-1],
                        reads=[dbufs_r[i2][0], dbufs_r[i2][1], fcs_r], writes=[yr])
                    t_, tr_ = sfp()
                    S.op("dve", lambda e, t_=t_, vv=vv, cs=cs: e.tensor_tensor(out=t_, in0=vv[:, cs], in1=biasv[:, cs], op=ALU.mult),
                         reads=[vvr, WA_r[1]], writes=[tr_])
                    S.op("dve", lambda e, y=y, t_=t_: e.tensor_tensor(out=t_, in0=y[:], in1=t_, op=ALU.add),
                         reads=[yr, tr_], writes=[tr_])
                    S.op("dve", lambda e, t_=t_, gg=gg, vv=vv, cs=cs: e.tensor_tensor(out=vv[:, cs], in0=t_, in1=gg[:, cs], op=ALU.mult),
                         reads=[tr_, ggr], writes=[vvr])
                dstv, dr = (z1v, z1_r) if o == 0 else (zzv, zz_r)
                S.dma("sp", dstv[r, :, :], vv, reads=[vvr], writes=[dr])
        join(BB_r, [x for l in dbufs_r for x in l])

    def hy_seq_fft():
        fft_consts()
        bt_open()
        for o in range(2):
            fft_filters(o)
        for o in range(2):
            fft_conv(o)
        bt_close()

    phase = [0]

    def P(fn, *a, **k):
        if phase[0] >= DBG_STOP:
            raise _Stop()
        fn(*a, **k)
        phase[0] += 1

    def post_v(ti, j0, b, br):
        si = rot("sb16", 4)
        if ti < 32:
            copy(evq(), SB16[si][:], b[:], [br], [SB16_r[si]])
        else:
            s2 = rot("stg", 6)
            copy("dve", STG[s2][:], b[:], [br], [STG_r[s2]])
            for hh in range(2):
                S.dma("sp", nv[(ti - 32) * 128:(ti - 31) * 128, hh * 256:(hh + 1) * 256], STG[s2][:, hh * 256:(hh + 1) * 256],
                      reads=[STG_r[s2]], writes=[Res("nv", True)])
            copy("act", SB16[si][:], STG[s2][:], [STG_r[s2]], [SB16_r[si]])
        S.dma("sp", vtm[ti * 128:(ti + 1) * 128, :], SB16[si][:], reads=[SB16_r[si]], writes=[vtm_r])

    def post_k(ti, j0, b, br):
        s2 = rot("stg", 6)
        copy(evq(), STG[s2][:], b[:], [br], [STG_r[s2]])
        for hh in range(2):
            S.dma("sp", nk[(ti - 32) * 128:(ti - 31) * 128, hh * 256:(hh + 1) * 256], STG[s2][:, hh * 256:(hh + 1) * 256],
                  reads=[STG_r[s2]], writes=[Res("nk", True)])

    def final_norm(xi_):
        for ti in range(36):
            xi = rot("xt", 2)
            S.dma("sp", XT[xi][:], xres[xi_][ti * 128:(ti + 1) * 128, :], reads=[xres_r[xi_]], writes=[XT_r[xi]])
            S.op("act", lambda e, xi=xi: e.activation(out=JK[:], in_=XT[xi][:], func=AF.Square, accum_out=sm[:, 0:1]),
                 reads=[XT_r[xi]], writes=[JK_r, sm_r])
            S.op("dve", lambda e: e.tensor_scalar(out=sm[:, 1:2], in0=sm[:, 0:1], scalar1=1.0 / D, scalar2=EPS,
                                                  op0=ALU.mult, op1=ALU.add), reads=[sm_r], writes=[sm_r])
            S.op("act", lambda e: e.activation(out=sm[:, 2:3], in_=sm[:, 1:2], func=AF.Sqrt), reads=[sm_r], writes=[sm_r])
            S.op("dve", lambda e: e.reciprocal(out=sm[:, 3:4], in_=sm[:, 2:3]), reads=[sm_r], writes=[sm_r])
            if ti == 0:
                S.dma("sp", ACC[:, 0:D], norm_final.partition_broadcast(128), reads=[in_r], writes=[ACC_r])
            S.op("dve", lambda e, xi=xi: e.scalar_tensor_tensor(out=XT[xi][:], in0=XT[xi][:], scalar=sm[:, 3:4],
                                                               in1=ACC[:, 0:D], op0=ALU.mult, op1=ALU.mult),
                 reads=[XT_r[xi], sm_r, ACC_r], writes=[XT_r[xi]])
            dst = ys[ti * 128:(ti + 1) * 128, :] if ti < 32 else yp[(ti - 32) * 128:(ti - 31) * 128, :]
            for hh in range(4):
                S.dma("sp", dst[:, hh * 512:(hh + 1) * 512], XT[xi][:, hh * 512:(hh + 1) * 512], reads=[XT_r[xi]],
                      writes=[Res("y", True)])

    if os.environ.get("MK_VAR", "") == "nvtest":
        S.op("dve", lambda e: e.memset(STG[0][:], 1.0), writes=[STG_r[0]])
        S.dma("sp", nv[0:128, :], STG[0][:], reads=[STG_r[0]], writes=[Res("nv", True)])
    try:
        P(mod_phase, 0)
        P(norm_T, x_src(None), 0)
        P(gemm_fm, w_qkv, 2560, hT, hT_r, 16, post_store_fm(qT, qT_r))
        P(gemm_tm, hTt, hTt_r, 16, w_qkv[:, 2560:3072], 512, post_v, tiled=True)
        P(gemm_tm, hTt, hTt_r, 16, w_qkv[:, 2048:2560], 512, post_k, tiles=range(32, 36), tiled=True)
        P(attention)
        P(gemm_tm, oT, oT_r, 16, w_o, D, post_residual(x_src(None), 0, xres[0], xres_r[0]), tiled=True)
        P(norm_T, x_src(0), 1)
        P(gemm_fm, ffn_w_up[0], 2 * DFF, hT, hT_r, 16, post_store_fm(abT, abT_r))
        P(ffn_act, 0)
        P(gemm_tm, gT, gT_r, 44, ffn_w_down[0], D, post_residual(x_src(0), 1, xres[1], xres_r[1]))
        P(mod_phase, 1)
        P(norm_T, x_src(1), 0)
        P(gemm_tm, hTt, hTt_r, 16, hy_w_in, 3 * D, post_ptm, tiled=True)
        P(hy_conv3)
        P(hy_filters, "S", LS, hsd, hsd_r)
        if os.environ.get("MK_DFT", "") == "big":
            P(hy_seq, "S", LS, 0, hsd, hsd_r, ghat, ghat_r, yhat, yhat_r)
        else:
            P(hy_seq_fft)
        P(hy_filters, "P", LP, hsdP, hsdP_r)
        P(hy_seq, "P", LP, 4096, hsdP, hsdP_r, ghatP, ghatP_r, yhatP, yhatP_r)
        P(hy_seq, "P", LP, 4352, hsdP, hsdP_r, ghatP, ghatP_r, yhatP, yhatP_r, spectra=False)
        P(norm_T, None, 0, plain_src=(zz, zz_r))
        P(gemm_tm, hTt, hTt_r, 16, hy_w_out, D, post_residual(x_src(1), 0, xres[2], xres_r[2]), tiled=True)
        P(norm_T, x_src(2), 1)
        P(gemm_fm, ffn_w_up[1], 2 * DFF, hT, hT_r, 16, post_store_fm(abT, abT_r))
        P(ffn_act, 1)
        P(gemm_tm, gT, gT_r, 44, ffn_w_down[1], D, post_residual(x_src(2), 1, xres[3], xres_r[3]))
        P(final_norm, 3)
    except _Stop:
        pass
    S.finish()
    S.emit()
    S.st.close()
    return nc


_NC = None


def kernel(x_prompt, x_sample, cache_k, cache_v, c, c_ctx, w_mod, b_mod, norm_mix, norm_ffn, norm_final,
           w_qkv, w_o, attn_sink, hy_w_in, hy_conv, hy_w_f1, hy_b_f1, hy_w_f2, hy_b_f2, hy_w_f3, hy_freq,
           hy_bias, hy_w_out, ffn_w_up, ffn_conv, ffn_w_down):
    global _NC
    f = lambda a: np.ascontiguousarray(np.asarray(a, dtype=np.float32))
    x_prompt, x_sample, cache_k, cache_v, c, c_ctx = map(f, (x_prompt, x_sample, cache_k, cache_v, c, c_ctx))
    cst = _consts()
    if _NC is None:
        _NC = build()
    nc = _NC
    smallv = np.concatenate([f(b_mod).reshape(2, 96, 128), f(norm_mix).reshape(2, 16, 128),
                             f(norm_ffn).reshape(2, 16, 128)], axis=1)
    fc = f(ffn_conv).reshape(2, 3, 44, 128).reshape(2, 132, 128)
    shared = {
        "w_mod": f(w_mod), "smallv": np.ascontiguousarray(smallv), "norm_final": f(norm_final),
        "w_qkv": f(w_qkv)[0], "w_o": f(w_o)[0], "sink": f(attn_sink)[0],
        "hy_w_in": f(hy_w_in)[0], "hy_conv": f(hy_conv)[0], "hy_w_f1": f(hy_w_f1)[0], "hy_w_f2": f(hy_w_f2)[0],
        "hy_w_f3": f(hy_w_f3)[0],
        "hy_small": np.ascontiguousarray(np.stack([f(hy_b_f1)[0], f(hy_b_f2)[0], f(hy_freq)[0]], axis=1)),
        "hy_bias": f(hy_bias)[0], "hy_w_out": f(hy_w_out)[0],
        "ffn_w_up": f(ffn_w_up), "ffn_convT": np.ascontiguousarray(fc), "ffn_w_down": f(ffn_w_down),
    }
    for k, v in cst.items():
        shared["c_" + k] = v
    in_maps = []
    for b in range(8):
        m = dict(shared)
        m["xs"] = x_sample[b]
        m["xp"] = x_prompt[2 * b:2 * b + 2].reshape(512, D)
        m["ck"] = cache_k[b, 0].reshape(256, 512)
        m["cv"] = cache_v[b, 0].reshape(256, 512)
        m["cvec"] = np.ascontiguousarray(np.concatenate([c[b].reshape(16, 128), c_ctx.reshape(16, 128)], axis=0))
        in_maps.append(m)
    res = run_bass_kernel_spmd(nc, in_maps, core_ids=list(range(8)))
    R = res.results
    y_prompt = np.concatenate([R[b]["yp"].reshape(2, 256, D) for b in range(8)], axis=0)
    y_sample = np.stack([R[b]["ys"] for b in range(8)], axis=0)
    nk = np.concatenate([R[b]["nk"].reshape(2, 1, 256, 4, 128) for b in range(8)], axis=0)
    nv = np.concatenate([R[b]["nv"].reshape(2, 1, 256, 4, 128) for b in range(8)], axis=0)
    return (y_prompt.astype(np.float32), y_sample.astype(np.float32), nk.astype(np.float32), nv.astype(np.float32))
```
